# Optimizing a Trainium2 kernel written in Bass

```python
import jax, jax.numpy as jnp
from jax import lax
import numpy as np

D_MODEL = 1024
BATCH = 2
SEQ = 16384
DEPTH = 1
DEC_BATCH = 8
DEC_SEQ = 2048
PAST_LEN = 128

GRID_W = 64
MLA_HEADS = 8
Q_LORA = 256
KV_LORA = 128
QK_NOPE = 64
QK_ROPE = 32
V_DIM = 64
ROPE_THETA = 10000.0
Q_BLOCK = 128
NA_HEADS = 8
NA_DIM = 64
NA_WIN_R = 8
NA_WIN_C = 16
NA_COL_BLK = 16
NA_KEY_COLS = NA_COL_BLK + NA_WIN_C
MLA_WIDTH = MLA_HEADS * V_DIM
NA_WIDTH = NA_HEADS * NA_DIM
MIX_WIDTH = MLA_WIDTH + NA_WIDTH
IN_WIDTH = Q_LORA + KV_LORA + QK_ROPE + 3 * NA_WIDTH
D_FF = 2816
CONV_W = 3
EPS = 1e-6
NEG_INF = -1e30

kernel_name = "hybrid_mla_natten_convffn_encoder"


def rms_norm(x, g):
    xf = x.astype(jnp.float32)
    y = xf * lax.rsqrt(jnp.mean(xf * xf, axis=-1, keepdims=True) + EPS)
    return (y * g.astype(jnp.float32)).astype(x.dtype)


def rope_tables(S):
    inv = 1.0 / (ROPE_THETA ** (jnp.arange(0, QK_ROPE, 2, dtype=jnp.float32) / QK_ROPE))
    ang = jnp.arange(S, dtype=jnp.float32)[:, None] * inv[None, :]
    return jnp.cos(ang), jnp.sin(ang)


def apply_rope(x, cos, sin):
    x1, x2 = jnp.split(x.astype(jnp.float32), 2, axis=-1)
    return jnp.concatenate([x1 * cos - x2 * sin, x1 * sin + x2 * cos], axis=-1).astype(x.dtype)


def mla_attention(c_q, c_kv, k_rope, g_q_lat, w_q_up, g_kv_lat, w_kv_up):
    B, S, _ = c_q.shape
    cos, sin = rope_tables(S)
    q = (rms_norm(c_q, g_q_lat) @ w_q_up).reshape(B, S, MLA_HEADS, QK_NOPE + QK_ROPE)
    q = jnp.concatenate([q[..., :QK_NOPE], apply_rope(q[..., QK_NOPE:], cos[:, None, :], sin[:, None, :])], axis=-1)
    kv = (rms_norm(c_kv, g_kv_lat) @ w_kv_up).reshape(B, S, MLA_HEADS, QK_NOPE + V_DIM)
    k_pe = apply_rope(k_rope, cos, sin)
    k = jnp.concatenate([kv[..., :QK_NOPE], jnp.broadcast_to(k_pe[:, :, None, :], (B, S, MLA_HEADS, QK_ROPE))], axis=-1)
    v = kv[..., QK_NOPE:]
    scale = (QK_NOPE + QK_ROPE) ** -0.5
    nb = S // Q_BLOCK
    qb = q.reshape(B, nb, Q_BLOCK, MLA_HEADS, QK_NOPE + QK_ROPE).transpose(1, 0, 2, 3, 4)

    def block(qi):
        s = jnp.einsum('bqhd,bkhd->bhqk', qi, k, preferred_element_type=jnp.float32) * scale
        p = jax.nn.softmax(s, axis=-1)
        return jnp.einsum('bhqk,bkhd->bqhd', p.astype(v.dtype), v)

    o = lax.map(block, qb)
    return o.transpose(1, 0, 2, 3, 4).reshape(B, S, MLA_WIDTH)


def neighborhood_attention(q, k, v, rpb):
    B, S, _ = q.shape
    rows = S // GRID_W
    kr = min(NA_WIN_R, rows)
    ncb = GRID_W // NA_COL_BLK
    shp = (B, rows, GRID_W, NA_HEADS, NA_DIM)
    q, k, v = q.reshape(shp), k.reshape(shp), v.reshape(shp)
    r = jnp.arange(rows, dtype=jnp.int32)
    row_start = jnp.clip(r - kr // 2, 0, rows - kr)
    key_rows = row_start[:, None] + jnp.arange(kr, dtype=jnp.int32)[None, :]
    j = jnp.arange(ncb, dtype=jnp.int32)
    blk_col_start = jnp.clip(j * NA_COL_BLK - NA_WIN_C // 2, 0, GRID_W - NA_KEY_COLS)
    key_cols = blk_col_start[:, None] + jnp.arange(NA_KEY_COLS, dtype=jnp.int32)[None, :]
    kb = k[:, key_rows[:, None, :, None], key_cols[None, :, None, :]]
    vb = v[:, key_rows[:, None, :, None], key_cols[None, :, None, :]]
    qb = q.reshape(B, rows, ncb, NA_COL_BLK, NA_HEADS, NA_DIM)
    s = jnp.einsum('brjqhd,brjikhd->bhrjqik', qb, kb, preferred_element_type=jnp.float32) * (NA_DIM ** -0.5)
    qcol = j[:, None] * NA_COL_BLK + jnp.arange(NA_COL_BLK, dtype=jnp.int32)[None, :]
    qcol_start = jnp.clip(qcol - NA_WIN_C // 2, 0, GRID_W - NA_WIN_C)
    kc = key_cols[:, None, :]
    col_valid = (kc >= qcol_start[:, :, None]) & (kc < qcol_start[:, :, None] + NA_WIN_C)
    dr_idx = key_rows - r[:, None] + (NA_WIN_R - 1)
    dc_idx = jnp.clip(kc - qcol[:, :, None] + (NA_WIN_C - 1), 0, 2 * NA_WIN_C - 2)
    bias = rpb.astype(jnp.float32)[:, dr_idx[:, None, None, :, None], dc_idx[None, :, :, None, :]]
    s = jnp.where(col_valid[:, :, None, :], s + bias[None], NEG_INF)
    p = jax.nn.softmax(s.reshape(B, NA_HEADS, rows, ncb, NA_COL_BLK, kr * NA_KEY_COLS), axis=-1)
    o = jnp.einsum('bhrjqn,brjnhd->brjqhd', p.astype(vb.dtype),
                   vb.reshape(B, rows, ncb, kr * NA_KEY_COLS, NA_HEADS, NA_DIM))
    return o.reshape(B, S, NA_WIDTH)


def conv_ffn(x, w_up, conv_w, conv_b, w_down):
    S = x.shape[1]
    h = x @ w_up
    pad = CONV_W // 2
    hp = jnp.pad(h, ((0, 0), (pad, pad), (0, 0)))
    hc = conv_b
    for t in range(CONV_W):
        hc = hc + hp[:, t:t + S] * conv_w[t]
    g, u = jnp.split(hc, 2, axis=-1)
    return (jax.nn.gelu(g, approximate=True) * u) @ w_down


def encoder_layer(x, g_mix_pre, w_in, g_q_lat, w_q_up, g_kv_lat, w_kv_up, na_rpb, w_o, g_mix_post,
                  g_ffn_pre, w_ffn_up, ffn_conv_w, ffn_conv_b, w_ffn_down, g_ffn_post):
    h = rms_norm(x, g_mix_pre)
    z = h @ w_in
    o1 = Q_LORA
    o2 = o1 + KV_LORA
    o3 = o2 + QK_ROPE
    c_q, c_kv, k_rope = z[..., :o1], z[..., o1:o2], z[..., o2:o3]
    nq = z[..., o3:o3 + NA_WIDTH]
    nk = z[..., o3 + NA_WIDTH:o3 + 2 * NA_WIDTH]
    nv = z[..., o3 + 2 * NA_WIDTH:]
    a = mla_attention(c_q, c_kv, k_rope, g_q_lat, w_q_up, g_kv_lat, w_kv_up)
    n = neighborhood_attention(nq, nk, nv, na_rpb)
    mix = jnp.concatenate([a, n], axis=-1) @ w_o
    x = x + rms_norm(mix, g_mix_post)
    h = rms_norm(x, g_ffn_pre)
    x = x + rms_norm(conv_ffn(h, w_ffn_up, ffn_conv_w, ffn_conv_b, w_ffn_down), g_ffn_post)
    return x


def setup_inputs(seed: int = 0) -> dict:
    key = jax.random.key(seed)
    ks = jax.random.split(key, 20)
    f32 = jnp.float32

    def nrm(k, shape, scale):
        return jax.random.normal(k, shape, f32) * scale

    def gain(k, n):
        return 1.0 + 0.02 * jax.random.normal(k, (DEPTH, n), f32)

    return {
        "x_prompt": jax.random.normal(ks[0], (BATCH, SEQ, D_MODEL), f32),
        "x_sample": jax.random.normal(ks[1], (DEC_BATCH, DEC_SEQ, D_MODEL), f32),
        "g_mix_pre": gain(ks[2], D_MODEL),
        "w_in": nrm(ks[3], (DEPTH, D_MODEL, IN_WIDTH), D_MODEL ** -0.5),
        "g_q_lat": gain(ks[4], Q_LORA),
        "w_q_up": nrm(ks[5], (DEPTH, Q_LORA, MLA_HEADS * (QK_NOPE + QK_ROPE)), Q_LORA ** -0.5),
        "g_kv_lat": gain(ks[6], KV_LORA),
        "w_kv_up": nrm(ks[7], (DEPTH, KV_LORA, MLA_HEADS * (QK_NOPE + V_DIM)), KV_LORA ** -0.5),
        "na_rpb": nrm(ks[8], (DEPTH, NA_HEADS, 2 * NA_WIN_R - 1, 2 * NA_WIN_C - 1), 0.1),
        "w_o": nrm(ks[9], (DEPTH, MIX_WIDTH, D_MODEL), MIX_WIDTH ** -0.5),
        "g_mix_post": gain(ks[10], D_MODEL),
        "g_ffn_pre": gain(ks[11], D_MODEL),
        "w_ffn_up": nrm(ks[12], (DEPTH, D_MODEL, 2 * D_FF), D_MODEL ** -0.5),
        "ffn_conv_w": nrm(ks[13], (DEPTH, CONV_W, 2 * D_FF), CONV_W ** -0.5),
        "ffn_conv_b": nrm(ks[14], (DEPTH, 2 * D_FF), 0.01),
        "w_ffn_down": nrm(ks[15], (DEPTH, D_FF, D_MODEL), D_FF ** -0.5),
        "g_ffn_post": gain(ks[16], D_MODEL),
    }


def reference(x_prompt, x_sample, g_mix_pre, w_in, g_q_lat, w_q_up, g_kv_lat, w_kv_up, na_rpb, w_o,
              g_mix_post, g_ffn_pre, w_ffn_up, ffn_conv_w, ffn_conv_b, w_ffn_down, g_ffn_post):
    y_prompt = x_prompt
    y_sample = x_sample
    for l in range(DEPTH):
        params = (g_mix_pre[l], w_in[l], g_q_lat[l], w_q_up[l], g_kv_lat[l], w_kv_up[l], na_rpb[l], w_o[l],
                  g_mix_post[l], g_ffn_pre[l], w_ffn_up[l], ffn_conv_w[l], ffn_conv_b[l], w_ffn_down[l],
                  g_ffn_post[l])
        y_prompt = encoder_layer(y_prompt, *params)
        y_sample = encoder_layer(y_sample, *params)
    return (y_prompt, y_sample)
```

```python
import contextlib
import numpy as np
import concourse.bass as bass
import concourse.mybir as mybir
from concourse.bass_utils import run_bass_kernel_spmd

F32 = mybir.dt.float32
BF16 = mybir.dt.bfloat16
AF = mybir.ActivationFunctionType
ALU = mybir.AluOpType

ENGS = ("pe", "act", "dve", "pool", "sp")
SEM_ROT = 24000
NEG = -30000.0
EPS = 1e-6
D = 1024
DFF = 2816
NJ = 22


class Res:
    __slots__ = ("name", "w", "r")

    def __init__(self, name):
        self.name = name
        self.w = None
        self.r = {}


class Prog:
    def __init__(self, nc):
        self.nc = nc
        self.stack = contextlib.ExitStack()
        self.ops = {e: [] for e in ENGS}
        self.esem = {}
        self.own = {e: set() for e in ENGS}
        self.ecnt = {e: 0 for e in ENGS}
        self.lazy = {e: False for e in ENGS}
        self.seen = {e: {} for e in ENGS}
        self.allsems = []
        self.dmastates = []
        for e in ENGS:
            self._newsem(e)
        self.ninstr = {e: 0 for e in ENGS}

    def _newsem(self, e):
        s = self.stack.enter_context(self.nc.semaphore())
        self.esem[e] = s
        self.own[e].add(id(s))
        self.ecnt[e] = 0
        self.allsems.append(s)

    def dsem(self):
        s = self.stack.enter_context(self.nc.semaphore())
        st = [s, 0]
        self.dmastates.append(st)
        return st

    def _need(self, eng, ev, waits):
        if ev is None:
            return
        sem, val = ev
        if id(sem) in self.own[eng]:
            if eng == "pe" or eng == "sp":
                return
            if sem is self.esem[eng] and val > self.ecnt[eng]:
                return
        if self.seen[eng].get(id(sem), 0) >= val:
            return
        cur = waits.get(id(sem), (sem, 0))
        if val > cur[1]:
            waits[id(sem)] = (sem, val)

    def _deps(self, eng, reads, writes):
        waits = {}
        for R in reads:
            self._need(eng, R.w, waits)
        for R in writes:
            self._need(eng, R.w, waits)
            for s, v in R.r.values():
                self._need(eng, (s, v), waits)
        for k, (s, v) in waits.items():
            self.seen[eng][k] = v
        return list(waits.values())

    def _mark(self, ev, reads, writes):
        sem, val = ev
        for R in reads:
            old = R.r.get(id(sem))
            if old is None or old[1] < val:
                R.r[id(sem)] = (sem, val)
        for R in writes:
            R.w = ev
            R.r = {}

    def op(self, eng, fn, reads=(), writes=(), signal=True):
        waits = self._deps(eng, reads, writes)
        if signal:
            if self.ecnt[eng] >= SEM_ROT and not self.lazy[eng]:
                self._newsem(eng)
            self.ecnt[eng] += 1
            val = self.ecnt[eng]
            self.lazy[eng] = False
        else:
            val = self.ecnt[eng] + 1
            self.lazy[eng] = True
        sem = self.esem[eng]
        self.ops[eng].append((waits, fn, sem if signal else None, 1))
        self.ninstr[eng] += 1
        ev = (sem, val)
        self._mark(ev, reads, writes)
        return ev

    def dma(self, eng, fn, st, reads=(), writes=()):
        waits = self._deps(eng, reads, writes)
        if st[1] >= SEM_ROT:
            st[0] = self.stack.enter_context(self.nc.semaphore())
            st[1] = 0
        st[1] += 16
        ev = (st[0], st[1])
        self.ops[eng].append((waits, fn, st[0], 16))
        self.ninstr[eng] += 1
        self._mark(ev, reads, writes)
        return ev

    def barrier(self, extra_events=()):
        evs = []
        for e in ENGS:
            if self.lazy[e]:
                raise RuntimeError("barrier with pending lazy event on " + e)
            if self.ecnt[e] > 0:
                evs.append((self.esem[e], self.ecnt[e]))
        for st in self.dmastates:
            if st[1] > 0:
                evs.append((st[0], st[1]))
        evs.extend(extra_events)
        for e in ENGS:
            waits = {}
            for ev in evs:
                self._need(e, ev, waits)
            for k, (s, v) in waits.items():
                self.seen[e][k] = v
            self.ops[e].append((list(waits.values()), None, None, 0))

    def flush(self, name=None):
        nc = self.nc
        engobj = {"pe": "tensor", "act": "scalar", "dve": "vector", "pool": "gpsimd", "sp": "sync"}
        self.nflush = getattr(self, "nflush", 0) + 1
        scope = nc.named_scope(name or f"ph{self.nflush}")
        with scope, nc.Block() as block:
            for e in ENGS:
                ops = self.ops[e]
                if not ops:
                    continue

                def body(q, ops=ops):
                    for waits, fn, sem, inc in ops:
                        for s, v in waits:
                            q.wait_ge(s, v)
                        if fn is not None:
                            ins = fn(q)
                            if sem is not None:
                                ins.then_inc(sem, inc)
                getattr(block, engobj[e])(body)
        self.ops = {e: [] for e in ENGS}


def job_geometry(kind):
    g = {}
    if kind == "p":
        g["NKV"] = 16384
        g["NQ"] = 4224
        g["NOWN"] = 4096
        g["NNA"] = 4736
        g["qtiles"] = [(i * 512, 512) for i in range(8)] + [(4096, 128)]
        own = [[(0, 128, 384 + 128 * i)] for i in range(32)]
        own.append([(0, 64, 320), (64, 64, 4480)])
        g["own_tiles"] = own
        groups = []
        for t in range(8):
            tiles = [(4 * t + 1 + m, (8 * t + 2 + 2 * m) - (6 + 8 * t)) for m in range(8)]
            groups.append(dict(q0=512 * t, N=512, tiles=tiles))
        groups.append(dict(q0=4096, N=64, tiles=[(m, 2 * m - 5) for m in range(5)]))
        groups.append(dict(q0=4160, N=64, tiles=[(33 + m, 66 + 2 * m - 70) for m in range(4)]))
        g["groups"] = groups
        g["na_of_q"] = lambda q: (384 + q) if q < 4096 else ((320 + q - 4096) if q < 4160 else (4480 + q - 4160))
        g["halo"] = True
    else:
        g["NKV"] = 2048
        g["NQ"] = 2048
        g["NOWN"] = 2048
        g["NNA"] = 2048
        g["qtiles"] = [(i * 512, 512) for i in range(4)]
        g["own_tiles"] = [[(0, 128, 128 * i)] for i in range(16)]
        groups = []
        for t in range(4):
            tiles = []
            for idx in range(4 * t - 2, 4 * t + 6):
                if 0 <= idx < 16:
                    tiles.append((idx, 2 * idx - 8 * t))
            groups.append(dict(q0=512 * t, N=512, tiles=tiles))
        g["groups"] = groups
        g["na_of_q"] = lambda q: q
        g["halo"] = False
    for gi, gr in enumerate(g["groups"]):
        gr["interior"] = (kind == "p" and 1 <= gi <= 6) or (kind == "s" and 1 <= gi <= 2)
        if gr["interior"]:
            gr["tiles"] = sorted(gr["tiles"], key=lambda to: (to[1] != 2, to[1]))
    slot = 0
    for gr in g["groups"]:
        gr["slots"] = list(range(slot, slot + len(gr["tiles"])))
        slot += len(gr["tiles"])
    g["nslots"] = slot
    return g


GEO = {"p": job_geometry("p"), "s": job_geometry("s")}


class _Stop(Exception):
    pass


class _Skip(Exception):
    pass


import os
SKIP = os.environ.get('SKIP', '')


def build_program(stop=None):
    nc = bass.Bass("TRN2", target_bir_lowering=False)

    def din(name, shape, dt=F32):
        return nc.dram_tensor(name, list(shape), dt, kind="ExternalInput").ap()

    def dout(name, shape, dt=F32):
        return nc.dram_tensor(name, list(shape), dt, kind="ExternalOutput").ap()

    def dscratch(name, shape, dt):
        return nc.dram_tensor(name, list(shape), dt, kind="Internal").ap()

    I = {}
    for k in ("p", "s"):
        G = GEO[k]
        I["xkv_" + k] = din("xkv_" + k, [G["NKV"], D])
        I["xna_" + k] = din("xna_" + k, [G["NNA"], D])
        I["cosk_" + k] = din("cosk_" + k, [128, (G["NKV"] // 128) * 16])
        I["sink_" + k] = din("sink_" + k, [128, (G["NKV"] // 128) * 16])
        I["qc_" + k] = din("qc_" + k, [96, G["NQ"]])
        I["qs_" + k] = din("qs_" + k, [96, G["NQ"]])
        I["rm_" + k] = din("rm_" + k, [8, G["nslots"] * 128])
        I["y_" + k] = dout("y_" + k, [G["NOWN"], D])
        I["at_" + k] = dscratch("at_" + k, [D, G["NQ"]], BF16)
    I["flags"] = din("flags", [128, 2])
    I["tr2"] = din("tr2", [8, 128, 24 * 64])
    I["tr2i"] = din("tr2i", [8, 128, 24 * 64])
    I["sel"] = din("sel", [4, 128, 96])
    I["ident"] = din("ident", [128, 128])
    I["qrsel"] = din("qrsel", [8, 512])
    for nm, shp in (("g_mix_pre", [D]), ("w_in", [D, 1952]), ("g_q_lat", [256]), ("w_q_up", [256, 768]),
                    ("g_kv_lat", [128]), ("w_kv_up", [128, 1024]), ("w_o", [D, D]), ("g_mix_post", [D]),
                    ("g_ffn_pre", [D]), ("w_ffn_up", [D, 2 * DFF]), ("ffn_conv_w", [3, 2 * DFF]),
                    ("ffn_conv_b", [2 * DFF]), ("w_ffn_down", [DFF, D]), ("g_ffn_post", [D])):
        I[nm] = din(nm, shp)

    P = Prog(nc)
    gs = contextlib.ExitStack()

    uid = [0]

    def sb(stack, name, shape, dt):
        uid[0] += 1
        return stack.enter_context(nc.sbuf_tensor(f"sb{uid[0]}_{name}", list(shape), dt))

    def ps(stack, name, shape, dt=F32):
        uid[0] += 1
        return stack.enter_context(nc.psum_tensor(f"ps{uid[0]}_{name}", list(shape), dt))

    ident = sb(gs, "ident", [128, 128], BF16)
    ones_f = sb(gs, "ones_f", [128, 128], F32)
    gT_pre = sb(gs, "gT_pre", [128, 8 * 128], F32)
    gT_ffn = sb(gs, "gT_ffn", [128, 8 * 128], F32)
    gT_q = sb(gs, "gT_q", [128, 2 * 128], F32)
    gcols = sb(gs, "gcols", [128, 32], F32)
    flags = sb(gs, "flags", [128, 2], F32)
    epsb = sb(gs, "epsb", [128, 1], F32)
    r_const = Res("const")
    cst = P.dsem()

    ident32 = sb(gs, "ident32", [128, 128], F32)
    r_id32 = Res("id32")
    P.dma("sp", lambda q: q.dma_start(out=ident32[:], in_=I["ident"]), cst, writes=[r_id32])
    P.op("dve", lambda q: q.tensor_copy(out=ident[:], in_=ident32[:]), reads=[r_id32], writes=[r_const])
    P.dma("sp", lambda q: q.dma_start(out=gcols[:, 0:8], in_=I["g_mix_pre"].rearrange("(k p) -> p k", p=128), allow_slow_non_contiguous=True), cst, writes=[r_const])
    P.dma("sp", lambda q: q.dma_start(out=gcols[:, 8:16], in_=I["g_ffn_pre"].rearrange("(k p) -> p k", p=128), allow_slow_non_contiguous=True), cst, writes=[r_const])
    P.dma("sp", lambda q: q.dma_start(out=gcols[:, 16:18], in_=I["g_q_lat"].rearrange("(k p) -> p k", p=128), allow_slow_non_contiguous=True), cst, writes=[r_const])
    P.dma("sp", lambda q: q.dma_start(out=gcols[:, 18:19], in_=I["g_kv_lat"].rearrange("(k p) -> p k", p=128), allow_slow_non_contiguous=True), cst, writes=[r_const])
    P.dma("sp", lambda q: q.dma_start(out=flags[:], in_=I["flags"]), cst, writes=[r_const])
    P.op("dve", lambda q: q.memset(ones_f[:], 1.0), writes=[r_const])
    P.op("dve", lambda q: q.memset(epsb[:], EPS), writes=[r_const])
    for k in range(8):
        P.op("dve", lambda q, k=k: q.tensor_scalar(out=gT_pre[:, k * 128:(k + 1) * 128], in0=ones_f[:], scalar1=gcols[:, k:k + 1], scalar2=None, op0=ALU.mult),
             reads=[r_const], writes=[r_const])
        P.op("dve", lambda q, k=k: q.tensor_scalar(out=gT_ffn[:, k * 128:(k + 1) * 128], in0=ones_f[:], scalar1=gcols[:, 8 + k:9 + k], scalar2=None, op0=ALU.mult),
             reads=[r_const], writes=[r_const])
    for k in range(2):
        P.op("dve", lambda q, k=k: q.tensor_scalar(out=gT_q[:, k * 128:(k + 1) * 128], in0=ones_f[:], scalar1=gcols[:, 16 + k:17 + k], scalar2=None, op0=ALU.mult),
             reads=[r_const], writes=[r_const])

    def rstd(src_ap, n, junk_ap, ss_ap, rs_ap, r_src, r_junk, r_stat):
        P.op("act", lambda q: q.activation(out=junk_ap, in_=src_ap, func=AF.Square, accum_out=ss_ap),
             reads=[r_src], writes=[r_junk, r_stat])
        P.op("act", lambda q: q.activation(out=rs_ap, in_=ss_ap, func=AF.Sqrt, bias=epsb[:, 0:1], scale=1.0 / n),
             reads=[r_stat, r_const], writes=[r_stat])
        P.op("dve", lambda q: q.reciprocal(out=rs_ap, in_=rs_ap), reads=[r_stat], writes=[r_stat])

    class Front:
        def __init__(self, st, gT, nb=3, ntp=2):
            self.NB = nb
            self.NTP = ntp
            self.n2 = 0
            self.gT = gT
            self.xt = [sb(st, f"f_xt{i}", [128, D], F32) for i in range(self.NB)]
            self.junk = sb(st, "f_junk", [128, D], F32)
            self.hb = [sb(st, f"f_hb{i}", [128, D], BF16) for i in range(self.NB)]
            self.hT = [sb(st, f"f_hT{i}", [128, D], BF16) for i in range(self.NB)]
            self.stat = [sb(st, f"f_st{i}", [128, 2], F32) for i in range(self.NB)]
            self.tp = [ps(st, f"f_tp{i}", [128, D], BF16) for i in range(self.NTP)]
            self.r_x = [Res("fx") for _ in range(self.NB)]
            self.r_j = Res("fj")
            self.r_s = [Res("fs") for _ in range(self.NB)]
            self.r_hb = [Res("fhb") for _ in range(self.NB)]
            self.r_tp = [Res("ftp") for _ in range(self.NTP)]
            self.r_hT = [Res("fhT") for _ in range(self.NB)]
            self.ld = [P.dsem() for _ in range(self.NB)]
            self.n = 0

        def run1(self, xd, pieces):
            s = self.n % self.NB
            self.n += 1
            for (p0, nr, r0) in pieces:
                P.dma("sp", lambda q, p0=p0, nr=nr, r0=r0: q.dma_start(out=self.xt[s][p0:p0 + nr, :], in_=xd[r0:r0 + nr, :]),
                      self.ld[s], writes=[self.r_x[s]])
            rstd(self.xt[s][:], D, self.junk[:], self.stat[s][:, 0:1], self.stat[s][:, 1:2], self.r_x[s], self.r_j, self.r_s[s])
            P.op("dve", lambda q: q.tensor_scalar(out=self.hb[s][:], in0=self.xt[s][:], scalar1=self.stat[s][:, 1:2], scalar2=None, op0=ALU.mult),
                 reads=[self.r_x[s], self.r_s[s]], writes=[self.r_hb[s]])
            return s

        def run2(self, s):
            tpi = self.n2 % self.NTP
            self.n2 += 1
            for k in range(8):
                P.op("pe", lambda q, k=k: q.transpose(out=self.tp[tpi][:, k * 128:(k + 1) * 128], in_=self.hb[s][:, k * 128:(k + 1) * 128], identity=ident[:]),
                     reads=[self.r_hb[s], r_const], writes=[self.r_tp[tpi]], signal=(k == 7))
            P.op("dve", lambda q: q.tensor_tensor(out=self.hT[s][:], in0=self.tp[tpi][:], in1=self.gT[:], op=ALU.mult),
                 reads=[self.r_tp[tpi], r_const], writes=[self.r_hT[s]])
            return s

        def run(self, xd, pieces):
            return self.run2(self.run1(xd, pieces))

    def pipeline(nitems, stages, extra=None):
        K = len(stages)
        for step in range(nitems + K - 1):
            for k in range(K):
                i = step - k
                if 0 <= i < nitems:
                    stages[k](i)
            if extra:
                extra.pop(0)()

    S = {}
    S["w_in"] = dscratch("s_w_in", [D, 1952], BF16)
    S["w_q_up"] = dscratch("s_w_q_up", [256, 768], BF16)
    S["w_kv_up"] = dscratch("s_w_kv_up", [128, 1024], BF16)
    S["w_o"] = dscratch("s_w_o", [D, D], BF16)
    S["wd"] = dscratch("s_wd", [DFF, D], BF16)
    S["wu"] = dscratch("s_wu", [NJ, 128, 8, 256], BF16)
    S["tr2"] = dscratch("s_tr2", [8 * 128, 1536], BF16)
    S["tr2i"] = dscratch("s_tr2i", [8 * 128, 1536], BF16)
    S["rm_p"] = dscratch("s_rm_p", [8, GEO["p"]["nslots"] * 128], BF16)
    S["rm_s"] = dscratch("s_rm_s", [8, GEO["s"]["nslots"] * 128], BF16)
    S["qrsel"] = dscratch("s_qrsel", [8, 512], BF16)
    S["sel"] = dscratch("s_sel", [4 * 128, 96], BF16)
    class Conv:
        def __init__(self, stack, CB, nstg=3, engs=("dve", "pool", "act"), store_q="act"):
            self.engs = engs
            self.store_q = store_q
            self.CB = CB
            self.n = nstg
            self.s32 = [sb(stack, f"stg32_{i}", [128, CB], F32) for i in range(nstg)]
            self.s16 = [sb(stack, f"stg16_{i}", [128, CB], BF16) for i in range(nstg)]
            self.r32 = [Res("s32") for _ in range(nstg)]
            self.r16 = [Res("s16") for _ in range(nstg)]
            self.lds = [P.dsem() for _ in range(nstg)]
            self.sts = [P.dsem() for _ in range(nstg)]
            self.cn = 0

        def block(self, src_ap, nrows, ncols, store_fn):
            i = self.cn % self.n
            e = self.engs[self.cn % len(self.engs)]
            self.cn += 1
            s32, s16 = self.s32[i], self.s16[i]
            P.dma("sp", lambda q: q.dma_start(out=s32[0:nrows, 0:ncols], in_=src_ap), self.lds[i], writes=[self.r32[i]])
            if e == "act":
                P.op("act", lambda q: q.copy(out=s16[0:nrows, 0:ncols], in_=s32[0:nrows, 0:ncols]), reads=[self.r32[i]], writes=[self.r16[i]])
            else:
                P.op(e, lambda q: q.tensor_copy(out=s16[0:nrows, 0:ncols], in_=s32[0:nrows, 0:ncols]), reads=[self.r32[i]], writes=[self.r16[i]])
            P.dma(self.store_q, store_fn(s16), self.sts[i], reads=[self.r16[i]])

        def tasks2d(self, src, dst, R, C):
            out = []
            for r0 in range(0, R, 128):
                nr = min(128, R - r0)
                for c0 in range(0, C, self.CB):
                    ncl = min(self.CB, C - c0)
                    out.append(lambda r0=r0, nr=nr, c0=c0, ncl=ncl: self.block(
                        src[r0:r0 + nr, c0:c0 + ncl], nr, ncl,
                        lambda t: (lambda q: q.dma_start(out=dst[r0:r0 + nr, c0:c0 + ncl], in_=t[0:nr, 0:ncl]))))
            return out

    with contextlib.ExitStack() as sW:
        cv = Conv(sW, 2816)
        early = []
        early += cv.tasks2d(I["w_in"], S["w_in"], D, 1952)
        early += cv.tasks2d(I["w_q_up"], S["w_q_up"], 256, 768)
        early += cv.tasks2d(I["w_kv_up"], S["w_kv_up"], 128, 1024)
        early += cv.tasks2d(I["sel"].rearrange("c p n -> (c p) n"), S["sel"], 512, 96)
        early += cv.tasks2d(I["tr2"].rearrange("h p n -> (h p) n"), S["tr2"], 1024, 1536)
        early += cv.tasks2d(I["tr2i"].rearrange("h p n -> (h p) n"), S["tr2i"], 1024, 1536)
        early += cv.tasks2d(I["rm_p"], S["rm_p"], 8, GEO["p"]["nslots"] * 128)
        early += cv.tasks2d(I["rm_s"], S["rm_s"], 8, GEO["s"]["nslots"] * 128)
        early += cv.tasks2d(I["qrsel"], S["qrsel"], 8, 512)
        for f in early:
            f()
        P.barrier()
        P.flush()

    def late_conv_tasks(cv2):
        out = []
        out += cv2.tasks2d(I["w_o"], S["w_o"], D, D)
        out += cv2.tasks2d(I["w_ffn_down"], S["wd"], DFF, D)
        wu_v = S["wu"].rearrange("j p k c -> p j k c")
        nh = DFF // cv2.CB
        jb = cv2.CB // 128
        for k in range(8):
            for gu in range(2):
                for hf in range(nh):
                    out.append(lambda k=k, gu=gu, hf=hf: cv2.block(
                        I["w_ffn_up"][k * 128:(k + 1) * 128, gu * DFF + hf * cv2.CB:gu * DFF + (hf + 1) * cv2.CB], 128, cv2.CB,
                        lambda t: (lambda q: q.dma_start(out=wu_v[:, hf * jb:(hf + 1) * jb, k, gu * 128:(gu + 1) * 128], in_=t[:, 0:cv2.CB].rearrange("p (j c) -> p j c", c=128)))))
        return out

    dbg = {}
    if stop is not None:
        dbg['d0'] = dout('dbg0', [128, 4096], F32)
        dbg['d1'] = dout('dbg1', [128, 4096], F32)
        dbg['d2'] = dout('dbg2', [128, 4096], F32)
    dst_ = P.dsem()

    def dump(key, src_ap, ncol, npart=128):
        P.dma('sp', lambda q: q.dma_start(out=dbg[key][0:npart, 0:ncol], in_=src_ap), dst_)

    def dumpconv(stack, key, src_ap, ncol, npart=128):
        t = sb(stack, 'dbgt', [128, 4096], F32)
        r = Res('dbgt')
        P.op('dve', lambda q: q.tensor_copy(out=t[0:npart, 0:ncol], in_=src_ap), writes=[r])
        P.dma('sp', lambda q: q.dma_start(out=dbg[key][0:npart, 0:ncol], in_=t[0:npart, 0:ncol]), dst_, reads=[r])

    try:
      for kind in ("p", "s"):
          G = GEO[kind]
          NKV, NQ, NOWN, NNA = G["NKV"], G["NQ"], G["NOWN"], G["NNA"]
          NKT = NKV // 128
          NT4 = NKT // 4
          xkv_d, xna_d, at_d, y_d = I["xkv_" + kind], I["xna_" + kind], I["at_" + kind], I["y_" + kind]

          with contextlib.ExitStack() as sAB:
              kvnT = sb(sAB, "kvnT", [128, NKV], BF16)
              KPE = sb(sAB, "KPE", [128, NT4 * 128], BF16)
              cqnT = sb(sAB, "cqnT", [128, 2 * NQ], BF16)
              r_kvn, r_kpe, r_cqn = Res("kvn"), Res("kpe"), Res("cqn")

              with contextlib.suppress(_Skip), contextlib.ExitStack() as sA:
                  if 'A' in SKIP:
                      raise _Skip()
                  w_a = sb(sA, "w_a", [128, 8 * 416], BF16)
                  cosk = sb(sA, "cosk", [128, NKT * 16], F32)
                  sink = sb(sA, "sink", [128, NKT * 16], F32)
                  r_wa = Res("wa")
                  wst = P.dsem()
                  for k in range(8):
                      P.dma("sp", lambda q, k=k: q.dma_start(out=w_a[:, k * 416:(k + 1) * 416], in_=S["w_in"][k * 128:(k + 1) * 128, 0:416]),
                            wst, writes=[r_wa])
                  P.dma("sp", lambda q: q.dma_start(out=cosk[:], in_=I["cosk_" + kind]), wst, writes=[r_wa])
                  P.dma("sp", lambda q: q.dma_start(out=sink[:], in_=I["sink_" + kind]), wst, writes=[r_wa])
                  fr = Front(sA, gT_pre, nb=6, ntp=3)
                  NZ = 4
                  zps = [ps(sA, f"a_z{i}", [128, 512], F32) for i in range(NZ)]
                  tp2 = ps(sA, "a_tp2", [128, 1024], BF16)
                  kvb = [sb(sA, f"a_kvb{i}", [128, 128], BF16) for i in range(NZ)]
                  krs = [sb(sA, f"a_krs{i}", [128, 32], F32) for i in range(NZ)]
                  tmp = [sb(sA, f"a_tmp{i}", [128, 64], F32) for i in range(NZ)]
                  X4 = [sb(sA, f"a_X4{i}", [128, 128], BF16) for i in range(2)]
                  cqb = [sb(sA, f"a_cqb{i}", [128, 256], BF16) for i in range(NZ)]
                  st2 = [sb(sA, f"a_st2{i}", [128, 2], F32) for i in range(NZ)]
                  junk2 = sb(sA, "a_junk2", [128, 256], F32)
                  r_z = [Res("z") for _ in range(NZ)]
                  r_kvb = [Res("kvb") for _ in range(NZ)]
                  r_krs = [Res("krs") for _ in range(NZ)]
                  r_tmp = [Res("tmp") for _ in range(NZ)]
                  r_X4 = [Res("X4") for _ in range(2)]
                  r_cqb = [Res("cqb") for _ in range(NZ)]
                  r_st2 = [Res("st2") for _ in range(NZ)]
                  r_j2 = Res("j2")
                  r_tp2a = r_tp2b = r_tp2c = Res("tp2")

                  late = []
                  items = [(t, c) for t in range(NT4) for c in range(4)]
                  slot = {}

                  def kv_s0(i):
                      t, c = items[i]
                      slot[i] = fr.run1(xkv_d, [(0, 128, (c * NT4 + t) * 128)])

                  def kv_s1(i):
                      fr.run2(slot[i])

                  def kv_s2(i):
                      t, c = items[i]
                      ti = c * NT4 + t
                      s_ = slot[i]
                      z = i % NZ
                      for k in range(8):
                          P.op("pe", lambda q, k=k: q.matmul(zps[z][:, 0:160], fr.hT[s_][:, k * 128:(k + 1) * 128], w_a[:, k * 416 + 256:k * 416 + 416], start=(k == 0), stop=(k == 7)),
                               reads=[fr.r_hT[s_], r_wa], writes=[r_z[z]], signal=(k == 7))
                      P.op("act", lambda q: q.copy(out=krs[z][:], in_=zps[z][:, 128:160]), reads=[r_z[z]], writes=[r_krs[z]])
                      rstd(zps[z][:, 0:128], 128, junk2[:, 0:128], st2[z][:, 0:1], st2[z][:, 1:2], r_z[z], r_j2, r_st2[z])
                      P.op("dve", lambda q: q.tensor_scalar(out=kvb[z][:], in0=zps[z][:, 0:128], scalar1=st2[z][:, 1:2], scalar2=None, op0=ALU.mult),
                           reads=[r_z[z], r_st2[z]], writes=[r_kvb[z]])
                      co = cosk[:, ti * 16:(ti + 1) * 16]
                      si = sink[:, ti * 16:(ti + 1) * 16]
                      x1_ = krs[z][:, 0:16]
                      x2_ = krs[z][:, 16:32]
                      tm = tmp[z]
                      xs = t % 2
                      P.op("pool", lambda q: q.tensor_tensor(out=tm[:, 0:16], in0=x1_, in1=co, op=ALU.mult), reads=[r_krs[z], r_wa], writes=[r_tmp[z]])
                      P.op("pool", lambda q: q.tensor_tensor(out=tm[:, 16:32], in0=x2_, in1=si, op=ALU.mult), reads=[r_krs[z], r_wa], writes=[r_tmp[z]])
                      P.op("pool", lambda q: q.tensor_tensor(out=tm[:, 32:48], in0=x1_, in1=si, op=ALU.mult), reads=[r_krs[z], r_wa], writes=[r_tmp[z]])
                      P.op("pool", lambda q: q.tensor_tensor(out=tm[:, 48:64], in0=x2_, in1=co, op=ALU.mult), reads=[r_krs[z], r_wa], writes=[r_tmp[z]])
                      P.op("pool", lambda q: q.tensor_tensor(out=X4[xs][:, 32 * c:32 * c + 16], in0=tm[:, 0:16], in1=tm[:, 16:32], op=ALU.subtract),
                           reads=[r_tmp[z]], writes=[r_X4[xs]])
                      P.op("pool", lambda q: q.tensor_tensor(out=X4[xs][:, 32 * c + 16:32 * c + 32], in0=tm[:, 32:48], in1=tm[:, 48:64], op=ALU.add),
                           reads=[r_tmp[z]], writes=[r_X4[xs]])

                  def kv_s3(i):
                      t, c = items[i]
                      ti = c * NT4 + t
                      z = i % NZ
                      xs = t % 2
                      P.op("pe", lambda q: q.transpose(out=tp2[:, 0:128], in_=kvb[z][:], identity=ident[:]),
                           reads=[r_kvb[z], r_const], writes=[r_tp2a])
                      P.op("dve", lambda q: q.tensor_scalar(out=kvnT[:, ti * 128:(ti + 1) * 128], in0=tp2[:, 0:128], scalar1=gcols[:, 18:19], scalar2=None, op0=ALU.mult),
                           reads=[r_tp2a, r_const], writes=[r_kvn])
                      if c == 3:
                          P.op("pe", lambda q: q.transpose(out=tp2[:, 128:256], in_=X4[xs][:], identity=ident[:]),
                               reads=[r_X4[xs], r_const], writes=[r_tp2b])
                          P.op("dve", lambda q: q.tensor_copy(out=KPE[:, t * 128:(t + 1) * 128], in_=tp2[:, 128:256]),
                               reads=[r_tp2b], writes=[r_kpe])

                  pipeline(len(items), [kv_s0, kv_s1, kv_s2, kv_s3], extra=late)

                  own = G["own_tiles"]
                  slot2 = {}

                  def ow_s0(i):
                      slot2[i] = fr.run1(xna_d, own[i])

                  def ow_s1(i):
                      fr.run2(slot2[i])

                  def ow_s2(i):
                      s_ = slot2[i]
                      z = i % NZ
                      for k in range(8):
                          P.op("pe", lambda q, k=k: q.matmul(zps[z][:, 0:256], fr.hT[s_][:, k * 128:(k + 1) * 128], w_a[:, k * 416:k * 416 + 256], start=(k == 0), stop=(k == 7)),
                               reads=[fr.r_hT[s_], r_wa], writes=[r_z[z]], signal=(k == 7))
                      rstd(zps[z][:, 0:256], 256, junk2[:, 0:256], st2[z][:, 0:1], st2[z][:, 1:2], r_z[z], r_j2, r_st2[z])
                      P.op("dve", lambda q: q.tensor_scalar(out=cqb[z][:], in0=zps[z][:, 0:256], scalar1=st2[z][:, 1:2], scalar2=None, op0=ALU.mult),
                           reads=[r_z[z], r_st2[z]], writes=[r_cqb[z]])

                  def ow_s3(i):
                      z = i % NZ
                      for kc in range(2):
                          P.op("pe", lambda q, kc=kc: q.transpose(out=tp2[:, 256 + kc * 128:256 + (kc + 1) * 128], in_=cqb[z][:, kc * 128:(kc + 1) * 128], identity=ident[:]),
                               reads=[r_cqb[z], r_const], writes=[r_tp2c], signal=(kc == 1))
                      for kc in range(2):
                          P.op("dve", lambda q, kc=kc: q.tensor_tensor(out=cqnT[:, kc * NQ + i * 128:kc * NQ + (i + 1) * 128], in0=tp2[:, 256 + kc * 128:256 + (kc + 1) * 128],
                                                                  in1=gT_q[:, kc * 128:(kc + 1) * 128], op=ALU.mult),
                               reads=[r_tp2c, r_const], writes=[r_cqn])

                  pipeline(len(own), [ow_s0, ow_s1, ow_s2, ow_s3], extra=late)
                  while late:
                      late.pop(0)()
                  P.barrier()
                  if stop == kind + "A":
                      dumpconv(sA, "d0", kvnT[:, 0:4096 if NKV >= 4096 else NKV], 4096 if NKV >= 4096 else NKV)
                      dumpconv(sA, "d1", KPE[:, 0:NT4 * 128], NT4 * 128)
                      dumpconv(sA, "d2", cqnT[:, 0:4096 if NQ >= 4096 else NQ], 4096 if NQ >= 4096 else NQ)
                      P.barrier()
                      P.flush()
                      raise _Stop()
                  P.flush()

              with contextlib.suppress(_Skip), contextlib.ExitStack() as sB:
                  if 'B' in SKIP:
                      raise _Skip()
                  KT = sb(sB, "KT", [96, NKV], BF16)
                  V = sb(sB, "V", [128, NKT * 65], BF16)
                  QT = sb(sB, "QT", [96, NQ], BF16)
                  wq = sb(sB, "wq", [128, 2 * 768], BF16)
                  wqr = sb(sB, "wqr", [128, 2 * 768], BF16)
                  wkv = sb(sB, "wkv", [128, 1024], BF16)
                  wk_ext = sb(sB, "wk_ext", [128, 8 * 96], BF16)
                  selb = sb(sB, "selb", [128, 4 * 96], BF16)
                  qc = [sb(sB, f"qc{i}", [96, 512], F32) for i in range(2)]
                  qs = [sb(sB, f"qs{i}", [96, 512], F32) for i in range(2)]
                  t1 = sb(sB, "t1", [96, 512], F32)
                  t2 = sb(sB, "t2", [96, 512], F32)
                  NPB = 3
                  pT = [sb(sB, f"pT{i}", [128, 512], BF16) for i in range(NPB)]
                  osb = [sb(sB, f"osb{i}", [65, 512], F32) for i in range(2)]
                  rrow = [sb(sB, f"rrow{i}", [65, 512], F32) for i in range(2)]
                  onb = [sb(sB, f"onb{i}", [64, 512], BF16) for i in range(2)]
                  Sps = [ps(sB, f"S{i}", [128, 512], F32) for i in range(NPB)]
                  Ops = [ps(sB, f"O{i}", [128, 512], F32) for i in range(2)]
                  Xps = [ps(sB, f"X{i}", [128, 512], F32) for i in range(3)]
                  r_w = Res("wB")
                  r_KT, r_V, r_QT = Res("KT"), Res("V"), Res("QT")
                  r_qcs = [Res("qcs") for _ in range(2)]
                  r_t = Res("t12")
                  r_pT = [Res("pT") for _ in range(NPB)]
                  r_S = [Res("S") for _ in range(NPB)]
                  r_O = [Res("O") for _ in range(2)]
                  r_X = [Res("X") for _ in range(3)]
                  r_osb = [Res("osb") for _ in range(2)]
                  r_rrow = [Res("rrow") for _ in range(2)]
                  r_onb = [Res("onb") for _ in range(2)]
                  wst = P.dsem()
                  tst = [P.dsem() for _ in range(2)]
                  ost = [P.dsem() for _ in range(2)]
                  for kc in range(2):
                      P.dma("sp", lambda q, kc=kc: q.dma_start(out=wq[:, kc * 768:(kc + 1) * 768], in_=S["w_q_up"][kc * 128:(kc + 1) * 128, :]), wst, writes=[r_w])
                      P.dma("sp", lambda q, kc=kc: q.dma_start(out=wqr[:, kc * 768:(kc + 1) * 768], in_=S["w_q_up"][kc * 128:(kc + 1) * 128, :]), wst, writes=[r_w])
                  P.dma("sp", lambda q: q.dma_start(out=wkv[:], in_=S["w_kv_up"]), wst, writes=[r_w])
                  for c in range(4):
                      P.dma("sp", lambda q, c=c: q.dma_start(out=selb[:, c * 96:(c + 1) * 96], in_=S["sel"][c * 128:(c + 1) * 128, :]), wst, writes=[r_w])
                  for kc in range(2):
                      for h in range(8):
                          b = kc * 768 + h * 96
                          P.op("pool", lambda q, b=b: q.tensor_scalar(out=wqr[:, b + 64:b + 80], in0=wq[:, b + 80:b + 96], scalar1=-1.0, scalar2=None, op0=ALU.mult),
                               reads=[r_w], writes=[r_w])
                          P.op("pool", lambda q, b=b: q.tensor_copy(out=wqr[:, b + 80:b + 96], in_=wq[:, b + 64:b + 80]), reads=[r_w], writes=[r_w])
                  P.op("pool", lambda q: q.memset(wk_ext[:], 0.0), reads=[r_w], writes=[r_w])
                  for h in range(8):
                      P.op("pool", lambda q, h=h: q.tensor_copy(out=wk_ext[:, h * 96:h * 96 + 64], in_=wkv[:, h * 128:h * 128 + 64]), reads=[r_w], writes=[r_w])
                  P.op("pool", lambda q: q.memset(V[:], 1.0), writes=[r_V])

                  nX = 0
                  npair = 0
                  nO = 0
                  nqc = 0
                  pendB = []
                  lateB = []
                  if kind == "p":
                      lateB = late_conv_tasks(Conv(sB, 1408, engs=("dve", "pool"), store_q="sp"))
                  BB = [Xps[0], Xps[1], Sps[0], Sps[1], Sps[2], Ops[0], Ops[1]]
                  r_BB = [r_X[0], r_X[1], r_S[0], r_S[1], r_S[2], r_O[0], r_O[1]]
                  for h in range(8):
                      while pendB:
                          pendB.pop()[1]()
                      for cn in range(NKV // 512):
                          x = nX % 7
                          nX += 1
                          c = (cn * 512) // (NT4 * 128)
                          loc = cn * 512 - c * NT4 * 128
                          P.op("pe", lambda q, x=x, h=h, cn=cn: q.matmul(BB[x][0:96, :], wk_ext[:, h * 96:(h + 1) * 96], kvnT[:, cn * 512:(cn + 1) * 512], start=True, stop=False),
                               reads=[r_w, r_kvn], writes=[r_BB[x]], signal=False)
                          P.op("pe", lambda q, x=x, c=c, loc=loc: q.matmul(BB[x][0:96, :], selb[:, c * 96:(c + 1) * 96], KPE[:, loc:loc + 512], start=False, stop=True),
                               reads=[r_w, r_kpe], writes=[r_BB[x]])
                          eng = "dve" if cn % 2 == 0 else "act"
                          if eng == "dve":
                              P.op("dve", lambda q, x=x, cn=cn: q.tensor_copy(out=KT[:, cn * 512:(cn + 1) * 512], in_=BB[x][0:96, :]), reads=[r_BB[x]], writes=[r_KT])
                          else:
                              P.op("act", lambda q, x=x, cn=cn: q.copy(out=KT[:, cn * 512:(cn + 1) * 512], in_=BB[x][0:96, :]), reads=[r_BB[x]], writes=[r_KT])
                      for g8 in range(NKT // 8):
                          x = nX % 7
                          nX += 1
                          for j in range(8):
                              kt = g8 * 8 + j
                              P.op("pe", lambda q, x=x, j=j, kt=kt, h=h: q.matmul(BB[x][:, j * 64:(j + 1) * 64], kvnT[:, kt * 128:(kt + 1) * 128], wkv[:, h * 128 + 64:h * 128 + 128], start=True, stop=True),
                                   reads=[r_w, r_kvn], writes=[r_BB[x]], signal=(j == 7))
                          vdst = V[:, g8 * 8 * 65:(g8 + 1) * 8 * 65].rearrange("p (j d) -> p j d", d=65)[:, :, 0:64]
                          vsrc = BB[x][:, :].rearrange("p (j d) -> p j d", d=64)
                          if g8 % 2 == 0:
                              P.op("dve", lambda q, vdst=vdst, vsrc=vsrc: q.tensor_copy(out=vdst, in_=vsrc), reads=[r_BB[x]], writes=[r_V])
                          else:
                              P.op("act", lambda q, vdst=vdst, vsrc=vsrc: q.copy(out=vdst, in_=vsrc), reads=[r_BB[x]], writes=[r_V])
                      for (q0, N) in G["qtiles"]:
                          b = nqc % 2
                          nqc += 1
                          P.dma("sp", lambda q, b=b, q0=q0, N=N: q.dma_start(out=qc[b][:, 0:N], in_=I["qc_" + kind][:, q0:q0 + N]), tst[b], writes=[r_qcs[b]])
                          P.dma("sp", lambda q, b=b, q0=q0, N=N: q.dma_start(out=qs[b][:, 0:N], in_=I["qs_" + kind][:, q0:q0 + N]), tst[b], writes=[r_qcs[b]])
                          xa = nX % 7
                          xb_ = (nX + 1) % 7
                          nX += 2
                          for (xx, wsrc) in ((xa, wq), (xb_, wqr)):
                              for kc in range(2):
                                  P.op("pe", lambda q, xx=xx, wsrc=wsrc, kc=kc, h=h, q0=q0, N=N: q.matmul(BB[xx][0:96, 0:N], wsrc[:, kc * 768 + h * 96:kc * 768 + (h + 1) * 96],
                                                                                                    cqnT[:, kc * NQ + q0:kc * NQ + q0 + N], start=(kc == 0), stop=(kc == 1)),
                                       reads=[r_w, r_cqn], writes=[r_BB[xx]], signal=(kc == 1))
                          P.op("dve", lambda q, xa=xa, b=b, N=N: q.tensor_tensor(out=t1[:, 0:N], in0=BB[xa][0:96, 0:N], in1=qc[b][:, 0:N], op=ALU.mult),
                               reads=[r_BB[xa], r_qcs[b]], writes=[r_t])
                          P.op("dve", lambda q, xb_=xb_, b=b, N=N: q.tensor_tensor(out=t2[:, 0:N], in0=BB[xb_][0:96, 0:N], in1=qs[b][:, 0:N], op=ALU.mult),
                               reads=[r_BB[xb_], r_qcs[b]], writes=[r_t])
                          P.op("dve", lambda q, q0=q0, N=N: q.tensor_tensor(out=QT[:, q0:q0 + N], in0=t1[:, 0:N], in1=t2[:, 0:N], op=ALU.add),
                               reads=[r_t], writes=[r_QT])
                      units = []
                      for (q0, N) in G["qtiles"]:
                          units.append((q0, N, nO % 2))
                          nO += 1
                      pairs = [(u, kt) for u in range(len(units)) for kt in range(NKT)]
                      DL = NPB - 1

                      def qk(n):
                          u, kt = pairs[n]
                          q0, N, o = units[u]
                          i = (npair + n) % NPB
                          P.op("pe", lambda q: q.matmul(Sps[i][:, 0:N], KT[:, kt * 128:(kt + 1) * 128], QT[:, q0:q0 + N], start=True, stop=True),
                               reads=[r_KT, r_QT], writes=[r_S[i]])
                          P.op("act", lambda q: q.activation(out=pT[i][:, 0:N], in_=Sps[i][:, 0:N], func=AF.Exp, scale=96.0 ** -0.5),
                               reads=[r_S[i]], writes=[r_pT[i]])

                      def pv(n):
                          u, kt = pairs[n]
                          q0, N, o = units[u]
                          i = (npair + n) % NPB
                          P.op("pe", lambda q: q.matmul(Ops[o][0:65, 0:N], V[:, kt * 65:(kt + 1) * 65], pT[i][:, 0:N], start=(kt == 0), stop=(kt == NKT - 1)),
                               reads=[r_V, r_pT[i]], writes=[r_O[o]], signal=(kt == NKT - 1))
                          if kt == NKT - 1:
                              P.op("dve", lambda q: q.tensor_copy(out=osb[o][:, 0:N], in_=Ops[o][0:65, 0:N]), reads=[r_O[o]], writes=[r_osb[o]])

                              def fin2(o=o, N=N, h=h, q0=q0):
                                  x = 2
                                  P.op("pe", lambda q: q.matmul(Xps[x][0:64, 0:N], ones_f[64:65, 0:64], osb[o][64:65, 0:N], start=True, stop=True),
                                       reads=[r_osb[o], r_const], writes=[r_X[x]])
                                  P.op("dve", lambda q: q.reciprocal(out=rrow[o][0:64, 0:N], in_=Xps[x][0:64, 0:N]), reads=[r_X[x]], writes=[r_rrow[o]])
                                  P.op("dve", lambda q: q.tensor_tensor(out=onb[o][:, 0:N], in0=osb[o][0:64, 0:N], in1=rrow[o][0:64, 0:N], op=ALU.mult),
                                       reads=[r_osb[o], r_rrow[o]], writes=[r_onb[o]])
                                  P.dma("sp", lambda q: q.dma_start(out=at_d[h * 64:(h + 1) * 64, q0:q0 + N], in_=onb[o][:, 0:N]), ost[o], reads=[r_onb[o]])
                              while pendB:
                                  pendB.pop()[1]()
                              pendB.append([6, fin2])
                              if lateB:
                                  lateB.pop(0)()

                      tot = len(pairs)
                      for n in range(tot + DL):
                          if n < tot:
                              qk(n)
                          if n - DL >= 0:
                              pv(n - DL)
                          if pendB:
                              pendB[0][0] -= 1
                              if pendB[0][0] <= 0:
                                  pendB.pop()[1]()
                      npair += tot
                  while pendB:
                      pendB.pop()[1]()
                  while lateB:
                      lateB.pop(0)()
                  P.barrier()
                  if stop == kind + "B":
                      tt = sb(sB, "dbgl", [128, 4096], BF16)
                      rr = Res("dbgl")
                      nn = 4096 if NQ >= 4096 else NQ
                      P.dma("sp", lambda q: q.dma_start(out=tt[:, 0:nn], in_=at_d[0:0 + 128, 0:nn]), dst_, writes=[rr])
                      t32 = sb(sB, "dbgt2", [128, 4096], F32)
                      r32 = Res("dbgt2")
                      P.op("dve", lambda q: q.tensor_copy(out=t32[:, 0:nn], in_=tt[:, 0:nn]), reads=[rr], writes=[r32])
                      P.dma("sp", lambda q: q.dma_start(out=dbg["d0"][:, 0:nn], in_=t32[:, 0:nn]), dst_, reads=[r32])
                      P.barrier()
                      P.flush()
                      raise _Stop()
                  P.flush()

          for hh in range(2):
              with contextlib.suppress(_Skip), contextlib.ExitStack() as sC:
                  if 'C' in SKIP:
                      raise _Skip()
                  NKTN = NNA // 128
                  w_n = sb(sC, "w_n", [128, 8 * 768], BF16)
                  nqT = sb(sC, "nqT", [128, 2 * NQ], BF16)
                  nkT = sb(sC, "nkT", [128, 2 * NNA], BF16)
                  nv = sb(sC, "nv", [128, NKTN * 4 * 65], BF16)
                  tr2 = sb(sC, "tr2", [128, 4 * 1536], BF16)
                  tr2i = sb(sC, "tr2i", [128, 4 * 1536], BF16)
                  rmt = sb(sC, "rmt", [8, G["nslots"] * 128], BF16)
                  qrs = sb(sC, "qrs", [8, 512], BF16)
                  r_w = Res("wC")
                  r_nq, r_nk, r_nv = Res("nq"), Res("nk"), Res("nv")
                  wst = P.dsem()
                  for k in range(8):
                      for j in range(3):
                          c0 = 416 + j * 512 + hh * 256
                          P.dma("sp", lambda q, k=k, j=j, c0=c0: q.dma_start(out=w_n[:, k * 768 + j * 256:k * 768 + (j + 1) * 256], in_=S["w_in"][k * 128:(k + 1) * 128, c0:c0 + 256]),
                                wst, writes=[r_w])
                  for j in range(4 if '1' not in SKIP else 0):
                      for c3 in range(3):
                          P.dma("sp", lambda q, j=j, c3=c3: q.dma_start(out=tr2[:, j * 1536 + c3 * 512:j * 1536 + (c3 + 1) * 512], in_=S["tr2"][(4 * hh + j) * 128:(4 * hh + j + 1) * 128, c3 * 512:(c3 + 1) * 512]), wst, writes=[r_w])
                          P.dma("sp", lambda q, j=j, c3=c3: q.dma_start(out=tr2i[:, j * 1536 + c3 * 512:j * 1536 + (c3 + 1) * 512], in_=S["tr2i"][(4 * hh + j) * 128:(4 * hh + j + 1) * 128, c3 * 512:(c3 + 1) * 512]), wst, writes=[r_w])
                  for c0 in range(0, G["nslots"] if '1' not in SKIP else 0, 8):
                      c1 = min(c0 + 8, G["nslots"])
                      P.dma("sp", lambda q, c0=c0, c1=c1: q.dma_start(out=rmt[:, c0 * 128:c1 * 128], in_=S["rm_" + kind][:, c0 * 128:c1 * 128]), wst, writes=[r_w])
                  P.dma("sp", lambda q: q.dma_start(out=qrs[:], in_=S["qrsel"]), wst, writes=[r_w])
                  if '2' not in SKIP:
                      P.op("pool", lambda q: q.memset(nv[:], 1.0), writes=[r_nv])
                  fr = Front(sC, gT_pre, nb=6, ntp=2)
                  NZ = 3
                  zk = [ps(sC, f"c_zk{i}", [128, 512], F32) for i in range(NZ)]
                  zv = [ps(sC, f"c_zv{i}", [128, 512], F32) for i in range(NZ)]
                  r_zk = [Res("zk") for _ in range(NZ)]
                  r_zv = [Res("zv") for _ in range(NZ)]
                  q_of_na = {}
                  for oi, pieces in enumerate(G["own_tiles"]):
                      if len(pieces) == 1:
                          q_of_na[pieces[0][2] // 128] = oi * 128
                  slotc = {}

                  def c_s0(ti):
                      slotc[ti] = fr.run1(xna_d, [(0, 128, ti * 128)])

                  def c_s1(ti):
                      fr.run2(slotc[ti])

                  def c_s2(ti):
                      s_ = slotc[ti]
                      z = ti % NZ
                      for pr in range(2):
                          for k in range(8):
                              P.op("pe", lambda q, pr=pr, k=k: q.matmul(zk[z][:, pr * 128:(pr + 1) * 128], w_n[:, k * 768 + 256 + pr * 128:k * 768 + 256 + (pr + 1) * 128],
                                                                    fr.hT[s_][:, k * 128:(k + 1) * 128], start=(k == 0), stop=(k == 7)),
                                   reads=[fr.r_hT[s_], r_w], writes=[r_zk[z]], signal=False)
                      for pr in range(2):
                          for k in range(8):
                              P.op("pe", lambda q, pr=pr, k=k: q.matmul(zk[z][:, 256 + pr * 128:256 + (pr + 1) * 128], w_n[:, k * 768 + pr * 128:k * 768 + (pr + 1) * 128],
                                                                    fr.hT[s_][:, k * 128:(k + 1) * 128], start=(k == 0), stop=(k == 7)),
                                   reads=[fr.r_hT[s_], r_w], writes=[r_zk[z]], signal=(pr == 1 and k == 7))
                      for k in range(8):
                          P.op("pe", lambda q, k=k: q.matmul(zv[z][:, 0:256], fr.hT[s_][:, k * 128:(k + 1) * 128], w_n[:, k * 768 + 512:k * 768 + 768], start=(k == 0), stop=(k == 7)),
                               reads=[fr.r_hT[s_], r_w], writes=[r_zv[z]], signal=(k == 7))

                  def c_s3(ti):
                      z = ti % NZ
                      qcols = []
                      if kind == "s":
                          qcols = [(0, 128, ti * 128)]
                      else:
                          if ti in q_of_na:
                              qcols = [(0, 128, q_of_na[ti])]
                          elif ti == 2:
                              qcols = [(64, 64, 4096)]
                          elif ti == 35:
                              qcols = [(0, 64, 4160)]
                      for pr in range(2):
                          P.op("dve", lambda q, pr=pr: q.tensor_copy(out=nkT[:, pr * NNA + ti * 128:pr * NNA + (ti + 1) * 128], in_=zk[z][:, pr * 128:(pr + 1) * 128]),
                               reads=[r_zk[z]], writes=[r_nk])
                          for (c0, cn_, qd) in qcols:
                              P.op("dve", lambda q, pr=pr, c0=c0, cn_=cn_, qd=qd: q.tensor_scalar(out=nqT[:, pr * NQ + qd:pr * NQ + qd + cn_], in0=zk[z][:, 256 + pr * 128 + c0:256 + pr * 128 + c0 + cn_],
                                                                                            scalar1=0.125, scalar2=None, op0=ALU.mult),
                                   reads=[r_zk[z]], writes=[r_nq])
                      vdst = nv[:, ti * 260:(ti + 1) * 260].rearrange("p (j d) -> p j d", d=65)[:, :, 0:64]
                      vsrc = zv[z][:, 0:256].rearrange("p (j d) -> p j d", d=64)
                      P.op("act", lambda q: q.copy(out=vdst, in_=vsrc), reads=[r_zv[z]], writes=[r_nv])

                  pipeline(NKTN, [c_s0, c_s1, c_s2, c_s3])
                  P.barrier()
                  if stop == kind + "Cp":
                      dumpconv(sC, "d0", nkT[:, 0:4096], 4096)
                      dumpconv(sC, "d1", nqT[:, 0:4096], 4096)
                      dumpconv(sC, "d2", nv[:, 0:4096], 4096)
                      P.barrier()
                      P.flush()
                      raise _Stop()
                  P.flush()

                  NPB = 3
                  pT = [sb(sC, f"c_pT{i}", [128, 512], BF16) for i in range(NPB)]
                  osb = [sb(sC, f"c_osb{i}", [65, 512], F32) for i in range(2)]
                  rrow = [sb(sC, f"c_rrow{i}", [65, 512], F32) for i in range(2)]
                  onb = [sb(sC, f"c_onb{i}", [64, 512], BF16) for i in range(2)]
                  Sps = [zk[1], zk[2], zv[0]]
                  OpsL = [zv[1], zv[2]]
                  Bps = zk[0]
                  r_Bps = r_zk[0]
                  r_pT = [Res("pT") for _ in range(NPB)]
                  r_S = [r_zk[1], r_zk[2], r_zv[0]]
                  r_OL = [r_zv[1], r_zv[2]]
                  r_osb = [Res("osb") for _ in range(2)]
                  r_rrow = [Res("rrow") for _ in range(2)]
                  r_onb = [Res("onb") for _ in range(2)]
                  ost = [P.dsem() for _ in range(2)]
                  npair = 0
                  nO = 0
                  pendC = []
                  for gr in G["groups"]:
                      q0, N = gr["q0"], gr["N"]
                      for hl in range(4):
                          pr, hp = hl // 2, (hl % 2) * 64
                          h = 4 * hh + hl
                          tiles = gr["tiles"]
                          nt = len(tiles)
                          DL = NPB - 1

                          def crange(off, N=N, gr=gr):
                              if not gr["interior"]:
                                  return 0, N
                              lo, hi = max(0, off - 3), min(7, off + 5)
                              return 64 * lo, 64 * (hi + 1)

                          def qk(m, i, q0=q0, N=N, pr=pr, hp=hp, hl=hl, gr=gr):
                              ti, off = gr["tiles"][m]
                              slot = gr["slots"][m]
                              w0 = (10 - off) * 64
                              c0, c1 = crange(off)
                              n_ = c1 - c0
                              inter = gr["interior"]
                              tab = tr2i if inter else tr2
                              P.op("pe", lambda q: q.matmul(Sps[i][:, 0:n_], nkT[hp:hp + 64, pr * NNA + ti * 128:pr * NNA + (ti + 1) * 128], nqT[hp:hp + 64, pr * NQ + q0 + c0:pr * NQ + q0 + c1], start=True, stop=False),
                                   reads=[r_nk, r_nq], writes=[r_S[i]], signal=False)
                              P.op("pe", lambda q: q.matmul(Sps[i][:, 0:n_], ident[:], tab[:, hl * 1536 + w0 + c0:hl * 1536 + w0 + c1], start=False, stop=inter),
                                   reads=[r_w, r_const], writes=[r_S[i]], signal=inter)
                              if not inter:
                                  P.op("pe", lambda q: q.matmul(Sps[i][:, 0:n_], rmt[:, slot * 128:(slot + 1) * 128], qrs[:, 0:n_], start=False, stop=True),
                                       reads=[r_w], writes=[r_S[i]])
                              P.op("act", lambda q: q.activation(out=pT[i][:, 0:n_], in_=Sps[i][:, 0:n_], func=AF.Exp, scale=1.0),
                                   reads=[r_S[i]], writes=[r_pT[i]])

                          o = nO % 2
                          nO += 1
                          Ops = OpsL[o]
                          r_O = r_OL[o]

                          def pv(m, i, N=N, hl=hl, gr=gr, nt=nt, Ops=Ops, r_O=r_O):
                              ti, off = gr["tiles"][m]
                              c0, c1 = crange(off)
                              n_ = c1 - c0
                              P.op("pe", lambda q: q.matmul(Ops[0:65, c0:c1], nv[:, ti * 260 + hl * 65:ti * 260 + (hl + 1) * 65], pT[i][:, 0:n_], start=(m == 0), stop=(m == nt - 1)),
                                   reads=[r_nv, r_pT[i]], writes=[r_O], signal=(m == nt - 1))

                          base = npair
                          for m in range(nt + DL):
                              if m < nt:
                                  qk(m, (base + m) % NPB)
                              if m - DL >= 0:
                                  pv(m - DL, (base + m - DL) % NPB)
                              if m == 2 and pendC:
                                  pendC.pop()()
                          npair += nt
                          P.op("dve", lambda q, o=o, N=N, Ops=Ops: q.tensor_copy(out=osb[o][:, 0:N], in_=Ops[0:65, 0:N]), reads=[r_O], writes=[r_osb[o]])

                          def fin2(o=o, N=N, h=h, q0=q0):
                              P.op("pe", lambda q: q.matmul(Bps[0:64, 0:N], ones_f[64:65, 0:64], osb[o][64:65, 0:N], start=True, stop=True),
                                   reads=[r_osb[o], r_const], writes=[r_Bps])
                              P.op("dve", lambda q: q.reciprocal(out=rrow[o][0:64, 0:N], in_=Bps[0:64, 0:N]), reads=[r_Bps], writes=[r_rrow[o]])
                              P.op("dve", lambda q: q.tensor_tensor(out=onb[o][:, 0:N], in0=osb[o][0:64, 0:N], in1=rrow[o][0:64, 0:N], op=ALU.mult),
                                   reads=[r_osb[o], r_rrow[o]], writes=[r_onb[o]])
                              P.dma("sp", lambda q: q.dma_start(out=at_d[512 + h * 64:512 + (h + 1) * 64, q0:q0 + N], in_=onb[o][:, 0:N]), ost[o], reads=[r_onb[o]])
                          if pendC:
                              pendC.pop()()
                          pendC.append(fin2)
                  while pendC:
                      pendC.pop()()
                  P.barrier()
                  if hh == 1:
                      if stop == kind + "C":
                          tt = sb(sC, "dbgl", [128, 1024], BF16)
                          rr = Res("dbgl")
                          nn = 1024
                          P.dma("sp", lambda q: q.dma_start(out=tt[:, 0:nn], in_=at_d[512:512 + 128, 0:nn]), dst_, writes=[rr])
                          t32 = sb(sC, "dbgt2", [128, 1024], F32)
                          r32 = Res("dbgt2")
                          P.op("dve", lambda q: q.tensor_copy(out=t32[:, 0:nn], in_=tt[:, 0:nn]), reads=[rr], writes=[r32])
                          P.dma("sp", lambda q: q.dma_start(out=dbg["d0"][:, 0:nn], in_=t32[:, 0:nn]), dst_, reads=[r32])
                          P.barrier()
                          P.flush()
                          raise _Stop()
                  P.flush()

          with contextlib.ExitStack() as sD:
              NSUB = NQ // 128
              NOT_ = NOWN // 512
              w_o = sb(sD, "w_o", [128, 8 * D], BF16)
              w_dn = sb(sD, "w_dn", [128, NJ * D], BF16)
              gpost = sb(sD, "gpost", [128, D], F32)
              gfpost = sb(sD, "gfpost", [128, D], F32)
              cw = sb(sD, "cw", [128, 4 * 44], F32)
              HT = [sb(sD, f"HT{i}", [128, 8 * 514], BF16) for i in range(2)]
              NX1 = 9
              x1 = [sb(sD, f"x1_{i}", [128, D], F32) for i in range(NX1)]
              aT = [sb(sD, f"aT{i}", [128, 8 * 128], BF16) for i in range(2)]
              junk = sb(sD, "d_junk", [128, D], F32)
              tmpm = sb(sD, "d_tmpm", [128, D], F32)
              h2 = [sb(sD, f"h2_{i}", [128, D], BF16) for i in range(2)]
              stt = [sb(sD, f"d_st{i}", [128, 8], F32) for i in range(2)]
              actT = sb(sD, "actT", [128, NJ * 512], BF16)
              wu = [sb(sD, f"wu{i}", [128, 8 * 256], BF16) for i in range(2)]
              hs = [sb(sD, f"hs{i}", [128, 2 * 514], F32) for i in range(2)]
              hc = [sb(sD, f"hc{i}", [128, 2 * 512], F32) for i in range(2)]
              gl = [sb(sD, "gl0", [128, 512], F32)] * 2
              mix_ps = [ps(sD, f"mix{i}", [128, 512], F32) for i in range(2)]
              tp_ps = ps(sD, "d_tp", [128, D], BF16)
              hg_ps = [[ps(sD, f"hg{p_}{i}", [128, 512], F32) for i in range(2)] for p_ in range(2)]
              he_ps = ps(sD, "he", [128, 512], F32)
              dn_ps = mix_ps
              r_w = Res("wD")
              r_HT = [Res("HT") for _ in range(2)]
              r_x1 = [Res("x1") for _ in range(NX1)]
              r_aT = [Res("aT") for _ in range(2)]
              r_junk, r_tmpm = Res("junk"), Res("tmpm")
              r_h2 = [Res("h2") for _ in range(2)]
              r_stt = [Res("stt") for _ in range(2)]
              r_act = Res("actT")
              r_wu = [Res("wu") for _ in range(2)]
              r_hs = [Res("hs") for _ in range(2)]
              r_hc = [Res("hc") for _ in range(2)]
              r_gl = [Res("gl")] * 2
              r_mix = [Res("mix") for _ in range(2)]
              r_tp = Res("tp")
              r_hg = [[Res("hg") for _ in range(2)] for _ in range(2)]
              r_he = Res("he")
              r_dn = r_mix
              dnb = [mix_ps, hg_ps[0]]
              r_dnb = [r_mix, r_hg[0]]
              wst = P.dsem()
              xld = [P.dsem() for _ in range(2)]
              ald = [P.dsem() for _ in range(2)]
              wuld = [P.dsem() for _ in range(2)]
              yst = [P.dsem() for _ in range(2)]
              for k in range(8):
                  P.dma("sp", lambda q, k=k: q.dma_start(out=w_o[:, k * D:(k + 1) * D], in_=S["w_o"][k * 128:(k + 1) * 128, :]), wst, writes=[r_w])
              r_wdn = Res("wdn")
              wdst = P.dsem()
              for j in range(NJ):
                  P.dma("act", lambda q, j=j: q.dma_start(out=w_dn[:, j * D:(j + 1) * D], in_=S["wd"][j * 128:(j + 1) * 128, :]), wdst, writes=[r_wdn])
              P.dma("sp", lambda q: q.dma_start(out=gpost[:], in_=I["g_mix_post"].partition_broadcast(128)), wst, writes=[r_w])
              P.dma("sp", lambda q: q.dma_start(out=gfpost[:], in_=I["g_ffn_post"].partition_broadcast(128)), wst, writes=[r_w])
              for t3 in range(3):
                  P.dma("sp", lambda q, t3=t3: q.dma_start(out=cw[:, t3 * 44:(t3 + 1) * 44], in_=I["ffn_conv_w"][t3].rearrange("(c p) -> p c", p=128), allow_slow_non_contiguous=True), wst, writes=[r_w])
              P.dma("sp", lambda q: q.dma_start(out=cw[:, 132:176], in_=I["ffn_conv_b"].rearrange("(c p) -> p c", p=128), allow_slow_non_contiguous=True), wst, writes=[r_w])
              for i in range(2):
                  P.op("pool", lambda q, i=i: q.memset(HT[i][:], 0.0), writes=[r_HT[i]])

              cnt = {"d1": 0}
              x1slot = {}

              def D1_stages(sub, post):
                  n = cnt["d1"]
                  cnt["d1"] += 1
                  b = n % 2
                  xs = n % NX1
                  x1slot[sub] = xs
                  q0 = sub * 128
                  st = stt[b]

                  def dl():
                      P.dma("sp", lambda q: q.dma_start(out=aT[b][:].rearrange("p (k t) -> p k t", k=8), in_=at_d[:, q0:q0 + 128].rearrange("(k p) t -> p k t", p=128)),
                            ald[b], writes=[r_aT[b]])
                      for (p0, nr, r0) in G["own_tiles"][sub]:
                          P.dma("sp", lambda q, p0=p0, nr=nr, r0=r0: q.dma_start(out=x1[xs][p0:p0 + nr, :], in_=xna_d[r0:r0 + nr, :]), xld[b], writes=[r_x1[xs]])

                  def d0():
                      for half in range(2):
                          for k in range(8):
                              P.op("pe", lambda q, half=half, k=k: q.matmul(mix_ps[half][:, :], aT[b][:, k * 128:(k + 1) * 128], w_o[:, k * D + half * 512:k * D + (half + 1) * 512], start=(k == 0), stop=(k == 7)),
                                   reads=[r_aT[b], r_w], writes=[r_mix[half]], signal=(k == 7))

                  def d1():
                      for half in range(2):
                          P.op("act", lambda q, half=half: q.activation(out=junk[:, half * 512:(half + 1) * 512], in_=mix_ps[half][:, :], func=AF.Square, accum_out=st[:, half:half + 1]),
                               reads=[r_mix[half]], writes=[r_junk, r_stt[b]])
                      P.op("dve", lambda q: q.tensor_tensor(out=st[:, 2:3], in0=st[:, 0:1], in1=st[:, 1:2], op=ALU.add), reads=[r_stt[b]], writes=[r_stt[b]])
                      P.op("act", lambda q: q.activation(out=st[:, 3:4], in_=st[:, 2:3], func=AF.Sqrt, bias=epsb[:, 0:1], scale=1.0 / D), reads=[r_stt[b], r_const], writes=[r_stt[b]])
                      P.op("dve", lambda q: q.reciprocal(out=st[:, 3:4], in_=st[:, 3:4]), reads=[r_stt[b]], writes=[r_stt[b]])
                      for half in range(2):
                          P.op("dve", lambda q, half=half: q.scalar_tensor_tensor(out=tmpm[:, half * 512:(half + 1) * 512], in0=mix_ps[half][:, :], scalar=st[:, 3:4], in1=gpost[:, half * 512:(half + 1) * 512],
                                                                                  op0=ALU.mult, op1=ALU.mult),
                               reads=[r_mix[half], r_stt[b], r_w], writes=[r_tmpm])
                      P.op("pool", lambda q: q.tensor_tensor(out=x1[xs][:], in0=tmpm[:], in1=x1[xs][:], op=ALU.add), reads=[r_tmpm, r_x1[xs]], writes=[r_x1[xs]])

                  def d2():
                      P.op("act", lambda q: q.activation(out=junk[:], in_=x1[xs][:], func=AF.Square, accum_out=st[:, 4:5]), reads=[r_x1[xs]], writes=[r_junk, r_stt[b]])
                      P.op("act", lambda q: q.activation(out=st[:, 5:6], in_=st[:, 4:5], func=AF.Sqrt, bias=epsb[:, 0:1], scale=1.0 / D), reads=[r_stt[b], r_const], writes=[r_stt[b]])
                      P.op("dve", lambda q: q.reciprocal(out=st[:, 5:6], in_=st[:, 5:6]), reads=[r_stt[b]], writes=[r_stt[b]])
                      P.op("dve", lambda q: q.tensor_scalar(out=h2[b][:], in0=x1[xs][:], scalar1=st[:, 5:6], scalar2=None, op0=ALU.mult), reads=[r_x1[xs], r_stt[b]], writes=[r_h2[b]])

                  def d3():
                      for k in range(8):
                          P.op("pe", lambda q, k=k: q.transpose(out=tp_ps[:, k * 128:(k + 1) * 128], in_=h2[b][:, k * 128:(k + 1) * 128], identity=ident[:]),
                               reads=[r_h2[b], r_const], writes=[r_tp], signal=(k == 7))
                      post()
                  return [dl, d0, d1, d2, d3]

              def D1(sub, post):
                  for f in D1_stages(sub, post):
                      f()

              def put_ht(htb, col0, src0, ncol, flagcol=None):
                  dst = HT[htb][:].rearrange("p (k t) -> p k t", k=8)[:, :, col0:col0 + ncol]
                  src = tp_ps[:].rearrange("p (k t) -> p k t", k=8)[:, :, src0:src0 + ncol]
                  gsrc = gT_ffn[:].rearrange("p (k t) -> p k t", k=8)[:, :, 0:ncol]
                  P.op("dve", lambda q: q.tensor_tensor(out=dst, in0=src, in1=gsrc, op=ALU.mult), reads=[r_tp, r_const], writes=[r_HT[htb]])
                  if flagcol is not None:
                      P.op("dve", lambda q: q.tensor_scalar(out=dst, in0=dst, scalar1=flags[:, flagcol:flagcol + 1], scalar2=None, op0=ALU.mult),
                           reads=[r_HT[htb], r_const], writes=[r_HT[htb]])

              edge = sb(sD, "edge", [128, 8], BF16)
              r_edge = Res("edge")

              nwu = {"n": 0}

              def FFN(t, sched=None):
                  hb = t % 2
                  HTv = HT[hb]
                  pendG = []
                  sched = sched or {}
                  for j in range(NJ):
                      for f in sched.pop(j, []):
                          f()
                      wb = nwu["n"] % 2
                      nwu["n"] += 1
                      P.dma("sp", lambda q, j=j, wb=wb: q.dma_start(out=wu[wb][:], in_=S["wu"][j].rearrange("p k c -> p (k c)")), wuld[wb], writes=[r_wu[wb]])
                      pb = j % 2
                      for gu in range(2):
                          for k in range(8):
                              P.op("pe", lambda q, gu=gu, k=k, wb=wb, pb=pb: q.matmul(hg_ps[pb][gu][:, :], wu[wb][:, k * 256 + gu * 128:k * 256 + (gu + 1) * 128], HTv[:, k * 514:k * 514 + 512], start=(k == 0), stop=(k == 7)),
                                   reads=[r_wu[wb], r_HT[hb]], writes=[r_hg[pb][gu]], signal=(k == 7))
                      for gu in range(2):
                          for k in range(8):
                              P.op("pe", lambda q, gu=gu, k=k, wb=wb, pb=pb: q.matmul(he_ps[:, pb * 4 + gu * 2:pb * 4 + gu * 2 + 2], wu[wb][:, k * 256 + gu * 128:k * 256 + (gu + 1) * 128], HTv[:, k * 514 + 512:k * 514 + 514],
                                                                                    start=(k == 0), stop=(k == 7)),
                                   reads=[r_wu[wb], r_HT[hb]], writes=[r_he], signal=(gu == 1 and k == 7))
                      for gu in range(2):
                          P.op("act", lambda q, gu=gu, pb=pb: q.copy(out=hs[pb][:, gu * 514:gu * 514 + 512], in_=hg_ps[pb][gu][:, :]), reads=[r_hg[pb][gu]], writes=[r_hs[pb]])
                          P.op("act", lambda q, gu=gu, pb=pb: q.copy(out=hs[pb][:, gu * 514 + 512:gu * 514 + 514], in_=he_ps[:, pb * 4 + gu * 2:pb * 4 + gu * 2 + 2]), reads=[r_he], writes=[r_hs[pb]])
                      for gu in range(2 if 'G' not in SKIP else 0):
                          ch = gu * NJ + j
                          eng = "dve" if (gu == 0 or 'P' in SKIP) else "pool"
                          hv = hs[pb]
                          o0 = gu * 514
                          hcv = hc[pb][:, gu * 512:(gu + 1) * 512]
                          P.op(eng, lambda q, hv=hv, o0=o0, hcv=hcv, ch=ch: q.tensor_scalar(out=hcv, in0=hv[:, o0:o0 + 512], scalar1=cw[:, ch:ch + 1], scalar2=cw[:, 132 + ch:133 + ch], op0=ALU.mult, op1=ALU.add),
                               reads=[r_hs[pb], r_w], writes=[r_hc[pb]])
                          P.op("dve", lambda q, hv=hv, o0=o0, hcv=hcv, ch=ch: q.scalar_tensor_tensor(out=hcv, in0=hv[:, o0 + 1:o0 + 513], scalar=cw[:, 44 + ch:45 + ch], in1=hcv, op0=ALU.mult, op1=ALU.add),
                               reads=[r_hs[pb], r_hc[pb], r_w], writes=[r_hc[pb]])
                          P.op("dve", lambda q, hv=hv, o0=o0, hcv=hcv, ch=ch: q.scalar_tensor_tensor(out=hcv, in0=hv[:, o0 + 2:o0 + 514], scalar=cw[:, 88 + ch:89 + ch], in1=hcv, op0=ALU.mult, op1=ALU.add),
                               reads=[r_hs[pb], r_hc[pb], r_w], writes=[r_hc[pb]])
                      def gelu_part(pb=pb, j=j):
                          P.op("act", lambda q: q.activation(out=gl[pb][:], in_=hc[pb][:, 0:512], func=AF.Gelu_apprx_tanh), reads=[r_hc[pb]], writes=[r_gl[pb]])
                          P.op("dve", lambda q: q.tensor_tensor(out=actT[:, j * 512:(j + 1) * 512], in0=gl[pb][:], in1=hc[pb][:, 512:1024], op=ALU.mult),
                               reads=[r_gl[pb], r_hc[pb]], writes=[r_act])
                      if pendG:
                          pendG.pop()()
                      pendG.append(gelu_part)
                  while pendG:
                      pendG.pop()()
                  for j in sorted(sched):
                      for f in sched[j]:
                          f()
                  for s4 in range(4 if 'H' not in SKIP else 0):
                      sub = t * 4 + s4
                      xs = x1slot[sub]
                      yb = sub % 2
                      for half in range(2):
                          for j in range(NJ):
                              P.op("pe", lambda q, half=half, j=j, s4=s4: q.matmul(dnb[s4 % 2][half][:, :], actT[:, j * 512 + s4 * 128:j * 512 + (s4 + 1) * 128], w_dn[:, j * D + half * 512:j * D + (half + 1) * 512],
                                                                                   start=(j == 0), stop=(j == NJ - 1)),
                                   reads=[r_act, r_wdn], writes=[r_dnb[s4 % 2][half]], signal=(j == NJ - 1))
                      if 'J' in SKIP:
                          continue
                      st = stt[yb]
                      for half in range(2):
                          P.op("act", lambda q, half=half, st=st, s4=s4: q.activation(out=junk[:, half * 512:(half + 1) * 512], in_=dnb[s4 % 2][half][:, :], func=AF.Square, accum_out=st[:, 6 + half:7 + half]),
                               reads=[r_dnb[s4 % 2][half]], writes=[r_junk, r_stt[yb]])
                      P.op("dve", lambda q, st=st: q.tensor_tensor(out=st[:, 6:7], in0=st[:, 6:7], in1=st[:, 7:8], op=ALU.add), reads=[r_stt[yb]], writes=[r_stt[yb]])
                      P.op("act", lambda q, st=st: q.activation(out=st[:, 7:8], in_=st[:, 6:7], func=AF.Sqrt, bias=epsb[:, 0:1], scale=1.0 / D), reads=[r_stt[yb], r_const], writes=[r_stt[yb]])
                      P.op("dve", lambda q, st=st: q.reciprocal(out=st[:, 7:8], in_=st[:, 7:8]), reads=[r_stt[yb]], writes=[r_stt[yb]])
                      for half in range(2):
                          P.op("dve", lambda q, half=half, st=st, s4=s4: q.scalar_tensor_tensor(out=tmpm[:, half * 512:(half + 1) * 512], in0=dnb[s4 % 2][half][:, :], scalar=st[:, 7:8], in1=gfpost[:, half * 512:(half + 1) * 512],
                                                                                  op0=ALU.mult, op1=ALU.mult),
                               reads=[r_dnb[s4 % 2][half], r_stt[yb], r_w], writes=[r_tmpm])
                      P.op("pool", lambda q, xs=xs: q.tensor_tensor(out=x1[xs][:], in0=tmpm[:], in1=x1[xs][:], op=ALU.add), reads=[r_tmpm, r_x1[xs]], writes=[r_x1[xs]])
                      if 'I' not in SKIP:
                          P.dma("act", lambda q, xs=xs, sub=sub: q.dma_start(out=y_d[sub * 128:(sub + 1) * 128, :], in_=x1[xs][:]), yst[yb], reads=[r_x1[xs]])

              def ht3(i):
                  return HT[i][:].rearrange("p (k t) -> p k t", k=8)

              edv = edge[:].rearrange("p (k t) -> p k t", k=8)

              def post_halo():
                  put_ht(0, 0, 63, 1, flagcol=0)
                  srcv = tp_ps[:].rearrange("p (k t) -> p k t", k=8)[:, :, 64:65]
                  gsrc = gT_ffn[:].rearrange("p (k t) -> p k t", k=8)[:, :, 0:1]
                  P.op("dve", lambda q: q.tensor_tensor(out=edv, in0=srcv, in1=gsrc, op=ALU.mult), reads=[r_tp, r_const], writes=[r_edge])
                  P.op("dve", lambda q: q.tensor_scalar(out=edv, in0=edv, scalar1=flags[:, 1:2], scalar2=None, op0=ALU.mult), reads=[r_edge, r_const], writes=[r_edge])

              if G["halo"]:
                  D1(32, post_halo)
              else:
                  P.op("pool", lambda q: q.memset(edge[:], 0.0), writes=[r_edge])

              def post_main(T, s4):
                  return lambda: put_ht(T % 2, 1 + s4 * 128, 0, 128)

              def lookahead(T):
                  def post():
                      put_ht((T - 1) % 2, 513, 0, 1)
                      put_ht(T % 2, 1, 0, 128)
                      P.op("pool", lambda q: q.tensor_copy(out=ht3(T % 2)[:, :, 0:1], in_=ht3((T - 1) % 2)[:, :, 512:513]),
                           reads=[r_HT[(T - 1) % 2]], writes=[r_HT[T % 2]])
                  D1(4 * T, post)

              def right_edge(T):
                  P.op("pool", lambda q: q.tensor_copy(out=ht3(T % 2)[:, :, 513:514], in_=edv), reads=[r_edge], writes=[r_HT[T % 2]])

              for s4 in range(4):
                  D1(s4, post_main(0, s4))
              if NOT_ > 1:
                  lookahead(1)
              else:
                  right_edge(0)
              def lookahead_stages(T):
                  def post():
                      put_ht((T - 1) % 2, 513, 0, 1)
                      put_ht(T % 2, 1, 0, 128)
                      P.op("pool", lambda q: q.tensor_copy(out=ht3(T % 2)[:, :, 0:1], in_=ht3((T - 1) % 2)[:, :, 512:513]),
                           reads=[r_HT[(T - 1) % 2]], writes=[r_HT[T % 2]])
                  return D1_stages(4 * T, post)

              slots_j = [(1, 2, 4, 6, 8), (5, 9, 11, 13, 15), (10, 14, 16, 18, 20)]
              for t in range(NOT_):
                  sched = {}
                  if t + 1 < NOT_:
                      for s4 in range(1, 4):
                          st5 = D1_stages(4 * (t + 1) + s4, post_main(t + 1, s4))
                          for jj, f in zip(slots_j[s4 - 1], st5):
                              sched.setdefault(jj, []).append(f)
                  last_stage = None
                  if t + 2 < NOT_:
                      st5 = lookahead_stages(t + 2)
                      for jj, f in zip((15, 19, 21, 22), st5[:4]):
                          sched.setdefault(jj, []).append(f)
                      last_stage = st5[4]
                  FFN(t, sched)
                  if last_stage is not None:
                      last_stage()
                  elif t + 1 < NOT_:
                      right_edge(t + 1)
              P.barrier()
              P.flush()
              if stop == kind + "D":
                  raise _Stop()


    except _Stop:
        pass

    P.barrier()
    P.flush()
    print("instr counts", P.ninstr, "sems", len(P.allsems) + len(P.dmastates))
    return nc, P, gs


def _rope_tabs(pos):
    inv = (1.0 / (np.float32(10000.0) ** (np.arange(0, 32, 2, dtype=np.float32) / np.float32(32)))).astype(np.float32)
    ang = pos.astype(np.float32)[:, None] * inv[None, :]
    return np.cos(ang).astype(np.float32), np.sin(ang).astype(np.float32)


def _rm_entry(abs_q_rows, abs_key_row0, rows_total):
    m = np.full((8, 128), NEG, np.float32)
    for gi, r in enumerate(abs_q_rows):
        if r is None or r < 0 or r >= rows_total:
            m[gi, :] = 0.0
            continue
        rs = min(max(r - 4, 0), rows_total - 8)
        for krl in range(2):
            kr = abs_key_row0 + krl
            if 0 <= kr < rows_total and rs <= kr < rs + 8:
                m[gi, krl * 64:(krl + 1) * 64] = 0.0
    return m


def _tr2_table(rpb, interior=False):
    H = rpb.shape[0]
    T = np.full((H, 128, 24, 64), NEG, np.float32)
    kc = np.arange(64)[:, None]
    qc = np.arange(64)[None, :]
    qs = np.clip(qc - 8, 0, 48)
    colv = (kc >= qs) & (kc < qs + 16)
    dc = np.clip(kc - qc + 15, 0, 30)
    for krl in range(2):
        for ei in range(24):
            dr = 7 + krl - (ei - 10)
            if (3 <= dr <= 10) if interior else (0 <= dr <= 14):
                blk = rpb[:, dr, :][:, dc]
                blk = np.where(colv[None], blk, np.float32(NEG))
                T[:, krl * 64:(krl + 1) * 64, ei, :] = blk
    return T.reshape(H, 128, 24 * 64)


_CACHE = {}


def make_in_maps(x_prompt, x_sample, g_mix_pre, w_in, g_q_lat, w_q_up, g_kv_lat, w_kv_up, na_rpb, w_o,
                 g_mix_post, g_ffn_pre, w_ffn_up, ffn_conv_w, ffn_conv_b, w_ffn_down, g_ffn_post):
    f32 = np.float32
    x_prompt = np.asarray(x_prompt, f32)
    x_sample = np.asarray(x_sample, f32)

    shared = {
        "g_mix_pre": np.asarray(g_mix_pre[0], f32), "w_in": np.asarray(w_in[0], f32), "g_q_lat": np.asarray(g_q_lat[0], f32),
        "w_q_up": np.asarray(w_q_up[0], f32), "g_kv_lat": np.asarray(g_kv_lat[0], f32), "w_kv_up": np.asarray(w_kv_up[0], f32),
        "w_o": np.asarray(w_o[0], f32), "g_mix_post": np.asarray(g_mix_post[0], f32), "g_ffn_pre": np.asarray(g_ffn_pre[0], f32),
        "w_ffn_up": np.asarray(w_ffn_up[0], f32), "ffn_conv_w": np.asarray(ffn_conv_w[0], f32), "ffn_conv_b": np.asarray(ffn_conv_b[0], f32),
        "w_ffn_down": np.asarray(w_ffn_down[0], f32), "g_ffn_post": np.asarray(g_ffn_post[0], f32),
        "tr2": _tr2_table(np.asarray(na_rpb[0], f32)),
        "tr2i": _tr2_table(np.asarray(na_rpb[0], f32), interior=True),
        "ident": np.eye(128, dtype=f32),
    }
    sel = np.zeros((4, 128, 96), f32)
    for c in range(4):
        for r in range(32):
            sel[c, 32 * c + r, 64 + r] = 1.0
    shared["sel"] = sel
    qrsel = np.zeros((8, 512), f32)
    for gi in range(8):
        qrsel[gi, gi * 64:(gi + 1) * 64] = 1.0
    shared["qrsel"] = qrsel

    in_maps = []
    for core in range(8):
        pb, pq = core // 4, core % 4
        m = dict(shared)
        G = GEO["p"]
        xb = x_prompt[pb]
        order = [pq] + [i for i in range(4) if i != pq]
        pos_kv = np.concatenate([np.arange(o * 4096, (o + 1) * 4096) for o in order])
        m["xkv_p"] = np.ascontiguousarray(xb[pos_kv])
        r0 = pq * 64
        xg = xb.reshape(256, 64, D)
        xna = np.zeros((74, 64, D), f32)
        lo, hi = r0 - 6, r0 + 68
        a, b = max(lo, 0), min(hi, 256)
        xna[a - lo:b - lo] = xg[a:b]
        m["xna_p"] = xna.reshape(74 * 64, D)
        ck, sk = _rope_tabs(pos_kv)
        m["cosk_p"] = np.ascontiguousarray(ck.reshape(128, 128, 16).transpose(1, 0, 2).reshape(128, 128 * 16))
        m["sink_p"] = np.ascontiguousarray(sk.reshape(128, 128, 16).transpose(1, 0, 2).reshape(128, 128 * 16))
        pos_q = np.concatenate([np.arange(pq * 4096, (pq + 1) * 4096), np.arange(pq * 4096 - 64, pq * 4096), np.arange((pq + 1) * 4096, (pq + 1) * 4096 + 64)])
        cq, sq = _rope_tabs(np.clip(pos_q, 0, 16383))
        qc_t = np.ones((96, G["NQ"]), f32)
        qs_t = np.zeros((96, G["NQ"]), f32)
        qc_t[64:80] = cq.T
        qc_t[80:96] = cq.T
        qs_t[64:80] = sq.T
        qs_t[80:96] = sq.T
        m["qc_p"], m["qs_p"] = qc_t, qs_t
        rm = np.zeros((8, G["nslots"] * 128), f32)
        for gi, gr in enumerate(G["groups"]):
            if gi < 8:
                qrows = [r0 + 8 * gi + i for i in range(8)]
            elif gi == 8:
                qrows = [r0 - 1] + [None] * 7
            else:
                qrows = [r0 + 64] + [None] * 7
            for (ti, off), slot in zip(gr["tiles"], gr["slots"]):
                rm[:, slot * 128:(slot + 1) * 128] = _rm_entry(qrows, (r0 - 6) + 2 * ti, 256)
        m["rm_p"] = rm
        m["flags"] = np.tile(np.array([[1.0 if pq > 0 else 0.0, 1.0 if pq < 3 else 0.0]], f32), (128, 1))
        G = GEO["s"]
        xs = x_sample[core]
        m["xkv_s"] = xs
        m["xna_s"] = xs
        pos = np.arange(2048)
        ck, sk = _rope_tabs(pos)
        m["cosk_s"] = np.ascontiguousarray(ck.reshape(16, 128, 16).transpose(1, 0, 2).reshape(128, 16 * 16))
        m["sink_s"] = np.ascontiguousarray(sk.reshape(16, 128, 16).transpose(1, 0, 2).reshape(128, 16 * 16))
        qc_t = np.ones((96, 2048), f32)
        qs_t = np.zeros((96, 2048), f32)
        qc_t[64:80] = ck.T
        qc_t[80:96] = ck.T
        qs_t[64:80] = sk.T
        qs_t[80:96] = sk.T
        m["qc_s"], m["qs_s"] = qc_t, qs_t
        rm = np.zeros((8, G["nslots"] * 128), f32)
        for gi, gr in enumerate(G["groups"]):
            qrows = [8 * gi + i for i in range(8)]
            for (ti, off), slot in zip(gr["tiles"], gr["slots"]):
                rm[:, slot * 128:(slot + 1) * 128] = _rm_entry(qrows, 2 * ti, 32)
        m["rm_s"] = rm
        in_maps.append(m)
    return in_maps


def kernel(**inputs):
    f32 = np.float32
    in_maps = make_in_maps(**inputs)
    if "nc" not in _CACHE:
        _CACHE["nc"] = build_program()
    nc, P, gs = _CACHE["nc"]
    res = run_bass_kernel_spmd(nc, in_maps, core_ids=list(range(8)))
    y_p = np.zeros((2, 16384, D), f32)
    y_s = np.zeros((8, 2048, D), f32)
    for core in range(8):
        pb, pq = core // 4, core % 4
        y_p[pb, pq * 4096:(pq + 1) * 4096] = res.results[core]["y_p"]
        y_s[core] = res.results[core]["y_s"]
    return (y_p, y_s)
```

```python
import contextlib
import numpy as np
import concourse.bass as bass
import concourse.mybir as mybir
from concourse.bass_utils import run_bass_kernel_spmd

F32 = mybir.dt.float32
BF16 = mybir.dt.bfloat16
AF = mybir.ActivationFunctionType
ALU = mybir.AluOpType

ENGS = ("pe", "act", "dve", "pool", "sp")
SEM_ROT = 24000
NEG = -30000.0
EPS = 1e-6
D = 1024
DFF = 2816
NJ = 22


class Res:
    __slots__ = ("name", "w", "r")

    def __init__(self, name):
        self.name = name
        self.w = None
        self.r = {}


class Prog:
    def __init__(self, nc):
        self.nc = nc
        self.stack = contextlib.ExitStack()
        self.ops = {e: [] for e in ENGS}
        self.esem = {}
        self.own = {e: set() for e in ENGS}
        self.ecnt = {e: 0 for e in ENGS}
        self.lazy = {e: False for e in ENGS}
        self.seen = {e: {} for e in ENGS}
        self.allsems = []
        self.dmastates = []
        for e in ENGS:
            self._newsem(e)
        self.ninstr = {e: 0 for e in ENGS}

    def _newsem(self, e):
        s = self.stack.enter_context(self.nc.semaphore())
        self.esem[e] = s
        self.own[e].add(id(s))
        self.ecnt[e] = 0
        self.allsems.append(s)

    def dsem(self):
        s = self.stack.enter_context(self.nc.semaphore())
        st = [s, 0]
        self.dmastates.append(st)
        return st

    def _need(self, eng, ev, waits):
        if ev is None:
            return
        sem, val = ev
        if id(sem) in self.own[eng]:
            if eng == "pe" or eng == "sp":
                return
            if sem is self.esem[eng] and val > self.ecnt[eng]:
                return
        if self.seen[eng].get(id(sem), 0) >= val:
            return
        cur = waits.get(id(sem), (sem, 0))
        if val > cur[1]:
            waits[id(sem)] = (sem, val)

    def _deps(self, eng, reads, writes):
        waits = {}
        for R in reads:
            self._need(eng, R.w, waits)
        for R in writes:
            self._need(eng, R.w, waits)
            for s, v in R.r.values():
                self._need(eng, (s, v), waits)
        for k, (s, v) in waits.items():
            self.seen[eng][k] = v
        return list(waits.values())

    def _mark(self, ev, reads, writes):
        sem, val = ev
        for R in reads:
            old = R.r.get(id(sem))
            if old is None or old[1] < val:
                R.r[id(sem)] = (sem, val)
        for R in writes:
            R.w = ev
            R.r = {}

    def op(self, eng, fn, reads=(), writes=(), signal=True):
        waits = self._deps(eng, reads, writes)
        if signal:
            if self.ecnt[eng] >= SEM_ROT and not self.lazy[eng]:
                self._newsem(eng)
            self.ecnt[eng] += 1
            val = self.ecnt[eng]
            self.lazy[eng] = False
        else:
            val = self.ecnt[eng] + 1
            self.lazy[eng] = True
        sem = self.esem[eng]
        self.ops[eng].append((waits, fn, sem if signal else None, 1))
        self.ninstr[eng] += 1
        ev = (sem, val)
        self._mark(ev, reads, writes)
        return ev

    def dma(self, eng, fn, st, reads=(), writes=()):
        waits = self._deps(eng, reads, writes)
        if st[1] >= SEM_ROT:
            st[0] = self.stack.enter_context(self.nc.semaphore())
            st[1] = 0
        st[1] += 16
        ev = (st[0], st[1])
        self.ops[eng].append((waits, fn, st[0], 16))
        self.ninstr[eng] += 1
        self._mark(ev, reads, writes)
        return ev

    def barrier(self, extra_events=()):
        evs = []
        for e in ENGS:
            if self.lazy[e]:
                raise RuntimeError("barrier with pending lazy event on " + e)
            if self.ecnt[e] > 0:
                evs.append((self.esem[e], self.ecnt[e]))
        for st in self.dmastates:
            if st[1] > 0:
                evs.append((st[0], st[1]))
        evs.extend(extra_events)
        for e in ENGS:
            waits = {}
            for ev in evs:
                self._need(e, ev, waits)
            for k, (s, v) in waits.items():
                self.seen[e][k] = v
            self.ops[e].append((list(waits.values()), None, None, 0))

    def flush(self, name=None):
        nc = self.nc
        engobj = {"pe": "tensor", "act": "scalar", "dve": "vector", "pool": "gpsimd", "sp": "sync"}
        self.nflush = getattr(self, "nflush", 0) + 1
        scope = nc.named_scope(name or f"ph{self.nflush}")
        with scope, nc.Block() as block:
            for e in ENGS:
                ops = self.ops[e]
                if not ops:
                    continue

                def body(q, ops=ops):
                    for waits, fn, sem, inc in ops:
                        for s, v in waits:
                            q.wait_ge(s, v)
                        if fn is not None:
                            ins = fn(q)
                            if sem is not None:
                                ins.then_inc(sem, inc)
                getattr(block, engobj[e])(body)
        self.ops = {e: [] for e in ENGS}


def job_geometry(kind):
    g = {}
    if kind == "p":
        g["NKV"] = 16384
        g["NQ"] = 4224
        g["NOWN"] = 4096
        g["NNA"] = 4736
        g["qtiles"] = [(i * 512, 512) for i in range(8)] + [(4096, 128)]
        own = [[(0, 128, 384 + 128 * i)] for i in range(32)]
        own.append([(0, 64, 320), (64, 64, 4480)])
        g["own_tiles"] = own
        groups = []
        for t in range(8):
            tiles = [(4 * t + 1 + m, (8 * t + 2 + 2 * m) - (6 + 8 * t)) for m in range(8)]
            groups.append(dict(q0=512 * t, N=512, tiles=tiles))
        groups.append(dict(q0=4096, N=64, tiles=[(m, 2 * m - 5) for m in range(5)]))
        groups.append(dict(q0=4160, N=64, tiles=[(33 + m, 66 + 2 * m - 70) for m in range(4)]))
        g["groups"] = groups
        g["na_of_q"] = lambda q: (384 + q) if q < 4096 else ((320 + q - 4096) if q < 4160 else (4480 + q - 4160))
        g["halo"] = True
    else:
        g["NKV"] = 2048
        g["NQ"] = 2048
        g["NOWN"] = 2048
        g["NNA"] = 2048
        g["qtiles"] = [(i * 512, 512) for i in range(4)]
        g["own_tiles"] = [[(0, 128, 128 * i)] for i in range(16)]
        groups = []
        for t in range(4):
            tiles = []
            for idx in range(4 * t - 2, 4 * t + 6):
                if 0 <= idx < 16:
                    tiles.append((idx, 2 * idx - 8 * t))
            groups.append(dict(q0=512 * t, N=512, tiles=tiles))
        g["groups"] = groups
        g["na_of_q"] = lambda q: q
        g["halo"] = False
    for gi, gr in enumerate(g["groups"]):
        gr["interior"] = (kind == "p" and 1 <= gi <= 6) or (kind == "s" and 1 <= gi <= 2)
        if gr["interior"]:
            gr["tiles"] = sorted(gr["tiles"], key=lambda to: (to[1] != 2, to[1]))
    slot = 0
    for gr in g["groups"]:
        gr["slots"] = list(range(slot, slot + len(gr["tiles"])))
        slot += len(gr["tiles"])
    g["nslots"] = slot
    return g


GEO = {"p": job_geometry("p"), "s": job_geometry("s")}


class _Stop(Exception):
    pass


class _Skip(Exception):
    pass


import os
SKIP = os.environ.get('SKIP', '')


def build_program(stop=None):
    nc = bass.Bass("TRN2", target_bir_lowering=False)

    def din(name, shape, dt=F32):
        return nc.dram_tensor(name, list(shape), dt, kind="ExternalInput").ap()

    def dout(name, shape, dt=F32):
        return nc.dram_tensor(name, list(shape), dt, kind="ExternalOutput").ap()

    def dscratch(name, shape, dt):
        return nc.dram_tensor(name, list(shape), dt, kind="Internal").ap()

    I = {}
    for k in ("p", "s"):
        G = GEO[k]
        I["xkv_" + k] = din("xkv_" + k, [G["NKV"], D])
        I["xna_" + k] = din("xna_" + k, [G["NNA"], D])
        I["cosk_" + k] = din("cosk_" + k, [128, (G["NKV"] // 128) * 16])
        I["sink_" + k] = din("sink_" + k, [128, (G["NKV"] // 128) * 16])
        I["qc_" + k] = din("qc_" + k, [96, G["NQ"]])
        I["qs_" + k] = din("qs_" + k, [96, G["NQ"]])
        I["rm_" + k] = din("rm_" + k, [8, G["nslots"] * 128])
        I["y_" + k] = dout("y_" + k, [G["NOWN"], D])
        I["at_" + k] = dscratch("at_" + k, [D, G["NQ"]], BF16)
    I["flags"] = din("flags", [128, 2])
    I["tr2"] = din("tr2", [8, 128, 24 * 64])
    I["tr2i"] = din("tr2i", [8, 128, 24 * 64])
    I["sel"] = din("sel", [4, 128, 96])
    I["ident"] = din("ident", [128, 128])
    I["qrsel"] = din("qrsel", [8, 512])
    for nm, shp in (("g_mix_pre", [D]), ("w_in", [D, 1952]), ("g_q_lat", [256]), ("w_q_up", [256, 768]),
                    ("g_kv_lat", [128]), ("w_kv_up", [128, 1024]), ("w_o", [D, D]), ("g_mix_post", [D]),
                    ("g_ffn_pre", [D]), ("w_ffn_up", [D, 2 * DFF]), ("ffn_conv_w", [3, 2 * DFF]),
                    ("ffn_conv_b", [2 * DFF]), ("w_ffn_down", [DFF, D]), ("g_ffn_post", [D])):
        I[nm] = din(nm, shp)

    P = Prog(nc)
    gs = contextlib.ExitStack()

    uid = [0]

    def sb(stack, name, shape, dt):
        uid[0] += 1
        return stack.enter_context(nc.sbuf_tensor(f"sb{uid[0]}_{name}", list(shape), dt))

    def ps(stack, name, shape, dt=F32):
        uid[0] += 1
        return stack.enter_context(nc.psum_tensor(f"ps{uid[0]}_{name}", list(shape), dt))

    ident = sb(gs, "ident", [128, 128], BF16)
    ones_f = sb(gs, "ones_f", [128, 128], F32)
    gT_pre = sb(gs, "gT_pre", [128, 8 * 128], F32)
    gT_ffn = sb(gs, "gT_ffn", [128, 8 * 128], F32)
    gT_q = sb(gs, "gT_q", [128, 2 * 128], F32)
    gcols = sb(gs, "gcols", [128, 32], F32)
    flags = sb(gs, "flags", [128, 2], F32)
    epsb = sb(gs, "epsb", [128, 1], F32)
    r_const = Res("const")
    cst = P.dsem()

    ident32 = sb(gs, "ident32", [128, 128], F32)
    r_id32 = Res("id32")
    P.dma("sp", lambda q: q.dma_start(out=ident32[:], in_=I["ident"]), cst, writes=[r_id32])
    P.op("dve", lambda q: q.tensor_copy(out=ident[:], in_=ident32[:]), reads=[r_id32], writes=[r_const])
    P.dma("sp", lambda q: q.dma_start(out=gcols[:, 0:8], in_=I["g_mix_pre"].rearrange("(k p) -> p k", p=128), allow_slow_non_contiguous=True), cst, writes=[r_const])
    P.dma("sp", lambda q: q.dma_start(out=gcols[:, 8:16], in_=I["g_ffn_pre"].rearrange("(k p) -> p k", p=128), allow_slow_non_contiguous=True), cst, writes=[r_const])
    P.dma("sp", lambda q: q.dma_start(out=gcols[:, 16:18], in_=I["g_q_lat"].rearrange("(k p) -> p k", p=128), allow_slow_non_contiguous=True), cst, writes=[r_const])
    P.dma("sp", lambda q: q.dma_start(out=gcols[:, 18:19], in_=I["g_kv_lat"].rearrange("(k p) -> p k", p=128), allow_slow_non_contiguous=True), cst, writes=[r_const])
    P.dma("sp", lambda q: q.dma_start(out=flags[:], in_=I["flags"]), cst, writes=[r_const])
    P.op("dve", lambda q: q.memset(ones_f[:], 1.0), writes=[r_const])
    P.op("dve", lambda q: q.memset(epsb[:], EPS), writes=[r_const])
    for k in range(8):
        P.op("dve", lambda q, k=k: q.tensor_scalar(out=gT_pre[:, k * 128:(k + 1) * 128], in0=ones_f[:], scalar1=gcols[:, k:k + 1], scalar2=None, op0=ALU.mult),
             reads=[r_const], writes=[r_const])
        P.op("dve", lambda q, k=k: q.tensor_scalar(out=gT_ffn[:, k * 128:(k + 1) * 128], in0=ones_f[:], scalar1=gcols[:, 8 + k:9 + k], scalar2=None, op0=ALU.mult),
             reads=[r_const], writes=[r_const])
    for k in range(2):
        P.op("dve", lambda q, k=k: q.tensor_scalar(out=gT_q[:, k * 128:(k + 1) * 128], in0=ones_f[:], scalar1=gcols[:, 16 + k:17 + k], scalar2=None, op0=ALU.mult),
             reads=[r_const], writes=[r_const])

    def rstd(src_ap, n, junk_ap, ss_ap, rs_ap, r_src, r_junk, r_stat):
        P.op("act", lambda q: q.activation(out=junk_ap, in_=src_ap, func=AF.Square, accum_out=ss_ap),
             reads=[r_src], writes=[r_junk, r_stat])
        P.op("act", lambda q: q.activation(out=rs_ap, in_=ss_ap, func=AF.Sqrt, bias=epsb[:, 0:1], scale=1.0 / n),
             reads=[r_stat, r_const], writes=[r_stat])
        P.op("dve", lambda q: q.reciprocal(out=rs_ap, in_=rs_ap), reads=[r_stat], writes=[r_stat])

    class Front:
        def __init__(self, st, gT, nb=3, ntp=2):
            self.NB = nb
            self.NTP = ntp
            self.n2 = 0
            self.gT = gT
            self.xt = [sb(st, f"f_xt{i}", [128, D], F32) for i in range(self.NB)]
            self.junk = sb(st, "f_junk", [128, D], F32)
            self.hb = [sb(st, f"f_hb{i}", [128, D], BF16) for i in range(self.NB)]
            self.hT = [sb(st, f"f_hT{i}", [128, D], BF16) for i in range(self.NB)]
            self.stat = [sb(st, f"f_st{i}", [128, 2], F32) for i in range(self.NB)]
            self.tp = [ps(st, f"f_tp{i}", [128, D], BF16) for i in range(self.NTP)]
            self.r_x = [Res("fx") for _ in range(self.NB)]
            self.r_j = Res("fj")
            self.r_s = [Res("fs") for _ in range(self.NB)]
            self.r_hb = [Res("fhb") for _ in range(self.NB)]
            self.r_tp = [Res("ftp") for _ in range(self.NTP)]
            self.r_hT = [Res("fhT") for _ in range(self.NB)]
            self.ld = [P.dsem() for _ in range(self.NB)]
            self.n = 0

        def run1(self, xd, pieces):
            s = self.n % self.NB
            self.n += 1
            for (p0, nr, r0) in pieces:
                P.dma("sp", lambda q, p0=p0, nr=nr, r0=r0: q.dma_start(out=self.xt[s][p0:p0 + nr, :], in_=xd[r0:r0 + nr, :]),
                      self.ld[s], writes=[self.r_x[s]])
            rstd(self.xt[s][:], D, self.junk[:], self.stat[s][:, 0:1], self.stat[s][:, 1:2], self.r_x[s], self.r_j, self.r_s[s])
            P.op("dve", lambda q: q.tensor_scalar(out=self.hb[s][:], in0=self.xt[s][:], scalar1=self.stat[s][:, 1:2], scalar2=None, op0=ALU.mult),
                 reads=[self.r_x[s], self.r_s[s]], writes=[self.r_hb[s]])
            return s

        def run2(self, s):
            tpi = self.n2 % self.NTP
            self.n2 += 1
            for k in range(8):
                P.op("pe", lambda q, k=k: q.transpose(out=self.tp[tpi][:, k * 128:(k + 1) * 128], in_=self.hb[s][:, k * 128:(k + 1) * 128], identity=ident[:]),
                     reads=[self.r_hb[s], r_const], writes=[self.r_tp[tpi]], signal=(k == 7))
            P.op("dve", lambda q: q.tensor_tensor(out=self.hT[s][:], in0=self.tp[tpi][:], in1=self.gT[:], op=ALU.mult),
                 reads=[self.r_tp[tpi], r_const], writes=[self.r_hT[s]])
            return s

        def run(self, xd, pieces):
            return self.run2(self.run1(xd, pieces))

    def pipeline(nitems, stages, extra=None):
        K = len(stages)
        for step in range(nitems + K - 1):
            for k in range(K):
                i = step - k
                if 0 <= i < nitems:
                    stages[k](i)
            if extra:
                extra.pop(0)()

    S = {}
    S["w_in"] = dscratch("s_w_in", [D, 1952], BF16)
    S["w_q_up"] = dscratch("s_w_q_up", [256, 768], BF16)
    S["w_kv_up"] = dscratch("s_w_kv_up", [128, 1024], BF16)
    S["w_o"] = dscratch("s_w_o", [D, D], BF16)
    S["wd"] = dscratch("s_wd", [DFF, D], BF16)
    S["wu"] = dscratch("s_wu", [NJ, 128, 8, 256], BF16)
    S["tr2"] = dscratch("s_tr2", [8 * 128, 1536], BF16)
    S["tr2i"] = dscratch("s_tr2i", [8 * 128, 1536], BF16)
    S["rm_p"] = dscratch("s_rm_p", [8, GEO["p"]["nslots"] * 128], BF16)
    S["rm_s"] = dscratch("s_rm_s", [8, GEO["s"]["nslots"] * 128], BF16)
    S["qrsel"] = dscratch("s_qrsel", [8, 512], BF16)
    S["sel"] = dscratch("s_sel", [4 * 128, 96], BF16)
    class Conv:
        def __init__(self, stack, CB, nstg=3, engs=("dve", "pool", "act"), store_q="act"):
            self.engs = engs
            self.store_q = store_q
            self.CB = CB
            self.n = nstg
            self.s32 = [sb(stack, f"stg32_{i}", [128, CB], F32) for i in range(nstg)]
            self.s16 = [sb(stack, f"stg16_{i}", [128, CB], BF16) for i in range(nstg)]
            self.r32 = [Res("s32") for _ in range(nstg)]
            self.r16 = [Res("s16") for _ in range(nstg)]
            self.lds = [P.dsem() for _ in range(nstg)]
            self.sts = [P.dsem() for _ in range(nstg)]
            self.cn = 0

        def block(self, src_ap, nrows, ncols, store_fn):
            i = self.cn % self.n
            e = self.engs[self.cn % len(self.engs)]
            self.cn += 1
            s32, s16 = self.s32[i], self.s16[i]
            P.dma("sp", lambda q: q.dma_start(out=s32[0:nrows, 0:ncols], in_=src_ap), self.lds[i], writes=[self.r32[i]])
            if e == "act":
                P.op("act", lambda q: q.copy(out=s16[0:nrows, 0:ncols], in_=s32[0:nrows, 0:ncols]), reads=[self.r32[i]], writes=[self.r16[i]])
            else:
                P.op(e, lambda q: q.tensor_copy(out=s16[0:nrows, 0:ncols], in_=s32[0:nrows, 0:ncols]), reads=[self.r32[i]], writes=[self.r16[i]])
            P.dma(self.store_q, store_fn(s16), self.sts[i], reads=[self.r16[i]])

        def tasks2d(self, src, dst, R, C):
            out = []
            for r0 in range(0, R, 128):
                nr = min(128, R - r0)
                for c0 in range(0, C, self.CB):
                    ncl = min(self.CB, C - c0)
                    out.append(lambda r0=r0, nr=nr, c0=c0, ncl=ncl: self.block(
                        src[r0:r0 + nr, c0:c0 + ncl], nr, ncl,
                        lambda t: (lambda q: q.dma_start(out=dst[r0:r0 + nr, c0:c0 + ncl], in_=t[0:nr, 0:ncl]))))
            return out

    with contextlib.ExitStack() as sW:
        cv = Conv(sW, 2816)
        early = []
        early += cv.tasks2d(I["w_in"], S["w_in"], D, 1952)
        early += cv.tasks2d(I["w_q_up"], S["w_q_up"], 256, 768)
        early += cv.tasks2d(I["w_kv_up"], S["w_kv_up"], 128, 1024)
        early += cv.tasks2d(I["sel"].rearrange("c p n -> (c p) n"), S["sel"], 512, 96)
        early += cv.tasks2d(I["tr2"].rearrange("h p n -> (h p) n"), S["tr2"], 1024, 1536)
        early += cv.tasks2d(I["tr2i"].rearrange("h p n -> (h p) n"), S["tr2i"], 1024, 1536)
        early += cv.tasks2d(I["rm_p"], S["rm_p"], 8, GEO["p"]["nslots"] * 128)
        early += cv.tasks2d(I["rm_s"], S["rm_s"], 8, GEO["s"]["nslots"] * 128)
        early += cv.tasks2d(I["qrsel"], S["qrsel"], 8, 512)
        for f in early:
            f()
        P.barrier()
        P.flush()

    def late_conv_tasks(cv2):
        out = []
        out += cv2.tasks2d(I["w_o"], S["w_o"], D, D)
        out += cv2.tasks2d(I["w_ffn_down"], S["wd"], DFF, D)
        wu_v = S["wu"].rearrange("j p k c -> p j k c")
        nh = DFF // cv2.CB
        jb = cv2.CB // 128
        for k in range(8):
            for gu in range(2):
                for hf in range(nh):
                    out.append(lambda k=k, gu=gu, hf=hf: cv2.block(
                        I["w_ffn_up"][k * 128:(k + 1) * 128, gu * DFF + hf * cv2.CB:gu * DFF + (hf + 1) * cv2.CB], 128, cv2.CB,
                        lambda t: (lambda q: q.dma_start(out=wu_v[:, hf * jb:(hf + 1) * jb, k, gu * 128:(gu + 1) * 128], in_=t[:, 0:cv2.CB].rearrange("p (j c) -> p j c", c=128)))))
        return out

    dbg = {}
    if stop is not None:
        dbg['d0'] = dout('dbg0', [128, 4096], F32)
        dbg['d1'] = dout('dbg1', [128, 4096], F32)
        dbg['d2'] = dout('dbg2', [128, 4096], F32)
    dst_ = P.dsem()

    def dump(key, src_ap, ncol, npart=128):
        P.dma('sp', lambda q: q.dma_start(out=dbg[key][0:npart, 0:ncol], in_=src_ap), dst_)

    def dumpconv(stack, key, src_ap, ncol, npart=128):
        t = sb(stack, 'dbgt', [128, 4096], F32)
        r = Res('dbgt')
        P.op('dve', lambda q: q.tensor_copy(out=t[0:npart, 0:ncol], in_=src_ap), writes=[r])
        P.dma('sp', lambda q: q.dma_start(out=dbg[key][0:npart, 0:ncol], in_=t[0:npart, 0:ncol]), dst_, reads=[r])

    try:
      for kind in ("p", "s"):
          G = GEO[kind]
          NKV, NQ, NOWN, NNA = G["NKV"], G["NQ"], G["NOWN"], G["NNA"]
          NKT = NKV // 128
          NT4 = NKT // 4
          xkv_d, xna_d, at_d, y_d = I["xkv_" + kind], I["xna_" + kind], I["at_" + kind], I["y_" + kind]

          with contextlib.ExitStack() as sAB:
              kvnT = sb(sAB, "kvnT", [128, NKV], BF16)
              KPE = sb(sAB, "KPE", [128, NT4 * 128], BF16)
              cqnT = sb(sAB, "cqnT", [128, 2 * NQ], BF16)
              r_kvn, r_kpe, r_cqn = Res("kvn"), Res("kpe"), Res("cqn")

              with contextlib.suppress(_Skip), contextlib.ExitStack() as sA:
                  if 'A' in SKIP:
                      raise _Skip()
                  w_a = sb(sA, "w_a", [128, 8 * 416], BF16)
                  cosk = sb(sA, "cosk", [128, NKT * 16], F32)
                  sink = sb(sA, "sink", [128, NKT * 16], F32)
                  r_wa = Res("wa")
                  wst = P.dsem()
                  for k in range(8):
                      P.dma("sp", lambda q, k=k: q.dma_start(out=w_a[:, k * 416:(k + 1) * 416], in_=S["w_in"][k * 128:(k + 1) * 128, 0:416]),
                            wst, writes=[r_wa])
                  P.dma("sp", lambda q: q.dma_start(out=cosk[:], in_=I["cosk_" + kind]), wst, writes=[r_wa])
                  P.dma("sp", lambda q: q.dma_start(out=sink[:], in_=I["sink_" + kind]), wst, writes=[r_wa])
                  fr = Front(sA, gT_pre, nb=6, ntp=3)
                  NZ = 4
                  zps = [ps(sA, f"a_z{i}", [128, 512], F32) for i in range(NZ)]
                  tp2 = ps(sA, "a_tp2", [128, 1024], BF16)
                  kvb = [sb(sA, f"a_kvb{i}", [128, 128], BF16) for i in range(NZ)]
                  krs = [sb(sA, f"a_krs{i}", [128, 32], F32) for i in range(NZ)]
                  tmp = [sb(sA, f"a_tmp{i}", [128, 64], F32) for i in range(NZ)]
                  X4 = [sb(sA, f"a_X4{i}", [128, 128], BF16) for i in range(2)]
                  cqb = [sb(sA, f"a_cqb{i}", [128, 256], BF16) for i in range(NZ)]
                  st2 = [sb(sA, f"a_st2{i}", [128, 2], F32) for i in range(NZ)]
                  junk2 = sb(sA, "a_junk2", [128, 256], F32)
                  r_z = [Res("z") for _ in range(NZ)]
                  r_kvb = [Res("kvb") for _ in range(NZ)]
                  r_krs = [Res("krs") for _ in range(NZ)]
                  r_tmp = [Res("tmp") for _ in range(NZ)]
                  r_X4 = [Res("X4") for _ in range(2)]
                  r_cqb = [Res("cqb") for _ in range(NZ)]
                  r_st2 = [Res("st2") for _ in range(NZ)]
                  r_j2 = Res("j2")
                  r_tp2a = r_tp2b = r_tp2c = Res("tp2")

                  late = []
                  items = [(t, c) for t in range(NT4) for c in range(4)]
                  slot = {}

                  def kv_s0(i):
                      t, c = items[i]
                      slot[i] = fr.run1(xkv_d, [(0, 128, (c * NT4 + t) * 128)])

                  def kv_s1(i):
                      fr.run2(slot[i])

                  def kv_s2(i):
                      t, c = items[i]
                      ti = c * NT4 + t
                      s_ = slot[i]
                      z = i % NZ
                      for k in range(8):
                          P.op("pe", lambda q, k=k: q.matmul(zps[z][:, 0:160], fr.hT[s_][:, k * 128:(k + 1) * 128], w_a[:, k * 416 + 256:k * 416 + 416], start=(k == 0), stop=(k == 7)),
                               reads=[fr.r_hT[s_], r_wa], writes=[r_z[z]], signal=(k == 7))
                      P.op("act", lambda q: q.copy(out=krs[z][:], in_=zps[z][:, 128:160]), reads=[r_z[z]], writes=[r_krs[z]])
                      rstd(zps[z][:, 0:128], 128, junk2[:, 0:128], st2[z][:, 0:1], st2[z][:, 1:2], r_z[z], r_j2, r_st2[z])
                      P.op("dve", lambda q: q.tensor_scalar(out=kvb[z][:], in0=zps[z][:, 0:128], scalar1=st2[z][:, 1:2], scalar2=None, op0=ALU.mult),
                           reads=[r_z[z], r_st2[z]], writes=[r_kvb[z]])
                      co = cosk[:, ti * 16:(ti + 1) * 16]
                      si = sink[:, ti * 16:(ti + 1) * 16]
                      x1_ = krs[z][:, 0:16]
                      x2_ = krs[z][:, 16:32]
                      tm = tmp[z]
                      xs = t % 2
                      P.op("pool", lambda q: q.tensor_tensor(out=tm[:, 0:16], in0=x1_, in1=co, op=ALU.mult), reads=[r_krs[z], r_wa], writes=[r_tmp[z]])
                      P.op("pool", lambda q: q.tensor_tensor(out=tm[:, 16:32], in0=x2_, in1=si, op=ALU.mult), reads=[r_krs[z], r_wa], writes=[r_tmp[z]])
                      P.op("pool", lambda q: q.tensor_tensor(out=tm[:, 32:48], in0=x1_, in1=si, op=ALU.mult), reads=[r_krs[z], r_wa], writes=[r_tmp[z]])
                      P.op("pool", lambda q: q.tensor_tensor(out=tm[:, 48:64], in0=x2_, in1=co, op=ALU.mult), reads=[r_krs[z], r_wa], writes=[r_tmp[z]])
                      P.op("pool", lambda q: q.tensor_tensor(out=X4[xs][:, 32 * c:32 * c + 16], in0=tm[:, 0:16], in1=tm[:, 16:32], op=ALU.subtract),
                           reads=[r_tmp[z]], writes=[r_X4[xs]])
                      P.op("pool", lambda q: q.tensor_tensor(out=X4[xs][:, 32 * c + 16:32 * c + 32], in0=tm[:, 32:48], in1=tm[:, 48:64], op=ALU.add),
                           reads=[r_tmp[z]], writes=[r_X4[xs]])

                  def kv_s3(i):
                      t, c = items[i]
                      ti = c * NT4 + t
                      z = i % NZ
                      xs = t % 2
                      P.op("pe", lambda q: q.transpose(out=tp2[:, 0:128], in_=kvb[z][:], identity=ident[:]),
                           reads=[r_kvb[z], r_const], writes=[r_tp2a])
                      P.op("dve", lambda q: q.tensor_scalar(out=kvnT[:, ti * 128:(ti + 1) * 128], in0=tp2[:, 0:128], scalar1=gcols[:, 18:19], scalar2=None, op0=ALU.mult),
                           reads=[r_tp2a, r_const], writes=[r_kvn])
                      if c == 3:
                          P.op("pe", lambda q: q.transpose(out=tp2[:, 128:256], in_=X4[xs][:], identity=ident[:]),
                               reads=[r_X4[xs], r_const], writes=[r_tp2b])
                          P.op("dve", lambda q: q.tensor_copy(out=KPE[:, t * 128:(t + 1) * 128], in_=tp2[:, 128:256]),
                               reads=[r_tp2b], writes=[r_kpe])

                  pipeline(len(items), [kv_s0, kv_s1, kv_s2, kv_s3], extra=late)

                  own = G["own_tiles"]
                  slot2 = {}

                  def ow_s0(i):
                      slot2[i] = fr.run1(xna_d, own[i])

                  def ow_s1(i):
                      fr.run2(slot2[i])

                  def ow_s2(i):
                      s_ = slot2[i]
                      z = i % NZ
                      for k in range(8):
                          P.op("pe", lambda q, k=k: q.matmul(zps[z][:, 0:256], fr.hT[s_][:, k * 128:(k + 1) * 128], w_a[:, k * 416:k * 416 + 256], start=(k == 0), stop=(k == 7)),
                               reads=[fr.r_hT[s_], r_wa], writes=[r_z[z]], signal=(k == 7))
                      rstd(zps[z][:, 0:256], 256, junk2[:, 0:256], st2[z][:, 0:1], st2[z][:, 1:2], r_z[z], r_j2, r_st2[z])
                      P.op("dve", lambda q: q.tensor_scalar(out=cqb[z][:], in0=zps[z][:, 0:256], scalar1=st2[z][:, 1:2], scalar2=None, op0=ALU.mult),
                           reads=[r_z[z], r_st2[z]], writes=[r_cqb[z]])

                  def ow_s3(i):
                      z = i % NZ
                      for kc in range(2):
                          P.op("pe", lambda q, kc=kc: q.transpose(out=tp2[:, 256 + kc * 128:256 + (kc + 1) * 128], in_=cqb[z][:, kc * 128:(kc + 1) * 128], identity=ident[:]),
                               reads=[r_cqb[z], r_const], writes=[r_tp2c], signal=(kc == 1))
                      for kc in range(2):
                          P.op("dve", lambda q, kc=kc: q.tensor_tensor(out=cqnT[:, kc * NQ + i * 128:kc * NQ + (i + 1) * 128], in0=tp2[:, 256 + kc * 128:256 + (kc + 1) * 128],
                                                                  in1=gT_q[:, kc * 128:(kc + 1) * 128], op=ALU.mult),
                               reads=[r_tp2c, r_const], writes=[r_cqn])

                  pipeline(len(own), [ow_s0, ow_s1, ow_s2, ow_s3], extra=late)
                  while late:
                      late.pop(0)()
                  P.barrier()
                  if stop == kind + "A":
                      dumpconv(sA, "d0", kvnT[:, 0:4096 if NKV >= 4096 else NKV], 4096 if NKV >= 4096 else NKV)
                      dumpconv(sA, "d1", KPE[:, 0:NT4 * 128], NT4 * 128)
                      dumpconv(sA, "d2", cqnT[:, 0:4096 if NQ >= 4096 else NQ], 4096 if NQ >= 4096 else NQ)
                      P.barrier()
                      P.flush()
                      raise _Stop()
                  P.flush()

              with contextlib.suppress(_Skip), contextlib.ExitStack() as sB:
                  if 'B' in SKIP:
                      raise _Skip()
                  KT = sb(sB, "KT", [96, NKV], BF16)
                  V = sb(sB, "V", [128, NKT * 65], BF16)
                  QT = sb(sB, "QT", [96, NQ], BF16)
                  wq = sb(sB, "wq", [128, 2 * 768], BF16)
                  wqr = sb(sB, "wqr", [128, 2 * 768], BF16)
                  wkv = sb(sB, "wkv", [128, 1024], BF16)
                  wk_ext = sb(sB, "wk_ext", [128, 8 * 96], BF16)
                  selb = sb(sB, "selb", [128, 4 * 96], BF16)
                  qc = [sb(sB, f"qc{i}", [96, 512], F32) for i in range(2)]
                  qs = [sb(sB, f"qs{i}", [96, 512], F32) for i in range(2)]
                  t1 = sb(sB, "t1", [96, 512], F32)
                  t2 = sb(sB, "t2", [96, 512], F32)
                  NPB = 3
                  pT = [sb(sB, f"pT{i}", [128, 512], BF16) for i in range(NPB)]
                  osb = [sb(sB, f"osb{i}", [65, 512], F32) for i in range(2)]
                  rrow = [sb(sB, f"rrow{i}", [65, 512], F32) for i in range(2)]
                  onb = [sb(sB, f"onb{i}", [64, 512], BF16) for i in range(2)]
                  Sps = [ps(sB, f"S{i}", [128, 512], F32) for i in range(NPB)]
                  Ops = [ps(sB, f"O{i}", [128, 512], F32) for i in range(2)]
                  Xps = [ps(sB, f"X{i}", [128, 512], F32) for i in range(3)]
                  r_w = Res("wB")
                  r_KT, r_V, r_QT = Res("KT"), Res("V"), Res("QT")
                  r_qcs = [Res("qcs") for _ in range(2)]
                  r_t = Res("t12")
                  r_pT = [Res("pT") for _ in range(NPB)]
                  r_S = [Res("S") for _ in range(NPB)]
                  r_O = [Res("O") for _ in range(2)]
                  r_X = [Res("X") for _ in range(3)]
                  r_osb = [Res("osb") for _ in range(2)]
                  r_rrow = [Res("rrow") for _ in range(2)]
                  r_onb = [Res("onb") for _ in range(2)]
                  wst = P.dsem()
                  tst = [P.dsem() for _ in range(2)]
                  ost = [P.dsem() for _ in range(2)]
                  for kc in range(2):
                      P.dma("sp", lambda q, kc=kc: q.dma_start(out=wq[:, kc * 768:(kc + 1) * 768], in_=S["w_q_up"][kc * 128:(kc + 1) * 128, :]), wst, writes=[r_w])
                      P.dma("sp", lambda q, kc=kc: q.dma_start(out=wqr[:, kc * 768:(kc + 1) * 768], in_=S["w_q_up"][kc * 128:(kc + 1) * 128, :]), wst, writes=[r_w])
                  P.dma("sp", lambda q: q.dma_start(out=wkv[:], in_=S["w_kv_up"]), wst, writes=[r_w])
                  for c in range(4):
                      P.dma("sp", lambda q, c=c: q.dma_start(out=selb[:, c * 96:(c + 1) * 96], in_=S["sel"][c * 128:(c + 1) * 128, :]), wst, writes=[r_w])
                  for kc in range(2):
                      for h in range(8):
                          b = kc * 768 + h * 96
                          P.op("pool", lambda q, b=b: q.tensor_scalar(out=wqr[:, b + 64:b + 80], in0=wq[:, b + 80:b + 96], scalar1=-1.0, scalar2=None, op0=ALU.mult),
                               reads=[r_w], writes=[r_w])
                          P.op("pool", lambda q, b=b: q.tensor_copy(out=wqr[:, b + 80:b + 96], in_=wq[:, b + 64:b + 80]), reads=[r_w], writes=[r_w])
                  P.op("pool", lambda q: q.memset(wk_ext[:], 0.0), reads=[r_w], writes=[r_w])
                  for h in range(8):
                      P.op("pool", lambda q, h=h: q.tensor_copy(out=wk_ext[:, h * 96:h * 96 + 64], in_=wkv[:, h * 128:h * 128 + 64]), reads=[r_w], writes=[r_w])
                  P.op("pool", lambda q: q.memset(V[:], 1.0), writes=[r_V])

                  nX = 0
                  npair = 0
                  nO = 0
                  nqc = 0
                  pendB = []
                  lateB = []
                  if kind == "p":
                      lateB = late_conv_tasks(Conv(sB, 1408, engs=("dve", "pool"), store_q="sp"))
                  BB = [Xps[0], Xps[1], Sps[0], Sps[1], Sps[2], Ops[0], Ops[1]]
                  r_BB = [r_X[0], r_X[1], r_S[0], r_S[1], r_S[2], r_O[0], r_O[1]]
                  for h in range(8):
                      while pendB:
                          pendB.pop()[1]()
                      for cn in range(NKV // 512):
                          x = nX % 7
                          nX += 1
                          c = (cn * 512) // (NT4 * 128)
                          loc = cn * 512 - c * NT4 * 128
                          P.op("pe", lambda q, x=x, h=h, cn=cn: q.matmul(BB[x][0:96, :], wk_ext[:, h * 96:(h + 1) * 96], kvnT[:, cn * 512:(cn + 1) * 512], start=True, stop=False),
                               reads=[r_w, r_kvn], writes=[r_BB[x]], signal=False)
                          P.op("pe", lambda q, x=x, c=c, loc=loc: q.matmul(BB[x][0:96, :], selb[:, c * 96:(c + 1) * 96], KPE[:, loc:loc + 512], start=False, stop=True),
                               reads=[r_w, r_kpe], writes=[r_BB[x]])
                          eng = "dve" if cn % 2 == 0 else "act"
                          if eng == "dve":
                              P.op("dve", lambda q, x=x, cn=cn: q.tensor_copy(out=KT[:, cn * 512:(cn + 1) * 512], in_=BB[x][0:96, :]), reads=[r_BB[x]], writes=[r_KT])
                          else:
                              P.op("act", lambda q, x=x, cn=cn: q.copy(out=KT[:, cn * 512:(cn + 1) * 512], in_=BB[x][0:96, :]), reads=[r_BB[x]], writes=[r_KT])
                      for g8 in range(NKT // 8):
                          x = nX % 7
                          nX += 1
                          for j in range(8):
                              kt = g8 * 8 + j
                              P.op("pe", lambda q, x=x, j=j, kt=kt, h=h: q.matmul(BB[x][:, j * 64:(j + 1) * 64], kvnT[:, kt * 128:(kt + 1) * 128], wkv[:, h * 128 + 64:h * 128 + 128], start=True, stop=True),
                                   reads=[r_w, r_kvn], writes=[r_BB[x]], signal=(j == 7))
                          vdst = V[:, g8 * 8 * 65:(g8 + 1) * 8 * 65].rearrange("p (j d) -> p j d", d=65)[:, :, 0:64]
                          vsrc = BB[x][:, :].rearrange("p (j d) -> p j d", d=64)
                          if g8 % 2 == 0:
                              P.op("dve", lambda q, vdst=vdst, vsrc=vsrc: q.tensor_copy(out=vdst, in_=vsrc), reads=[r_BB[x]], writes=[r_V])
                          else:
                              P.op("act", lambda q, vdst=vdst, vsrc=vsrc: q.copy(out=vdst, in_=vsrc), reads=[r_BB[x]], writes=[r_V])
                      for (q0, N) in G["qtiles"]:
                          b = nqc % 2
                          nqc += 1
                          P.dma("sp", lambda q, b=b, q0=q0, N=N: q.dma_start(out=qc[b][:, 0:N], in_=I["qc_" + kind][:, q0:q0 + N]), tst[b], writes=[r_qcs[b]])
                          P.dma("sp", lambda q, b=b, q0=q0, N=N: q.dma_start(out=qs[b][:, 0:N], in_=I["qs_" + kind][:, q0:q0 + N]), tst[b], writes=[r_qcs[b]])
                          xa = nX % 7
                          xb_ = (nX + 1) % 7
                          nX += 2
                          for (xx, wsrc) in ((xa, wq), (xb_, wqr)):
                              for kc in range(2):
                                  P.op("pe", lambda q, xx=xx, wsrc=wsrc, kc=kc, h=h, q0=q0, N=N: q.matmul(BB[xx][0:96, 0:N], wsrc[:, kc * 768 + h * 96:kc * 768 + (h + 1) * 96],
                                                                                                    cqnT[:, kc * NQ + q0:kc * NQ + q0 + N], start=(kc == 0), stop=(kc == 1)),
                                       reads=[r_w, r_cqn], writes=[r_BB[xx]], signal=(kc == 1))
                          P.op("dve", lambda q, xa=xa, b=b, N=N: q.tensor_tensor(out=t1[:, 0:N], in0=BB[xa][0:96, 0:N], in1=qc[b][:, 0:N], op=ALU.mult),
                               reads=[r_BB[xa], r_qcs[b]], writes=[r_t])
                          P.op("dve", lambda q, xb_=xb_, b=b, N=N: q.tensor_tensor(out=t2[:, 0:N], in0=BB[xb_][0:96, 0:N], in1=qs[b][:, 0:N], op=ALU.mult),
                               reads=[r_BB[xb_], r_qcs[b]], writes=[r_t])
                          P.op("dve", lambda q, q0=q0, N=N: q.tensor_tensor(out=QT[:, q0:q0 + N], in0=t1[:, 0:N], in1=t2[:, 0:N], op=ALU.add),
                               reads=[r_t], writes=[r_QT])
                      units = []
                      for (q0, N) in G["qtiles"]:
                          units.append((q0, N, nO % 2))
                          nO += 1
                      pairs = [(u, kt) for u in range(len(units)) for kt in range(NKT)]
                      DL = NPB - 1

                      def qk(n):
                          u, kt = pairs[n]
                          q0, N, o = units[u]
                          i = (npair + n) % NPB
                          P.op("pe", lambda q: q.matmul(Sps[i][:, 0:N], KT[:, kt * 128:(kt + 1) * 128], QT[:, q0:q0 + N], start=True, stop=True),
                               reads=[r_KT, r_QT], writes=[r_S[i]])
                          P.op("act", lambda q: q.activation(out=pT[i][:, 0:N], in_=Sps[i][:, 0:N], func=AF.Exp, scale=96.0 ** -0.5),
                               reads=[r_S[i]], writes=[r_pT[i]])

                      def pv(n):
                          u, kt = pairs[n]
                          q0, N, o = units[u]
                          i = (npair + n) % NPB
                          P.op("pe", lambda q: q.matmul(Ops[o][0:65, 0:N], V[:, kt * 65:(kt + 1) * 65], pT[i][:, 0:N], start=(kt == 0), stop=(kt == NKT - 1)),
                               reads=[r_V, r_pT[i]], writes=[r_O[o]], signal=(kt == NKT - 1))
                          if kt == NKT - 1:
                              P.op("dve", lambda q: q.tensor_copy(out=osb[o][:, 0:N], in_=Ops[o][0:65, 0:N]), reads=[r_O[o]], writes=[r_osb[o]])

                              def fin2(o=o, N=N, h=h, q0=q0):
                                  x = 2
                                  P.op("pe", lambda q: q.matmul(Xps[x][0:64, 0:N], ones_f[64:65, 0:64], osb[o][64:65, 0:N], start=True, stop=True),
                                       reads=[r_osb[o], r_const], writes=[r_X[x]])
                                  P.op("dve", lambda q: q.reciprocal(out=rrow[o][0:64, 0:N], in_=Xps[x][0:64, 0:N]), reads=[r_X[x]], writes=[r_rrow[o]])
                                  P.op("dve", lambda q: q.tensor_tensor(out=onb[o][:, 0:N], in0=osb[o][0:64, 0:N], in1=rrow[o][0:64, 0:N], op=ALU.mult),
                                       reads=[r_osb[o], r_rrow[o]], writes=[r_onb[o]])
                                  P.dma("sp", lambda q: q.dma_start(out=at_d[h * 64:(h + 1) * 64, q0:q0 + N], in_=onb[o][:, 0:N]), ost[o], reads=[r_onb[o]])
                              while pendB:
                                  pendB.pop()[1]()
                              pendB.append([6, fin2])
                              if lateB:
                                  lateB.pop(0)()

                      tot = len(pairs)
                      for n in range(tot + DL):
                          if n < tot:
                              qk(n)
                          if n - DL >= 0:
                              pv(n - DL)
                          if pendB:
                              pendB[0][0] -= 1
                              if pendB[0][0] <= 0:
                                  pendB.pop()[1]()
                      npair += tot
                  while pendB:
                      pendB.pop()[1]()
                  while lateB:
                      lateB.pop(0)()
                  P.barrier()
                  if stop == kind + "B":
                      tt = sb(sB, "dbgl", [128, 4096], BF16)
                      rr = Res("dbgl")
                      nn = 4096 if NQ >= 4096 else NQ
                      P.dma("sp", lambda q: q.dma_start(out=tt[:, 0:nn], in_=at_d[0:0 + 128, 0:nn]), dst_, writes=[rr])
                      t32 = sb(sB, "dbgt2", [128, 4096], F32)
                      r32 = Res("dbgt2")
                      P.op("dve", lambda q: q.tensor_copy(out=t32[:, 0:nn], in_=tt[:, 0:nn]), reads=[rr], writes=[r32])
                      P.dma("sp", lambda q: q.dma_start(out=dbg["d0"][:, 0:nn], in_=t32[:, 0:nn]), dst_, reads=[r32])
                      P.barrier()
                      P.flush()
                      raise _Stop()
                  P.flush()

          for hh in range(2):
              with contextlib.suppress(_Skip), contextlib.ExitStack() as sC:
                  if 'C' in SKIP:
                      raise _Skip()
                  NKTN = NNA // 128
                  w_n = sb(sC, "w_n", [128, 8 * 768], BF16)
                  nqT = sb(sC, "nqT", [128, 2 * NQ], BF16)
                  nkT = sb(sC, "nkT", [128, 2 * NNA], BF16)
                  nv = sb(sC, "nv", [128, NKTN * 4 * 65], BF16)
                  tr2 = sb(sC, "tr2", [128, 4 * 1536], BF16)
                  tr2i = sb(sC, "tr2i", [128, 4 * 1536], BF16)
                  rmt = sb(sC, "rmt", [8, G["nslots"] * 128], BF16)
                  qrs = sb(sC, "qrs", [8, 512], BF16)
                  r_w = Res("wC")
                  r_nq, r_nk, r_nv = Res("nq"), Res("nk"), Res("nv")
                  wst = P.dsem()
                  for k in range(8):
                      for j in range(3):
                          c0 = 416 + j * 512 + hh * 256
                          P.dma("sp", lambda q, k=k, j=j, c0=c0: q.dma_start(out=w_n[:, k * 768 + j * 256:k * 768 + (j + 1) * 256], in_=S["w_in"][k * 128:(k + 1) * 128, c0:c0 + 256]),
                                wst, writes=[r_w])
                  for j in range(4 if '1' not in SKIP else 0):
                      for c3 in range(3):
                          P.dma("sp", lambda q, j=j, c3=c3: q.dma_start(out=tr2[:, j * 1536 + c3 * 512:j * 1536 + (c3 + 1) * 512], in_=S["tr2"][(4 * hh + j) * 128:(4 * hh + j + 1) * 128, c3 * 512:(c3 + 1) * 512]), wst, writes=[r_w])
                          P.dma("sp", lambda q, j=j, c3=c3: q.dma_start(out=tr2i[:, j * 1536 + c3 * 512:j * 1536 + (c3 + 1) * 512], in_=S["tr2i"][(4 * hh + j) * 128:(4 * hh + j + 1) * 128, c3 * 512:(c3 + 1) * 512]), wst, writes=[r_w])
                  for c0 in range(0, G["nslots"] if '1' not in SKIP else 0, 8):
                      c1 = min(c0 + 8, G["nslots"])
                      P.dma("sp", lambda q, c0=c0, c1=c1: q.dma_start(out=rmt[:, c0 * 128:c1 * 128], in_=S["rm_" + kind][:, c0 * 128:c1 * 128]), wst, writes=[r_w])
                  P.dma("sp", lambda q: q.dma_start(out=qrs[:], in_=S["qrsel"]), wst, writes=[r_w])
                  if '2' not in SKIP:
                      P.op("pool", lambda q: q.memset(nv[:], 1.0), writes=[r_nv])
                  fr = Front(sC, gT_pre, nb=6, ntp=2)
                  NZ = 3
                  zk = [ps(sC, f"c_zk{i}", [128, 512], F32) for i in range(NZ)]
                  zv = [ps(sC, f"c_zv{i}", [128, 512], F32) for i in range(NZ)]
                  r_zk = [Res("zk") for _ in range(NZ)]
                  r_zv = [Res("zv") for _ in range(NZ)]
                  q_of_na = {}
                  for oi, pieces in enumerate(G["own_tiles"]):
                      if len(pieces) == 1:
                          q_of_na[pieces[0][2] // 128] = oi * 128
                  slotc = {}

                  def c_s0(ti):
                      slotc[ti] = fr.run1(xna_d, [(0, 128, ti * 128)])

                  def c_s1(ti):
                      fr.run2(slotc[ti])

                  def c_s2(ti):
                      s_ = slotc[ti]
                      z = ti % NZ
                      for pr in range(2):
                          for k in range(8):
                              P.op("pe", lambda q, pr=pr, k=k: q.matmul(zk[z][:, pr * 128:(pr + 1) * 128], w_n[:, k * 768 + 256 + pr * 128:k * 768 + 256 + (pr + 1) * 128],
                                                                    fr.hT[s_][:, k * 128:(k + 1) * 128], start=(k == 0), stop=(k == 7)),
                                   reads=[fr.r_hT[s_], r_w], writes=[r_zk[z]], signal=False)
                      for pr in range(2):
                          for k in range(8):
                              P.op("pe", lambda q, pr=pr, k=k: q.matmul(zk[z][:, 256 + pr * 128:256 + (pr + 1) * 128], w_n[:, k * 768 + pr * 128:k * 768 + (pr + 1) * 128],
                                                                    fr.hT[s_][:, k * 128:(k + 1) * 128], start=(k == 0), stop=(k == 7)),
                                   reads=[fr.r_hT[s_], r_w], writes=[r_zk[z]], signal=(pr == 1 and k == 7))
                      for k in range(8):
                          P.op("pe", lambda q, k=k: q.matmul(zv[z][:, 0:256], fr.hT[s_][:, k * 128:(k + 1) * 128], w_n[:, k * 768 + 512:k * 768 + 768], start=(k == 0), stop=(k == 7)),
                               reads=[fr.r_hT[s_], r_w], writes=[r_zv[z]], signal=(k == 7))

                  def c_s3(ti):
                      z = ti % NZ
                      qcols = []
                      if kind == "s":
                          qcols = [(0, 128, ti * 128)]
                      else:
                          if ti in q_of_na:
                              qcols = [(0, 128, q_of_na[ti])]
                          elif ti == 2:
                              qcols = [(64, 64, 4096)]
                          elif ti == 35:
                              qcols = [(0, 64, 4160)]
                      for pr in range(2):
                          P.op("dve", lambda q, pr=pr: q.tensor_copy(out=nkT[:, pr * NNA + ti * 128:pr * NNA + (ti + 1) * 128], in_=zk[z][:, pr * 128:(pr + 1) * 128]),
                               reads=[r_zk[z]], writes=[r_nk])
                          for (c0, cn_, qd) in qcols:
                              P.op("dve", lambda q, pr=pr, c0=c0, cn_=cn_, qd=qd: q.tensor_scalar(out=nqT[:, pr * NQ + qd:pr * NQ + qd + cn_], in0=zk[z][:, 256 + pr * 128 + c0:256 + pr * 128 + c0 + cn_],
                                                                                            scalar1=0.125, scalar2=None, op0=ALU.mult),
                                   reads=[r_zk[z]], writes=[r_nq])
                      vdst = nv[:, ti * 260:(ti + 1) * 260].rearrange("p (j d) -> p j d", d=65)[:, :, 0:64]
                      vsrc = zv[z][:, 0:256].rearrange("p (j d) -> p j d", d=64)
                      P.op("act", lambda q: q.copy(out=vdst, in_=vsrc), reads=[r_zv[z]], writes=[r_nv])

                  pipeline(NKTN, [c_s0, c_s1, c_s2, c_s3])
                  P.barrier()
                  if stop == kind + "Cp":
                      dumpconv(sC, "d0", nkT[:, 0:4096], 4096)
                      dumpconv(sC, "d1", nqT[:, 0:4096], 4096)
                      dumpconv(sC, "d2", nv[:, 0:4096], 4096)
                      P.barrier()
                      P.flush()
                      raise _Stop()
                  P.flush()

                  NPB = 3
                  pT = [sb(sC, f"c_pT{i}", [128, 512], BF16) for i in range(NPB)]
                  osb = [sb(sC, f"c_osb{i}", [65, 512], F32) for i in range(2)]
                  rrow = [sb(sC, f"c_rrow{i}", [65, 512], F32) for i in range(2)]
                  onb = [sb(sC, f"c_onb{i}", [64, 512], BF16) for i in range(2)]
                  Sps = [zk[1], zk[2], zv[0]]
                  OpsL = [zv[1], zv[2]]
                  Bps = zk[0]
                  r_Bps = r_zk[0]
                  r_pT = [Res("pT") for _ in range(NPB)]
                  r_S = [r_zk[1], r_zk[2], r_zv[0]]
                  r_OL = [r_zv[1], r_zv[2]]
                  r_osb = [Res("osb") for _ in range(2)]
                  r_rrow = [Res("rrow") for _ in range(2)]
                  r_onb = [Res("onb") for _ in range(2)]
                  ost = [P.dsem() for _ in range(2)]
                  npair = 0
                  nO = 0
                  pendC = []
                  for gr in G["groups"]:
                      q0, N = gr["q0"], gr["N"]
                      for hl in range(4):
                          pr, hp = hl // 2, (hl % 2) * 64
                          h = 4 * hh + hl
                          tiles = gr["tiles"]
                          nt = len(tiles)
                          DL = NPB - 1

                          def crange(off, N=N, gr=gr):
                              if not gr["interior"]:
                                  return 0, N
                              lo, hi = max(0, off - 3), min(7, off + 5)
                              return 64 * lo, 64 * (hi + 1)

                          def qk(m, i, q0=q0, N=N, pr=pr, hp=hp, hl=hl, gr=gr):
                              ti, off = gr["tiles"][m]
                              slot = gr["slots"][m]
                              w0 = (10 - off) * 64
                              c0, c1 = crange(off)
                              n_ = c1 - c0
                              inter = gr["interior"]
                              tab = tr2i if inter else tr2
                              P.op("pe", lambda q: q.matmul(Sps[i][:, 0:n_], nkT[hp:hp + 64, pr * NNA + ti * 128:pr * NNA + (ti + 1) * 128], nqT[hp:hp + 64, pr * NQ + q0 + c0:pr * NQ + q0 + c1], start=True, stop=False),
                                   reads=[r_nk, r_nq], writes=[r_S[i]], signal=False)
                              P.op("pe", lambda q: q.matmul(Sps[i][:, 0:n_], ident[:], tab[:, hl * 1536 + w0 + c0:hl * 1536 + w0 + c1], start=False, stop=inter),
                                   reads=[r_w, r_const], writes=[r_S[i]], signal=inter)
                              if not inter:
                                  P.op("pe", lambda q: q.matmul(Sps[i][:, 0:n_], rmt[:, slot * 128:(slot + 1) * 128], qrs[:, 0:n_], start=False, stop=True),
                                       reads=[r_w], writes=[r_S[i]])
                              P.op("act", lambda q: q.activation(out=pT[i][:, 0:n_], in_=Sps[i][:, 0:n_], func=AF.Exp, scale=1.0),
                                   reads=[r_S[i]], writes=[r_pT[i]])

                          o = nO % 2
                          nO += 1
                          Ops = OpsL[o]
                          r_O = r_OL[o]

                          def pv(m, i, N=N, hl=hl, gr=gr, nt=nt, Ops=Ops, r_O=r_O):
                              ti, off = gr["tiles"][m]
                              c0, c1 = crange(off)
                              n_ = c1 - c0
                              P.op("pe", lambda q: q.matmul(Ops[0:65, c0:c1], nv[:, ti * 260 + hl * 65:ti * 260 + (hl + 1) * 65], pT[i][:, 0:n_], start=(m == 0), stop=(m == nt - 1)),
                                   reads=[r_nv, r_pT[i]], writes=[r_O], signal=(m == nt - 1))

                          base = npair
                          for m in range(nt + DL):
                              if m < nt:
                                  qk(m, (base + m) % NPB)
                              if m - DL >= 0:
                                  pv(m - DL, (base + m - DL) % NPB)
                              if m == 2 and pendC:
                                  pendC.pop()()
                          npair += nt
                          P.op("dve", lambda q, o=o, N=N, Ops=Ops: q.tensor_copy(out=osb[o][:, 0:N], in_=Ops[0:65, 0:N]), reads=[r_O], writes=[r_osb[o]])

                          def fin2(o=o, N=N, h=h, q0=q0):
                              P.op("pe", lambda q: q.matmul(Bps[0:64, 0:N], ones_f[64:65, 0:64], osb[o][64:65, 0:N], start=True, stop=True),
                                   reads=[r_osb[o], r_const], writes=[r_Bps])
                              P.op("dve", lambda q: q.reciprocal(out=rrow[o][0:64, 0:N], in_=Bps[0:64, 0:N]), reads=[r_Bps], writes=[r_rrow[o]])
                              P.op("dve", lambda q: q.tensor_tensor(out=onb[o][:, 0:N], in0=osb[o][0:64, 0:N], in1=rrow[o][0:64, 0:N], op=ALU.mult),
                                   reads=[r_osb[o], r_rrow[o]], writes=[r_onb[o]])
                              P.dma("sp", lambda q: q.dma_start(out=at_d[512 + h * 64:512 + (h + 1) * 64, q0:q0 + N], in_=onb[o][:, 0:N]), ost[o], reads=[r_onb[o]])
                          if pendC:
                              pendC.pop()()
                          pendC.append(fin2)
                  while pendC:
                      pendC.pop()()
                  P.barrier()
                  if hh == 1:
                      if stop == kind + "C":
                          tt = sb(sC, "dbgl", [128, 1024], BF16)
                          rr = Res("dbgl")
                          nn = 1024
                          P.dma("sp", lambda q: q.dma_start(out=tt[:, 0:nn], in_=at_d[512:512 + 128, 0:nn]), dst_, writes=[rr])
                          t32 = sb(sC, "dbgt2", [128, 1024], F32)
                          r32 = Res("dbgt2")
                          P.op("dve", lambda q: q.tensor_copy(out=t32[:, 0:nn], in_=tt[:, 0:nn]), reads=[rr], writes=[r32])
                          P.dma("sp", lambda q: q.dma_start(out=dbg["d0"][:, 0:nn], in_=t32[:, 0:nn]), dst_, reads=[r32])
                          P.barrier()
                          P.flush()
                          raise _Stop()
                  P.flush()

          with contextlib.ExitStack() as sD:
              NSUB = NQ // 128
              NOT_ = NOWN // 512
              w_o = sb(sD, "w_o", [128, 8 * D], BF16)
              w_dn = sb(sD, "w_dn", [128, NJ * D], BF16)
              gpost = sb(sD, "gpost", [128, D], F32)
              gfpost = sb(sD, "gfpost", [128, D], F32)
              cw = sb(sD, "cw", [128, 4 * 44], F32)
              HT = [sb(sD, f"HT{i}", [128, 8 * 514], BF16) for i in range(2)]
              NX1 = 9
              x1 = [sb(sD, f"x1_{i}", [128, D], F32) for i in range(NX1)]
              aT = [sb(sD, f"aT{i}", [128, 8 * 128], BF16) for i in range(2)]
              junk = sb(sD, "d_junk", [128, D], F32)
              tmpm = sb(sD, "d_tmpm", [128, D], F32)
              h2 = [sb(sD, f"h2_{i}", [128, D], BF16) for i in range(2)]
              stt = [sb(sD, f"d_st{i}", [128, 8], F32) for i in range(2)]
              actT = sb(sD, "actT", [128, NJ * 512], BF16)
              wu = [sb(sD, f"wu{i}", [128, 8 * 256], BF16) for i in range(2)]
              hs = [sb(sD, f"hs{i}", [128, 2 * 514], F32) for i in range(2)]
              hc = [sb(sD, f"hc{i}", [128, 2 * 512], F32) for i in range(2)]
              gl = [sb(sD, "gl0", [128, 512], F32)] * 2
              mix_ps = [ps(sD, f"mix{i}", [128, 512], F32) for i in range(2)]
              tp_ps = ps(sD, "d_tp", [128, D], BF16)
              hg_ps = [[ps(sD, f"hg{p_}{i}", [128, 512], F32) for i in range(2)] for p_ in range(2)]
              he_ps = ps(sD, "he", [128, 512], F32)
              dn_ps = mix_ps
              r_w = Res("wD")
              r_HT = [Res("HT") for _ in range(2)]
              r_x1 = [Res("x1") for _ in range(NX1)]
              r_aT = [Res("aT") for _ in range(2)]
              r_junk, r_tmpm = Res("junk"), Res("tmpm")
              r_h2 = [Res("h2") for _ in range(2)]
              r_stt = [Res("stt") for _ in range(2)]
              r_act = [Res("actT") for _ in range(NJ)]
              r_wu = [Res("wu") for _ in range(2)]
              r_hs = [Res("hs") for _ in range(2)]
              r_hc = [Res("hc") for _ in range(2)]
              r_gl = [Res("gl")] * 2
              r_mix = [Res("mix") for _ in range(2)]
              r_tp = Res("tp")
              r_hg = [[Res("hg") for _ in range(2)] for _ in range(2)]
              r_he = Res("he")
              r_dn = r_mix
              dnb = [mix_ps, hg_ps[0]]
              r_dnb = [r_mix, r_hg[0]]
              wst = P.dsem()
              xld = [P.dsem() for _ in range(2)]
              ald = [P.dsem() for _ in range(2)]
              wuld = [P.dsem() for _ in range(2)]
              yst = [P.dsem() for _ in range(2)]
              for k in range(8):
                  P.dma("sp", lambda q, k=k: q.dma_start(out=w_o[:, k * D:(k + 1) * D], in_=S["w_o"][k * 128:(k + 1) * 128, :]), wst, writes=[r_w])
              r_wdn = Res("wdn")
              wdst = P.dsem()
              for j in range(NJ):
                  P.dma("act", lambda q, j=j: q.dma_start(out=w_dn[:, j * D:(j + 1) * D], in_=S["wd"][j * 128:(j + 1) * 128, :]), wdst, writes=[r_wdn])
              P.dma("sp", lambda q: q.dma_start(out=gpost[:], in_=I["g_mix_post"].partition_broadcast(128)), wst, writes=[r_w])
              P.dma("sp", lambda q: q.dma_start(out=gfpost[:], in_=I["g_ffn_post"].partition_broadcast(128)), wst, writes=[r_w])
              for t3 in range(3):
                  P.dma("sp", lambda q, t3=t3: q.dma_start(out=cw[:, t3 * 44:(t3 + 1) * 44], in_=I["ffn_conv_w"][t3].rearrange("(c p) -> p c", p=128), allow_slow_non_contiguous=True), wst, writes=[r_w])
              P.dma("sp", lambda q: q.dma_start(out=cw[:, 132:176], in_=I["ffn_conv_b"].rearrange("(c p) -> p c", p=128), allow_slow_non_contiguous=True), wst, writes=[r_w])
              for i in range(2):
                  P.op("pool", lambda q, i=i: q.memset(HT[i][:], 0.0), writes=[r_HT[i]])

              cnt = {"d1": 0}
              x1slot = {}

              def D1_stages(sub, post):
                  n = cnt["d1"]
                  cnt["d1"] += 1
                  b = n % 2
                  xs = n % NX1
                  x1slot[sub] = xs
                  q0 = sub * 128
                  st = stt[b]

                  def dl():
                      P.dma("sp", lambda q: q.dma_start(out=aT[b][:].rearrange("p (k t) -> p k t", k=8), in_=at_d[:, q0:q0 + 128].rearrange("(k p) t -> p k t", p=128)),
                            ald[b], writes=[r_aT[b]])
                      for (p0, nr, r0) in G["own_tiles"][sub]:
                          P.dma("sp", lambda q, p0=p0, nr=nr, r0=r0: q.dma_start(out=x1[xs][p0:p0 + nr, :], in_=xna_d[r0:r0 + nr, :]), xld[b], writes=[r_x1[xs]])

                  def d0():
                      for half in range(2):
                          for k in range(8):
                              P.op("pe", lambda q, half=half, k=k: q.matmul(mix_ps[half][:, :], aT[b][:, k * 128:(k + 1) * 128], w_o[:, k * D + half * 512:k * D + (half + 1) * 512], start=(k == 0), stop=(k == 7)),
                                   reads=[r_aT[b], r_w], writes=[r_mix[half]], signal=(k == 7))

                  def d1():
                      for half in range(2):
                          P.op("act", lambda q, half=half: q.activation(out=junk[:, half * 512:(half + 1) * 512], in_=mix_ps[half][:, :], func=AF.Square, accum_out=st[:, half:half + 1]),
                               reads=[r_mix[half]], writes=[r_junk, r_stt[b]])
                      P.op("dve", lambda q: q.tensor_tensor(out=st[:, 2:3], in0=st[:, 0:1], in1=st[:, 1:2], op=ALU.add), reads=[r_stt[b]], writes=[r_stt[b]])
                      P.op("act", lambda q: q.activation(out=st[:, 3:4], in_=st[:, 2:3], func=AF.Sqrt, bias=epsb[:, 0:1], scale=1.0 / D), reads=[r_stt[b], r_const], writes=[r_stt[b]])
                      P.op("dve", lambda q: q.reciprocal(out=st[:, 3:4], in_=st[:, 3:4]), reads=[r_stt[b]], writes=[r_stt[b]])
                      for half in range(2):
                          P.op("dve", lambda q, half=half: q.scalar_tensor_tensor(out=tmpm[:, half * 512:(half + 1) * 512], in0=mix_ps[half][:, :], scalar=st[:, 3:4], in1=gpost[:, half * 512:(half + 1) * 512],
                                                                                  op0=ALU.mult, op1=ALU.mult),
                               reads=[r_mix[half], r_stt[b], r_w], writes=[r_tmpm])
                      P.op("pool", lambda q: q.tensor_tensor(out=x1[xs][:], in0=tmpm[:], in1=x1[xs][:], op=ALU.add), reads=[r_tmpm, r_x1[xs]], writes=[r_x1[xs]])

                  def d2():
                      P.op("act", lambda q: q.activation(out=junk[:], in_=x1[xs][:], func=AF.Square, accum_out=st[:, 4:5]), reads=[r_x1[xs]], writes=[r_junk, r_stt[b]])
                      P.op("act", lambda q: q.activation(out=st[:, 5:6], in_=st[:, 4:5], func=AF.Sqrt, bias=epsb[:, 0:1], scale=1.0 / D), reads=[r_stt[b], r_const], writes=[r_stt[b]])
                      P.op("dve", lambda q: q.reciprocal(out=st[:, 5:6], in_=st[:, 5:6]), reads=[r_stt[b]], writes=[r_stt[b]])
                      P.op("dve", lambda q: q.tensor_scalar(out=h2[b][:], in0=x1[xs][:], scalar1=st[:, 5:6], scalar2=None, op0=ALU.mult), reads=[r_x1[xs], r_stt[b]], writes=[r_h2[b]])

                  def d3():
                      for k in range(8):
                          P.op("pe", lambda q, k=k: q.transpose(out=tp_ps[:, k * 128:(k + 1) * 128], in_=h2[b][:, k * 128:(k + 1) * 128], identity=ident[:]),
                               reads=[r_h2[b], r_const], writes=[r_tp], signal=(k == 7))
                      post()
                  return [dl, d0, d1, d2, d3]

              def D1(sub, post):
                  for f in D1_stages(sub, post):
                      f()

              def put_ht(htb, col0, src0, ncol, flagcol=None):
                  dst = HT[htb][:].rearrange("p (k t) -> p k t", k=8)[:, :, col0:col0 + ncol]
                  src = tp_ps[:].rearrange("p (k t) -> p k t", k=8)[:, :, src0:src0 + ncol]
                  gsrc = gT_ffn[:].rearrange("p (k t) -> p k t", k=8)[:, :, 0:ncol]
                  P.op("dve", lambda q: q.tensor_tensor(out=dst, in0=src, in1=gsrc, op=ALU.mult), reads=[r_tp, r_const], writes=[r_HT[htb]])
                  if flagcol is not None:
                      P.op("dve", lambda q: q.tensor_scalar(out=dst, in0=dst, scalar1=flags[:, flagcol:flagcol + 1], scalar2=None, op0=ALU.mult),
                           reads=[r_HT[htb], r_const], writes=[r_HT[htb]])

              edge = sb(sD, "edge", [128, 8], BF16)
              r_edge = Res("edge")

              nwu = {"n": 0}

              def FFN(t, sched=None):
                  hb = t % 2
                  HTv = HT[hb]
                  pendG = []
                  sched = sched or {}
                  for j in range(NJ):
                      for f in sched.pop(j, []):
                          f()
                      wb = nwu["n"] % 2
                      nwu["n"] += 1
                      P.dma("sp", lambda q, j=j, wb=wb: q.dma_start(out=wu[wb][:], in_=S["wu"][j].rearrange("p k c -> p (k c)")), wuld[wb], writes=[r_wu[wb]])
                      pb = j % 2
                      for gu in range(2):
                          for k in range(8):
                              P.op("pe", lambda q, gu=gu, k=k, wb=wb, pb=pb: q.matmul(hg_ps[pb][gu][:, :], wu[wb][:, k * 256 + gu * 128:k * 256 + (gu + 1) * 128], HTv[:, k * 514:k * 514 + 512], start=(k == 0), stop=(k == 7)),
                                   reads=[r_wu[wb], r_HT[hb]], writes=[r_hg[pb][gu]], signal=(k == 7))
                      for gu in range(2):
                          for k in range(8):
                              P.op("pe", lambda q, gu=gu, k=k, wb=wb, pb=pb: q.matmul(he_ps[:, pb * 4 + gu * 2:pb * 4 + gu * 2 + 2], wu[wb][:, k * 256 + gu * 128:k * 256 + (gu + 1) * 128], HTv[:, k * 514 + 512:k * 514 + 514],
                                                                                    start=(k == 0), stop=(k == 7)),
                                   reads=[r_wu[wb], r_HT[hb]], writes=[r_he], signal=(gu == 1 and k == 7))
                      for gu in range(2):
                          P.op("act", lambda q, gu=gu, pb=pb: q.copy(out=hs[pb][:, gu * 514:gu * 514 + 512], in_=hg_ps[pb][gu][:, :]), reads=[r_hg[pb][gu]], writes=[r_hs[pb]])
                          P.op("act", lambda q, gu=gu, pb=pb: q.copy(out=hs[pb][:, gu * 514 + 512:gu * 514 + 514], in_=he_ps[:, pb * 4 + gu * 2:pb * 4 + gu * 2 + 2]), reads=[r_he], writes=[r_hs[pb]])
                      for gu in range(2 if 'G' not in SKIP else 0):
                          ch = gu * NJ + j
                          eng = "dve" if (gu == 0 or 'P' in SKIP) else "pool"
                          hv = hs[pb]
                          o0 = gu * 514
                          hcv = hc[pb][:, gu * 512:(gu + 1) * 512]
                          P.op(eng, lambda q, hv=hv, o0=o0, hcv=hcv, ch=ch: q.tensor_scalar(out=hcv, in0=hv[:, o0:o0 + 512], scalar1=cw[:, ch:ch + 1], scalar2=cw[:, 132 + ch:133 + ch], op0=ALU.mult, op1=ALU.add),
                               reads=[r_hs[pb], r_w], writes=[r_hc[pb]])
                          P.op("dve", lambda q, hv=hv, o0=o0, hcv=hcv, ch=ch: q.scalar_tensor_tensor(out=hcv, in0=hv[:, o0 + 1:o0 + 513], scalar=cw[:, 44 + ch:45 + ch], in1=hcv, op0=ALU.mult, op1=ALU.add),
                               reads=[r_hs[pb], r_hc[pb], r_w], writes=[r_hc[pb]])
                          P.op("dve", lambda q, hv=hv, o0=o0, hcv=hcv, ch=ch: q.scalar_tensor_tensor(out=hcv, in0=hv[:, o0 + 2:o0 + 514], scalar=cw[:, 88 + ch:89 + ch], in1=hcv, op0=ALU.mult, op1=ALU.add),
                               reads=[r_hs[pb], r_hc[pb], r_w], writes=[r_hc[pb]])
                      def gelu_part(pb=pb, j=j):
                          P.op("act", lambda q: q.activation(out=gl[pb][:], in_=hc[pb][:, 0:512], func=AF.Gelu_apprx_tanh), reads=[r_hc[pb]], writes=[r_gl[pb]])
                          P.op("dve", lambda q: q.tensor_tensor(out=actT[:, j * 512:(j + 1) * 512], in0=gl[pb][:], in1=hc[pb][:, 512:1024], op=ALU.mult),
                               reads=[r_gl[pb], r_hc[pb]], writes=[r_act[j]])
                      if pendG:
                          pendG.pop()()
                      pendG.append(gelu_part)
                  while pendG:
                      pendG.pop()()
                  for j in sorted(sched):
                      for f in sched[j]:
                          f()
                  for s4 in range(4 if 'H' not in SKIP else 0):
                      sub = t * 4 + s4
                      xs = x1slot[sub]
                      yb = sub % 2
                      for half in range(2):
                          for j in range(NJ):
                              P.op("pe", lambda q, half=half, j=j, s4=s4: q.matmul(dnb[s4 % 2][half][:, :], actT[:, j * 512 + s4 * 128:j * 512 + (s4 + 1) * 128], w_dn[:, j * D + half * 512:j * D + (half + 1) * 512],
                                                                                   start=(j == 0), stop=(j == NJ - 1)),
                                   reads=[r_act[j], r_wdn], writes=[r_dnb[s4 % 2][half]], signal=(j == NJ - 1))
                      if 'J' in SKIP:
                          continue
                      st = stt[yb]
                      for half in range(2):
                          P.op("act", lambda q, half=half, st=st, s4=s4: q.activation(out=junk[:, half * 512:(half + 1) * 512], in_=dnb[s4 % 2][half][:, :], func=AF.Square, accum_out=st[:, 6 + half:7 + half]),
                               reads=[r_dnb[s4 % 2][half]], writes=[r_junk, r_stt[yb]])
                      P.op("dve", lambda q, st=st: q.tensor_tensor(out=st[:, 6:7], in0=st[:, 6:7], in1=st[:, 7:8], op=ALU.add), reads=[r_stt[yb]], writes=[r_stt[yb]])
                      P.op("act", lambda q, st=st: q.activation(out=st[:, 7:8], in_=st[:, 6:7], func=AF.Sqrt, bias=epsb[:, 0:1], scale=1.0 / D), reads=[r_stt[yb], r_const], writes=[r_stt[yb]])
                      P.op("dve", lambda q, st=st: q.reciprocal(out=st[:, 7:8], in_=st[:, 7:8]), reads=[r_stt[yb]], writes=[r_stt[yb]])
                      for half in range(2):
                          P.op("dve", lambda q, half=half, st=st, s4=s4: q.scalar_tensor_tensor(out=tmpm[:, half * 512:(half + 1) * 512], in0=dnb[s4 % 2][half][:, :], scalar=st[:, 7:8], in1=gfpost[:, half * 512:(half + 1) * 512],
                                                                                  op0=ALU.mult, op1=ALU.mult),
                               reads=[r_dnb[s4 % 2][half], r_stt[yb], r_w], writes=[r_tmpm])
                      P.op("pool", lambda q, xs=xs: q.tensor_tensor(out=x1[xs][:], in0=tmpm[:], in1=x1[xs][:], op=ALU.add), reads=[r_tmpm, r_x1[xs]], writes=[r_x1[xs]])
                      if 'I' not in SKIP:
                          P.dma("act", lambda q, xs=xs, sub=sub: q.dma_start(out=y_d[sub * 128:(sub + 1) * 128, :], in_=x1[xs][:]), yst[yb], reads=[r_x1[xs]])

              def ht3(i):
                  return HT[i][:].rearrange("p (k t) -> p k t", k=8)

              edv = edge[:].rearrange("p (k t) -> p k t", k=8)

              def post_halo():
                  put_ht(0, 0, 63, 1, flagcol=0)
                  srcv = tp_ps[:].rearrange("p (k t) -> p k t", k=8)[:, :, 64:65]
                  gsrc = gT_ffn[:].rearrange("p (k t) -> p k t", k=8)[:, :, 0:1]
                  P.op("dve", lambda q: q.tensor_tensor(out=edv, in0=srcv, in1=gsrc, op=ALU.mult), reads=[r_tp, r_const], writes=[r_edge])
                  P.op("dve", lambda q: q.tensor_scalar(out=edv, in0=edv, scalar1=flags[:, 1:2], scalar2=None, op0=ALU.mult), reads=[r_edge, r_const], writes=[r_edge])

              if G["halo"]:
                  D1(32, post_halo)
              else:
                  P.op("pool", lambda q: q.memset(edge[:], 0.0), writes=[r_edge])

              def post_main(T, s4):
                  return lambda: put_ht(T % 2, 1 + s4 * 128, 0, 128)

              def lookahead(T):
                  def post():
                      put_ht((T - 1) % 2, 513, 0, 1)
                      put_ht(T % 2, 1, 0, 128)
                      P.op("pool", lambda q: q.tensor_copy(out=ht3(T % 2)[:, :, 0:1], in_=ht3((T - 1) % 2)[:, :, 512:513]),
                           reads=[r_HT[(T - 1) % 2]], writes=[r_HT[T % 2]])
                  D1(4 * T, post)

              def right_edge(T):
                  P.op("pool", lambda q: q.tensor_copy(out=ht3(T % 2)[:, :, 513:514], in_=edv), reads=[r_edge], writes=[r_HT[T % 2]])

              for s4 in range(4):
                  D1(s4, post_main(0, s4))
              if NOT_ > 1:
                  lookahead(1)
              else:
                  right_edge(0)
              def lookahead_stages(T):
                  def post():
                      put_ht((T - 1) % 2, 513, 0, 1)
                      put_ht(T % 2, 1, 0, 128)
                      P.op("pool", lambda q: q.tensor_copy(out=ht3(T % 2)[:, :, 0:1], in_=ht3((T - 1) % 2)[:, :, 512:513]),
                           reads=[r_HT[(T - 1) % 2]], writes=[r_HT[T % 2]])
                  return D1_stages(4 * T, post)

              slots_j = [(0, 1, 3, 5, 7), (4, 7, 9, 11, 13), (8, 12, 14, 16, 18)]
              for t in range(NOT_):
                  sched = {}
                  if t + 1 < NOT_:
                      for s4 in range(1, 4):
                          st5 = D1_stages(4 * (t + 1) + s4, post_main(t + 1, s4))
                          for jj, f in zip(slots_j[s4 - 1], st5):
                              sched.setdefault(jj, []).append(f)
                  last_stage = None
                  if t + 2 < NOT_:
                      st5 = lookahead_stages(t + 2)
                      for jj, f in zip((10, 17, 19, 20), st5[:4]):
                          sched.setdefault(jj, []).append(f)
                      last_stage = st5[4]
                  FFN(t, sched)
                  if last_stage is not None:
                      last_stage()
                  elif t + 1 < NOT_:
                      right_edge(t + 1)
              P.barrier()
              P.flush()
              if stop == kind + "D":
                  raise _Stop()


    except _Stop:
        pass

    P.barrier()
    P.flush()
    print("instr counts", P.ninstr, "sems", len(P.allsems) + len(P.dmastates))
    return nc, P, gs


def _rope_tabs(pos):
    inv = (1.0 / (np.float32(10000.0) ** (np.arange(0, 32, 2, dtype=np.float32) / np.float32(32)))).astype(np.float32)
    ang = pos.astype(np.float32)[:, None] * inv[None, :]
    return np.cos(ang).astype(np.float32), np.sin(ang).astype(np.float32)


def _rm_entry(abs_q_rows, abs_key_row0, rows_total):
    m = np.full((8, 128), NEG, np.float32)
    for gi, r in enumerate(abs_q_rows):
        if r is None or r < 0 or r >= rows_total:
            m[gi, :] = 0.0
            continue
        rs = min(max(r - 4, 0), rows_total - 8)
        for krl in range(2):
            kr = abs_key_row0 + krl
            if 0 <= kr < rows_total and rs <= kr < rs + 8:
                m[gi, krl * 64:(krl + 1) * 64] = 0.0
    return m


def _tr2_table(rpb, interior=False):
    H = rpb.shape[0]
    T = np.full((H, 128, 24, 64), NEG, np.float32)
    kc = np.arange(64)[:, None]
    qc = np.arange(64)[None, :]
    qs = np.clip(qc - 8, 0, 48)
    colv = (kc >= qs) & (kc < qs + 16)
    dc = np.clip(kc - qc + 15, 0, 30)
    for krl in range(2):
        for ei in range(24):
            dr = 7 + krl - (ei - 10)
            if (3 <= dr <= 10) if interior else (0 <= dr <= 14):
                blk = rpb[:, dr, :][:, dc]
                blk = np.where(colv[None], blk, np.float32(NEG))
                T[:, krl * 64:(krl + 1) * 64, ei, :] = blk
    return T.reshape(H, 128, 24 * 64)


_CACHE = {}


def make_in_maps(x_prompt, x_sample, g_mix_pre, w_in, g_q_lat, w_q_up, g_kv_lat, w_kv_up, na_rpb, w_o,
                 g_mix_post, g_ffn_pre, w_ffn_up, ffn_conv_w, ffn_conv_b, w_ffn_down, g_ffn_post):
    f32 = np.float32
    x_prompt = np.asarray(x_prompt, f32)
    x_sample = np.asarray(x_sample, f32)

    shared = {
        "g_mix_pre": np.asarray(g_mix_pre[0], f32), "w_in": np.asarray(w_in[0], f32), "g_q_lat": np.asarray(g_q_lat[0], f32),
        "w_q_up": np.asarray(w_q_up[0], f32), "g_kv_lat": np.asarray(g_kv_lat[0], f32), "w_kv_up": np.asarray(w_kv_up[0], f32),
        "w_o": np.asarray(w_o[0], f32), "g_mix_post": np.asarray(g_mix_post[0], f32), "g_ffn_pre": np.asarray(g_ffn_pre[0], f32),
        "w_ffn_up": np.asarray(w_ffn_up[0], f32), "ffn_conv_w": np.asarray(ffn_conv_w[0], f32), "ffn_conv_b": np.asarray(ffn_conv_b[0], f32),
        "w_ffn_down": np.asarray(w_ffn_down[0], f32), "g_ffn_post": np.asarray(g_ffn_post[0], f32),
        "tr2": _tr2_table(np.asarray(na_rpb[0], f32)),
        "tr2i": _tr2_table(np.asarray(na_rpb[0], f32), interior=True),
        "ident": np.eye(128, dtype=f32),
    }
    sel = np.zeros((4, 128, 96), f32)
    for c in range(4):
        for r in range(32):
            sel[c, 32 * c + r, 64 + r] = 1.0
    shared["sel"] = sel
    qrsel = np.zeros((8, 512), f32)
    for gi in range(8):
        qrsel[gi, gi * 64:(gi + 1) * 64] = 1.0
    shared["qrsel"] = qrsel

    in_maps = []
    for core in range(8):
        pb, pq = core // 4, core % 4
        m = dict(shared)
        G = GEO["p"]
        xb = x_prompt[pb]
        order = [pq] + [i for i in range(4) if i != pq]
        pos_kv = np.concatenate([np.arange(o * 4096, (o + 1) * 4096) for o in order])
        m["xkv_p"] = np.ascontiguousarray(xb[pos_kv])
        r0 = pq * 64
        xg = xb.reshape(256, 64, D)
        xna = np.zeros((74, 64, D), f32)
        lo, hi = r0 - 6, r0 + 68
        a, b = max(lo, 0), min(hi, 256)
        xna[a - lo:b - lo] = xg[a:b]
        m["xna_p"] = xna.reshape(74 * 64, D)
        ck, sk = _rope_tabs(pos_kv)
        m["cosk_p"] = np.ascontiguousarray(ck.reshape(128, 128, 16).transpose(1, 0, 2).reshape(128, 128 * 16))
        m["sink_p"] = np.ascontiguousarray(sk.reshape(128, 128, 16).transpose(1, 0, 2).reshape(128, 128 * 16))
        pos_q = np.concatenate([np.arange(pq * 4096, (pq + 1) * 4096), np.arange(pq * 4096 - 64, pq * 4096), np.arange((pq + 1) * 4096, (pq + 1) * 4096 + 64)])
        cq, sq = _rope_tabs(np.clip(pos_q, 0, 16383))
        qc_t = np.ones((96, G["NQ"]), f32)
        qs_t = np.zeros((96, G["NQ"]), f32)
        qc_t[64:80] = cq.T
        qc_t[80:96] = cq.T
        qs_t[64:80] = sq.T
        qs_t[80:96] = sq.T
        m["qc_p"], m["qs_p"] = qc_t, qs_t
        rm = np.zeros((8, G["nslots"] * 128), f32)
        for gi, gr in enumerate(G["groups"]):
            if gi < 8:
                qrows = [r0 + 8 * gi + i for i in range(8)]
            elif gi == 8:
                qrows = [r0 - 1] + [None] * 7
            else:
                qrows = [r0 + 64] + [None] * 7
            for (ti, off), slot in zip(gr["tiles"], gr["slots"]):
                rm[:, slot * 128:(slot + 1) * 128] = _rm_entry(qrows, (r0 - 6) + 2 * ti, 256)
        m["rm_p"] = rm
        m["flags"] = np.tile(np.array([[1.0 if pq > 0 else 0.0, 1.0 if pq < 3 else 0.0]], f32), (128, 1))
        G = GEO["s"]
        xs = x_sample[core]
        m["xkv_s"] = xs
        m["xna_s"] = xs
        pos = np.arange(2048)
        ck, sk = _rope_tabs(pos)
        m["cosk_s"] = np.ascontiguousarray(ck.reshape(16, 128, 16).transpose(1, 0, 2).reshape(128, 16 * 16))
        m["sink_s"] = np.ascontiguousarray(sk.reshape(16, 128, 16).transpose(1, 0, 2).reshape(128, 16 * 16))
        qc_t = np.ones((96, 2048), f32)
        qs_t = np.zeros((96, 2048), f32)
        qc_t[64:80] = ck.T
        qc_t[80:96] = ck.T
        qs_t[64:80] = sk.T
        qs_t[80:96] = sk.T
        m["qc_s"], m["qs_s"] = qc_t, qs_t
        rm = np.zeros((8, G["nslots"] * 128), f32)
        for gi, gr in enumerate(G["groups"]):
            qrows = [8 * gi + i for i in range(8)]
            for (ti, off), slot in zip(gr["tiles"], gr["slots"]):
                rm[:, slot * 128:(slot + 1) * 128] = _rm_entry(qrows, 2 * ti, 32)
        m["rm_s"] = rm
        in_maps.append(m)
    return in_maps


def kernel(**inputs):
    f32 = np.float32
    in_maps = make_in_maps(**inputs)
    if "nc" not in _CACHE:
        _CACHE["nc"] = build_program()
    nc, P, gs = _CACHE["nc"]
    res = run_bass_kernel_spmd(nc, in_maps, core_ids=list(range(8)))
    y_p = np.zeros((2, 16384, D), f32)
    y_s = np.zeros((8, 2048, D), f32)
    for core in range(8):
        pb, pq = core // 4, core % 4
        y_p[pb, pq * 4096:(pq + 1) * 4096] = res.results[core]["y_p"]
        y_s[core] = res.results[core]["y_s"]
    return (y_p, y_s)
```

```python
import contextlib
import numpy as np
import concourse.bass as bass
import concourse.mybir as mybir
from concourse.bass_utils import run_bass_kernel_spmd

F32 = mybir.dt.float32
BF16 = mybir.dt.bfloat16
AF = mybir.ActivationFunctionType
ALU = mybir.AluOpType

ENGS = ("pe", "act", "dve", "pool", "sp")
SEM_ROT = 24000
NEG = -30000.0
EPS = 1e-6
D = 1024
DFF = 2816
NJ = 22


class Res:
    __slots__ = ("name", "w", "r")

    def __init__(self, name):
        self.name = name
        self.w = None
        self.r = {}


class Prog:
    def __init__(self, nc):
        self.nc = nc
        self.stack = contextlib.ExitStack()
        self.ops = {e: [] for e in ENGS}
        self.esem = {}
        self.own = {e: set() for e in ENGS}
        self.ecnt = {e: 0 for e in ENGS}
        self.lazy = {e: False for e in ENGS}
        self.seen = {e: {} for e in ENGS}
        self.allsems = []
        self.dmastates = []
        for e in ENGS:
            self._newsem(e)
        self.ninstr = {e: 0 for e in ENGS}

    def _newsem(self, e):
        s = self.stack.enter_context(self.nc.semaphore())
        self.esem[e] = s
        self.own[e].add(id(s))
        self.ecnt[e] = 0
        self.allsems.append(s)

    def dsem(self):
        s = self.stack.enter_context(self.nc.semaphore())
        st = [s, 0]
        self.dmastates.append(st)
        return st

    def _need(self, eng, ev, waits):
        if ev is None:
            return
        sem, val = ev
        if id(sem) in self.own[eng]:
            if eng == "pe" or eng == "sp":
                return
            if sem is self.esem[eng] and val > self.ecnt[eng]:
                return
        if self.seen[eng].get(id(sem), 0) >= val:
            return
        cur = waits.get(id(sem), (sem, 0))
        if val > cur[1]:
            waits[id(sem)] = (sem, val)

    def _deps(self, eng, reads, writes):
        waits = {}
        for R in reads:
            self._need(eng, R.w, waits)
        for R in writes:
            self._need(eng, R.w, waits)
            for s, v in R.r.values():
                self._need(eng, (s, v), waits)
        for k, (s, v) in waits.items():
            self.seen[eng][k] = v
        return list(waits.values())

    def _mark(self, ev, reads, writes):
        sem, val = ev
        for R in reads:
            old = R.r.get(id(sem))
            if old is None or old[1] < val:
                R.r[id(sem)] = (sem, val)
        for R in writes:
            R.w = ev
            R.r = {}

    def op(self, eng, fn, reads=(), writes=(), signal=True):
        waits = self._deps(eng, reads, writes)
        if signal:
            if self.ecnt[eng] >= SEM_ROT and not self.lazy[eng]:
                self._newsem(eng)
            self.ecnt[eng] += 1
            val = self.ecnt[eng]
            self.lazy[eng] = False
        else:
            val = self.ecnt[eng] + 1
            self.lazy[eng] = True
        sem = self.esem[eng]
        self.ops[eng].append((waits, fn, sem if signal else None, 1))
        self.ninstr[eng] += 1
        ev = (sem, val)
        self._mark(ev, reads, writes)
        return ev

    def dma(self, eng, fn, st, reads=(), writes=()):
        waits = self._deps(eng, reads, writes)
        if st[1] >= SEM_ROT:
            st[0] = self.stack.enter_context(self.nc.semaphore())
            st[1] = 0
        st[1] += 16
        ev = (st[0], st[1])
        self.ops[eng].append((waits, fn, st[0], 16))
        self.ninstr[eng] += 1
        self._mark(ev, reads, writes)
        return ev

    def barrier(self, extra_events=()):
        evs = []
        for e in ENGS:
            if self.lazy[e]:
                raise RuntimeError("barrier with pending lazy event on " + e)
            if self.ecnt[e] > 0:
                evs.append((self.esem[e], self.ecnt[e]))
        for st in self.dmastates:
            if st[1] > 0:
                evs.append((st[0], st[1]))
        evs.extend(extra_events)
        for e in ENGS:
            waits = {}
            for ev in evs:
                self._need(e, ev, waits)
            for k, (s, v) in waits.items():
                self.seen[e][k] = v
            self.ops[e].append((list(waits.values()), None, None, 0))

    def flush(self, name=None):
        nc = self.nc
        engobj = {"pe": "tensor", "act": "scalar", "dve": "vector", "pool": "gpsimd", "sp": "sync"}
        self.nflush = getattr(self, "nflush", 0) + 1
        scope = nc.named_scope(name or f"ph{self.nflush}")
        with scope, nc.Block() as block:
            for e in ENGS:
                ops = self.ops[e]
                if not ops:
                    continue

                def body(q, ops=ops):
                    for waits, fn, sem, inc in ops:
                        for s, v in waits:
                            q.wait_ge(s, v)
                        if fn is not None:
                            ins = fn(q)
                            if sem is not None:
                                ins.then_inc(sem, inc)
                getattr(block, engobj[e])(body)
        self.ops = {e: [] for e in ENGS}


def job_geometry(kind):
    g = {}
    if kind == "p":
        g["NKV"] = 16384
        g["NQ"] = 4224
        g["NOWN"] = 4096
        g["NNA"] = 4736
        g["qtiles"] = [(i * 512, 512) for i in range(8)] + [(4096, 128)]
        own = [[(0, 128, 384 + 128 * i)] for i in range(32)]
        own.append([(0, 64, 320), (64, 64, 4480)])
        g["own_tiles"] = own
        groups = []
        for t in range(8):
            tiles = [(4 * t + 1 + m, (8 * t + 2 + 2 * m) - (6 + 8 * t)) for m in range(8)]
            groups.append(dict(q0=512 * t, N=512, tiles=tiles))
        groups.append(dict(q0=4096, N=64, tiles=[(m, 2 * m - 5) for m in range(5)]))
        groups.append(dict(q0=4160, N=64, tiles=[(33 + m, 66 + 2 * m - 70) for m in range(4)]))
        g["groups"] = groups
        g["na_of_q"] = lambda q: (384 + q) if q < 4096 else ((320 + q - 4096) if q < 4160 else (4480 + q - 4160))
        g["halo"] = True
    else:
        g["NKV"] = 2048
        g["NQ"] = 2048
        g["NOWN"] = 2048
        g["NNA"] = 2048
        g["qtiles"] = [(i * 512, 512) for i in range(4)]
        g["own_tiles"] = [[(0, 128, 128 * i)] for i in range(16)]
        groups = []
        for t in range(4):
            tiles = []
            for idx in range(4 * t - 2, 4 * t + 6):
                if 0 <= idx < 16:
                    tiles.append((idx, 2 * idx - 8 * t))
            groups.append(dict(q0=512 * t, N=512, tiles=tiles))
        g["groups"] = groups
        g["na_of_q"] = lambda q: q
        g["halo"] = False
    for gi, gr in enumerate(g["groups"]):
        gr["interior"] = (kind == "p" and 1 <= gi <= 6) or (kind == "s" and 1 <= gi <= 2)
        if gr["interior"]:
            gr["tiles"] = sorted(gr["tiles"], key=lambda to: (to[1] != 2, to[1]))
    slot = 0
    for gr in g["groups"]:
        gr["slots"] = list(range(slot, slot + len(gr["tiles"])))
        slot += len(gr["tiles"])
    g["nslots"] = slot
    return g


GEO = {"p": job_geometry("p"), "s": job_geometry("s")}


class _Stop(Exception):
    pass


class _Skip(Exception):
    pass


import os
SKIP = os.environ.get('SKIP', '')


def build_program(stop=None):
    nc = bass.Bass("TRN2", target_bir_lowering=False)

    def din(name, shape, dt=F32):
        return nc.dram_tensor(name, list(shape), dt, kind="ExternalInput").ap()

    def dout(name, shape, dt=F32):
        return nc.dram_tensor(name, list(shape), dt, kind="ExternalOutput").ap()

    def dscratch(name, shape, dt):
        return nc.dram_tensor(name, list(shape), dt, kind="Internal").ap()

    I = {}
    for k in ("p", "s"):
        G = GEO[k]
        I["xkv_" + k] = din("xkv_" + k, [G["NKV"], D])
        I["xna_" + k] = din("xna_" + k, [G["NNA"], D])
        I["cosk_" + k] = din("cosk_" + k, [128, (G["NKV"] // 128) * 16])
        I["sink_" + k] = din("sink_" + k, [128, (G["NKV"] // 128) * 16])
        I["qc_" + k] = din("qc_" + k, [96, G["NQ"]])
        I["qs_" + k] = din("qs_" + k, [96, G["NQ"]])
        I["rm_" + k] = din("rm_" + k, [8, G["nslots"] * 128])
        I["y_" + k] = dout("y_" + k, [G["NOWN"], D])
        I["at_" + k] = dscratch("at_" + k, [G["NQ"] // 128, 128, 8, 128], BF16)
    I["flags"] = din("flags", [128, 2])
    I["gcols"] = din("gcols", [128, 32])
    I["cwl"] = din("cwl", [128, 176])
    I["tr2"] = din("tr2", [8, 128, 24 * 64])
    I["tr2i"] = din("tr2i", [8, 128, 24 * 64])
    I["sel"] = din("sel", [4, 128, 96])
    I["ident"] = din("ident", [128, 128])
    I["qrsel"] = din("qrsel", [8, 512])
    for nm, shp in (("g_mix_pre", [D]), ("w_in", [D, 1952]), ("g_q_lat", [256]), ("w_q_up", [256, 768]),
                    ("g_kv_lat", [128]), ("w_kv_up", [128, 1024]), ("w_o", [D, D]), ("g_mix_post", [D]),
                    ("g_ffn_pre", [D]), ("w_ffn_up", [D, 2 * DFF]), ("ffn_conv_w", [3, 2 * DFF]),
                    ("ffn_conv_b", [2 * DFF]), ("w_ffn_down", [DFF, D]), ("g_ffn_post", [D])):
        I[nm] = din(nm, shp)

    P = Prog(nc)
    gs = contextlib.ExitStack()

    uid = [0]

    def sb(stack, name, shape, dt):
        uid[0] += 1
        return stack.enter_context(nc.sbuf_tensor(f"sb{uid[0]}_{name}", list(shape), dt))

    def ps(stack, name, shape, dt=F32):
        uid[0] += 1
        return stack.enter_context(nc.psum_tensor(f"ps{uid[0]}_{name}", list(shape), dt))

    ident = sb(gs, "ident", [128, 128], BF16)
    ones_f = sb(gs, "ones_f", [128, 128], F32)
    gT_pre = sb(gs, "gT_pre", [128, 8 * 128], F32)
    gT_ffn = sb(gs, "gT_ffn", [128, 8 * 128], F32)
    gT_q = sb(gs, "gT_q", [128, 2 * 128], F32)
    gcols = sb(gs, "gcols", [128, 32], F32)
    flags = sb(gs, "flags", [128, 2], F32)
    epsb = sb(gs, "epsb", [128, 1], F32)
    r_const = Res("const")
    cst = P.dsem()

    ident32 = sb(gs, "ident32", [128, 128], F32)
    r_id32 = Res("id32")
    P.dma("sp", lambda q: q.dma_start(out=ident32[:], in_=I["ident"]), cst, writes=[r_id32])
    P.op("dve", lambda q: q.tensor_copy(out=ident[:], in_=ident32[:]), reads=[r_id32], writes=[r_const])
    P.dma("sp", lambda q: q.dma_start(out=gcols[:], in_=I["gcols"]), cst, writes=[r_const])
    P.dma("sp", lambda q: q.dma_start(out=flags[:], in_=I["flags"]), cst, writes=[r_const])
    P.op("dve", lambda q: q.memset(ones_f[:], 1.0), writes=[r_const])
    P.op("dve", lambda q: q.memset(epsb[:], EPS), writes=[r_const])
    for k in range(8):
        P.op("dve", lambda q, k=k: q.tensor_scalar(out=gT_pre[:, k * 128:(k + 1) * 128], in0=ones_f[:], scalar1=gcols[:, k:k + 1], scalar2=None, op0=ALU.mult),
             reads=[r_const], writes=[r_const])
        P.op("dve", lambda q, k=k: q.tensor_scalar(out=gT_ffn[:, k * 128:(k + 1) * 128], in0=ones_f[:], scalar1=gcols[:, 8 + k:9 + k], scalar2=None, op0=ALU.mult),
             reads=[r_const], writes=[r_const])
    for k in range(2):
        P.op("dve", lambda q, k=k: q.tensor_scalar(out=gT_q[:, k * 128:(k + 1) * 128], in0=ones_f[:], scalar1=gcols[:, 16 + k:17 + k], scalar2=None, op0=ALU.mult),
             reads=[r_const], writes=[r_const])

    def rstd(src_ap, n, junk_ap, ss_ap, rs_ap, r_src, r_junk, r_stat):
        P.op("act", lambda q: q.activation(out=junk_ap, in_=src_ap, func=AF.Square, accum_out=ss_ap),
             reads=[r_src], writes=[r_junk, r_stat])
        P.op("act", lambda q: q.activation(out=rs_ap, in_=ss_ap, func=AF.Sqrt, bias=epsb[:, 0:1], scale=1.0 / n),
             reads=[r_stat, r_const], writes=[r_stat])
        P.op("dve", lambda q: q.reciprocal(out=rs_ap, in_=rs_ap), reads=[r_stat], writes=[r_stat])

    class Front:
        def __init__(self, st, gT, nb=3, ntp=2):
            self.NB = nb
            self.NTP = ntp
            self.n2 = 0
            self.gT = gT
            self.xt = [sb(st, f"f_xt{i}", [128, D], F32) for i in range(self.NB)]
            self.junk = sb(st, "f_junk", [128, D], F32)
            self.hb = [sb(st, f"f_hb{i}", [128, D], BF16) for i in range(self.NB)]
            self.hT = [sb(st, f"f_hT{i}", [128, D], BF16) for i in range(self.NB)]
            self.stat = [sb(st, f"f_st{i}", [128, 2], F32) for i in range(self.NB)]
            self.tp = [ps(st, f"f_tp{i}", [128, D], BF16) for i in range(self.NTP)]
            self.r_x = [Res("fx") for _ in range(self.NB)]
            self.r_j = Res("fj")
            self.r_s = [Res("fs") for _ in range(self.NB)]
            self.r_hb = [Res("fhb") for _ in range(self.NB)]
            self.r_tp = [Res("ftp") for _ in range(self.NTP)]
            self.r_hT = [Res("fhT") for _ in range(self.NB)]
            self.ld = [P.dsem() for _ in range(self.NB)]
            self.n = 0

        def run1(self, xd, pieces):
            s = self.n % self.NB
            self.n += 1
            for (p0, nr, r0) in pieces:
                P.dma("sp", lambda q, p0=p0, nr=nr, r0=r0: q.dma_start(out=self.xt[s][p0:p0 + nr, :], in_=xd[r0:r0 + nr, :]),
                      self.ld[s], writes=[self.r_x[s]])
            rstd(self.xt[s][:], D, self.junk[:], self.stat[s][:, 0:1], self.stat[s][:, 1:2], self.r_x[s], self.r_j, self.r_s[s])
            P.op("dve", lambda q: q.tensor_scalar(out=self.hb[s][:], in0=self.xt[s][:], scalar1=self.stat[s][:, 1:2], scalar2=None, op0=ALU.mult),
                 reads=[self.r_x[s], self.r_s[s]], writes=[self.r_hb[s]])
            return s

        def run2(self, s):
            tpi = self.n2 % self.NTP
            self.n2 += 1
            for k in range(8):
                P.op("pe", lambda q, k=k: q.transpose(out=self.tp[tpi][:, k * 128:(k + 1) * 128], in_=self.hb[s][:, k * 128:(k + 1) * 128], identity=ident[:]),
                     reads=[self.r_hb[s], r_const], writes=[self.r_tp[tpi]], signal=(k == 7))
            P.op("dve", lambda q: q.tensor_tensor(out=self.hT[s][:], in0=self.tp[tpi][:], in1=self.gT[:], op=ALU.mult),
                 reads=[self.r_tp[tpi], r_const], writes=[self.r_hT[s]])
            return s

        def run(self, xd, pieces):
            return self.run2(self.run1(xd, pieces))

    def pipeline(nitems, stages, extra=None):
        K = len(stages)
        for step in range(nitems + K - 1):
            for k in range(K):
                i = step - k
                if 0 <= i < nitems:
                    stages[k](i)
            if extra:
                extra.pop(0)()

    S = {}
    S["w_in"] = dscratch("s_w_in", [D, 1952], BF16)
    S["w_q_up"] = dscratch("s_w_q_up", [256, 768], BF16)
    S["w_kv_up"] = dscratch("s_w_kv_up", [128, 1024], BF16)
    S["w_o"] = dscratch("s_w_o", [D, D], BF16)
    S["wd"] = dscratch("s_wd", [DFF, D], BF16)
    S["wu"] = dscratch("s_wu", [NJ, 128, 8, 256], BF16)
    S["tr2"] = dscratch("s_tr2", [8 * 128, 1536], BF16)
    S["tr2i"] = dscratch("s_tr2i", [8 * 128, 1536], BF16)
    S["rm_p"] = dscratch("s_rm_p", [8, GEO["p"]["nslots"] * 128], BF16)
    S["rm_s"] = dscratch("s_rm_s", [8, GEO["s"]["nslots"] * 128], BF16)
    S["qrsel"] = dscratch("s_qrsel", [8, 512], BF16)
    S["sel"] = dscratch("s_sel", [4 * 128, 96], BF16)
    class Conv:
        def __init__(self, stack, CB, nstg=3, engs=("dve", "pool", "act"), store_q="act"):
            self.engs = engs
            self.store_q = store_q
            self.CB = CB
            self.n = nstg
            self.s32 = [sb(stack, f"stg32_{i}", [128, CB], F32) for i in range(nstg)]
            self.s16 = [sb(stack, f"stg16_{i}", [128, CB], BF16) for i in range(nstg)]
            self.r32 = [Res("s32") for _ in range(nstg)]
            self.r16 = [Res("s16") for _ in range(nstg)]
            self.lds = [P.dsem() for _ in range(nstg)]
            self.sts = [P.dsem() for _ in range(nstg)]
            self.cn = 0

        def block(self, src_ap, nrows, ncols, store_fn):
            i = self.cn % self.n
            e = self.engs[self.cn % len(self.engs)]
            self.cn += 1
            s32, s16 = self.s32[i], self.s16[i]
            P.dma("sp", lambda q: q.dma_start(out=s32[0:nrows, 0:ncols], in_=src_ap), self.lds[i], writes=[self.r32[i]])
            if e == "act":
                P.op("act", lambda q: q.copy(out=s16[0:nrows, 0:ncols], in_=s32[0:nrows, 0:ncols]), reads=[self.r32[i]], writes=[self.r16[i]])
            else:
                P.op(e, lambda q: q.tensor_copy(out=s16[0:nrows, 0:ncols], in_=s32[0:nrows, 0:ncols]), reads=[self.r32[i]], writes=[self.r16[i]])
            fns = store_fn(s16)
            for fn in (fns if isinstance(fns, list) else [fns]):
                P.dma(self.store_q, fn, self.sts[i], reads=[self.r16[i]])

        def tasks2d(self, src, dst, R, C):
            out = []
            for r0 in range(0, R, 128):
                nr = min(128, R - r0)
                for c0 in range(0, C, self.CB):
                    ncl = min(self.CB, C - c0)
                    out.append(lambda r0=r0, nr=nr, c0=c0, ncl=ncl: self.block(
                        src[r0:r0 + nr, c0:c0 + ncl], nr, ncl,
                        lambda t: (lambda q: q.dma_start(out=dst[r0:r0 + nr, c0:c0 + ncl], in_=t[0:nr, 0:ncl]))))
            return out

    with contextlib.ExitStack() as sW:
        cv = Conv(sW, 2816)
        early = []
        early += cv.tasks2d(I["w_in"], S["w_in"], D, 1952)
        early += cv.tasks2d(I["w_q_up"], S["w_q_up"], 256, 768)
        early += cv.tasks2d(I["w_kv_up"], S["w_kv_up"], 128, 1024)
        early += cv.tasks2d(I["sel"].rearrange("c p n -> (c p) n"), S["sel"], 512, 96)
        early += cv.tasks2d(I["tr2"].rearrange("h p n -> (h p) n"), S["tr2"], 1024, 1536)
        early += cv.tasks2d(I["tr2i"].rearrange("h p n -> (h p) n"), S["tr2i"], 1024, 1536)
        early += cv.tasks2d(I["rm_p"], S["rm_p"], 8, GEO["p"]["nslots"] * 128)
        early += cv.tasks2d(I["rm_s"], S["rm_s"], 8, GEO["s"]["nslots"] * 128)
        early += cv.tasks2d(I["qrsel"], S["qrsel"], 8, 512)
        for f in early:
            f()
        P.barrier()
        P.flush()

    def late_conv_tasks(cv2):
        out = []
        out += cv2.tasks2d(I["w_o"], S["w_o"], D, D)
        out += cv2.tasks2d(I["w_ffn_down"], S["wd"], DFF, D)
        wu_v = S["wu"].rearrange("j p k c -> p j k c")
        nh = DFF // cv2.CB
        jb = cv2.CB // 128
        for k in range(8):
            for gu in range(2):
                for hf in range(nh):
                    out.append(lambda k=k, gu=gu, hf=hf: cv2.block(
                        I["w_ffn_up"][k * 128:(k + 1) * 128, gu * DFF + hf * cv2.CB:gu * DFF + (hf + 1) * cv2.CB], 128, cv2.CB,
                        lambda t: [(lambda q, j0=j0: q.dma_start(out=wu_v[:, hf * jb + j0:hf * jb + min(j0 + 4, jb), k, gu * 128:(gu + 1) * 128],
                                                              in_=t[:, j0 * 128:min(j0 + 4, jb) * 128].rearrange("p (j c) -> p j c", c=128)))
                                   for j0 in range(0, jb, 4)]))
        return out

    dbg = {}
    if stop is not None:
        dbg['d0'] = dout('dbg0', [128, 4096], F32)
        dbg['d1'] = dout('dbg1', [128, 4096], F32)
        dbg['d2'] = dout('dbg2', [128, 4096], F32)
    dst_ = P.dsem()

    def dump(key, src_ap, ncol, npart=128):
        P.dma('sp', lambda q: q.dma_start(out=dbg[key][0:npart, 0:ncol], in_=src_ap), dst_)

    def dumpconv(stack, key, src_ap, ncol, npart=128):
        t = sb(stack, 'dbgt', [128, 4096], F32)
        r = Res('dbgt')
        P.op('dve', lambda q: q.tensor_copy(out=t[0:npart, 0:ncol], in_=src_ap), writes=[r])
        P.dma('sp', lambda q: q.dma_start(out=dbg[key][0:npart, 0:ncol], in_=t[0:npart, 0:ncol]), dst_, reads=[r])

    try:
      for kind in ("p", "s"):
          G = GEO[kind]
          NKV, NQ, NOWN, NNA = G["NKV"], G["NQ"], G["NOWN"], G["NNA"]
          NKT = NKV // 128
          NT4 = NKT // 4
          xkv_d, xna_d, at_d, y_d = I["xkv_" + kind], I["xna_" + kind], I["at_" + kind], I["y_" + kind]

          def at_store(q, src_tile, c0, q0, N):
              kk, p0 = c0 // 128, c0 % 128
              if N % 128 == 0:
                  return q.dma_start(out=at_d[q0 // 128:(q0 + N) // 128, p0:p0 + 64, kk, :].rearrange("s p t -> p s t"),
                                     in_=src_tile[:, 0:N].rearrange("p (s t) -> p s t", t=128))
              t0 = q0 % 128
              return q.dma_start(out=at_d[q0 // 128, p0:p0 + 64, kk, t0:t0 + N], in_=src_tile[:, 0:N])

          with contextlib.ExitStack() as sAB:
              kvnT = sb(sAB, "kvnT", [128, NKV], BF16)
              KPE = sb(sAB, "KPE", [128, NT4 * 128], BF16)
              cqnT = sb(sAB, "cqnT", [128, 2 * NQ], BF16)
              r_kvn, r_kpe, r_cqn = Res("kvn"), Res("kpe"), Res("cqn")

              with contextlib.suppress(_Skip), contextlib.ExitStack() as sA:
                  if 'A' in SKIP:
                      raise _Skip()
                  w_a = sb(sA, "w_a", [128, 8 * 416], BF16)
                  cosk = sb(sA, "cosk", [128, NKT * 16], F32)
                  sink = sb(sA, "sink", [128, NKT * 16], F32)
                  r_wa = Res("wa")
                  wst = P.dsem()
                  for k in range(8):
                      P.dma("sp", lambda q, k=k: q.dma_start(out=w_a[:, k * 416:(k + 1) * 416], in_=S["w_in"][k * 128:(k + 1) * 128, 0:416]),
                            wst, writes=[r_wa])
                  P.dma("sp", lambda q: q.dma_start(out=cosk[:], in_=I["cosk_" + kind]), wst, writes=[r_wa])
                  P.dma("sp", lambda q: q.dma_start(out=sink[:], in_=I["sink_" + kind]), wst, writes=[r_wa])
                  fr = Front(sA, gT_pre, nb=6, ntp=3)
                  NZ = 4
                  zps = [ps(sA, f"a_z{i}", [128, 512], F32) for i in range(NZ)]
                  tp2 = ps(sA, "a_tp2", [128, 1024], BF16)
                  kvb = [sb(sA, f"a_kvb{i}", [128, 128], BF16) for i in range(NZ)]
                  krs = [sb(sA, f"a_krs{i}", [128, 32], F32) for i in range(NZ)]
                  tmp = [sb(sA, f"a_tmp{i}", [128, 64], F32) for i in range(NZ)]
                  X4 = [sb(sA, f"a_X4{i}", [128, 128], BF16) for i in range(2)]
                  cqb = [sb(sA, f"a_cqb{i}", [128, 256], BF16) for i in range(NZ)]
                  st2 = [sb(sA, f"a_st2{i}", [128, 2], F32) for i in range(NZ)]
                  junk2 = sb(sA, "a_junk2", [128, 256], F32)
                  r_z = [Res("z") for _ in range(NZ)]
                  r_kvb = [Res("kvb") for _ in range(NZ)]
                  r_krs = [Res("krs") for _ in range(NZ)]
                  r_tmp = [Res("tmp") for _ in range(NZ)]
                  r_X4 = [Res("X4") for _ in range(2)]
                  r_cqb = [Res("cqb") for _ in range(NZ)]
                  r_st2 = [Res("st2") for _ in range(NZ)]
                  r_j2 = Res("j2")
                  r_tp2a = r_tp2b = r_tp2c = Res("tp2")

                  late = []
                  items = [(t, c) for t in range(NT4) for c in range(4)]
                  slot = {}

                  def kv_s0(i):
                      t, c = items[i]
                      slot[i] = fr.run1(xkv_d, [(0, 128, (c * NT4 + t) * 128)])

                  def kv_s1(i):
                      fr.run2(slot[i])

                  def kv_s2(i):
                      t, c = items[i]
                      ti = c * NT4 + t
                      s_ = slot[i]
                      z = i % NZ
                      for k in range(8):
                          P.op("pe", lambda q, k=k: q.matmul(zps[z][:, 0:160], fr.hT[s_][:, k * 128:(k + 1) * 128], w_a[:, k * 416 + 256:k * 416 + 416], start=(k == 0), stop=(k == 7)),
                               reads=[fr.r_hT[s_], r_wa], writes=[r_z[z]], signal=(k == 7))
                      P.op("act", lambda q: q.copy(out=krs[z][:], in_=zps[z][:, 128:160]), reads=[r_z[z]], writes=[r_krs[z]])
                      rstd(zps[z][:, 0:128], 128, junk2[:, 0:128], st2[z][:, 0:1], st2[z][:, 1:2], r_z[z], r_j2, r_st2[z])
                      P.op("dve", lambda q: q.tensor_scalar(out=kvb[z][:], in0=zps[z][:, 0:128], scalar1=st2[z][:, 1:2], scalar2=None, op0=ALU.mult),
                           reads=[r_z[z], r_st2[z]], writes=[r_kvb[z]])
                      co = cosk[:, ti * 16:(ti + 1) * 16]
                      si = sink[:, ti * 16:(ti + 1) * 16]
                      x1_ = krs[z][:, 0:16]
                      x2_ = krs[z][:, 16:32]
                      tm = tmp[z]
                      xs = t % 2
                      P.op("pool", lambda q: q.tensor_tensor(out=tm[:, 0:16], in0=x1_, in1=co, op=ALU.mult), reads=[r_krs[z], r_wa], writes=[r_tmp[z]])
                      P.op("pool", lambda q: q.tensor_tensor(out=tm[:, 16:32], in0=x2_, in1=si, op=ALU.mult), reads=[r_krs[z], r_wa], writes=[r_tmp[z]])
                      P.op("pool", lambda q: q.tensor_tensor(out=tm[:, 32:48], in0=x1_, in1=si, op=ALU.mult), reads=[r_krs[z], r_wa], writes=[r_tmp[z]])
                      P.op("pool", lambda q: q.tensor_tensor(out=tm[:, 48:64], in0=x2_, in1=co, op=ALU.mult), reads=[r_krs[z], r_wa], writes=[r_tmp[z]])
                      P.op("pool", lambda q: q.tensor_tensor(out=X4[xs][:, 32 * c:32 * c + 16], in0=tm[:, 0:16], in1=tm[:, 16:32], op=ALU.subtract),
                           reads=[r_tmp[z]], writes=[r_X4[xs]])
                      P.op("pool", lambda q: q.tensor_tensor(out=X4[xs][:, 32 * c + 16:32 * c + 32], in0=tm[:, 32:48], in1=tm[:, 48:64], op=ALU.add),
                           reads=[r_tmp[z]], writes=[r_X4[xs]])

                  def kv_s3(i):
                      t, c = items[i]
                      ti = c * NT4 + t
                      z = i % NZ
                      xs = t % 2
                      P.op("pe", lambda q: q.transpose(out=tp2[:, 0:128], in_=kvb[z][:], identity=ident[:]),
                           reads=[r_kvb[z], r_const], writes=[r_tp2a])
                      P.op("dve", lambda q: q.tensor_scalar(out=kvnT[:, ti * 128:(ti + 1) * 128], in0=tp2[:, 0:128], scalar1=gcols[:, 18:19], scalar2=None, op0=ALU.mult),
                           reads=[r_tp2a, r_const], writes=[r_kvn])
                      if c == 3:
                          P.op("pe", lambda q: q.transpose(out=tp2[:, 128:256], in_=X4[xs][:], identity=ident[:]),
                               reads=[r_X4[xs], r_const], writes=[r_tp2b])
                          P.op("dve", lambda q: q.tensor_copy(out=KPE[:, t * 128:(t + 1) * 128], in_=tp2[:, 128:256]),
                               reads=[r_tp2b], writes=[r_kpe])

                  pipeline(len(items), [kv_s0, kv_s1, kv_s2, kv_s3], extra=late)

                  own = G["own_tiles"]
                  slot2 = {}

                  def ow_s0(i):
                      slot2[i] = fr.run1(xna_d, own[i])

                  def ow_s1(i):
                      fr.run2(slot2[i])

                  def ow_s2(i):
                      s_ = slot2[i]
                      z = i % NZ
                      for k in range(8):
                          P.op("pe", lambda q, k=k: q.matmul(zps[z][:, 0:256], fr.hT[s_][:, k * 128:(k + 1) * 128], w_a[:, k * 416:k * 416 + 256], start=(k == 0), stop=(k == 7)),
                               reads=[fr.r_hT[s_], r_wa], writes=[r_z[z]], signal=(k == 7))
                      rstd(zps[z][:, 0:256], 256, junk2[:, 0:256], st2[z][:, 0:1], st2[z][:, 1:2], r_z[z], r_j2, r_st2[z])
                      P.op("dve", lambda q: q.tensor_scalar(out=cqb[z][:], in0=zps[z][:, 0:256], scalar1=st2[z][:, 1:2], scalar2=None, op0=ALU.mult),
                           reads=[r_z[z], r_st2[z]], writes=[r_cqb[z]])

                  def ow_s3(i):
                      z = i % NZ
                      for kc in range(2):
                          P.op("pe", lambda q, kc=kc: q.transpose(out=tp2[:, 256 + kc * 128:256 + (kc + 1) * 128], in_=cqb[z][:, kc * 128:(kc + 1) * 128], identity=ident[:]),
                               reads=[r_cqb[z], r_const], writes=[r_tp2c], signal=(kc == 1))
                      for kc in range(2):
                          P.op("dve", lambda q, kc=kc: q.tensor_tensor(out=cqnT[:, kc * NQ + i * 128:kc * NQ + (i + 1) * 128], in0=tp2[:, 256 + kc * 128:256 + (kc + 1) * 128],
                                                                  in1=gT_q[:, kc * 128:(kc + 1) * 128], op=ALU.mult),
                               reads=[r_tp2c, r_const], writes=[r_cqn])

                  pipeline(len(own), [ow_s0, ow_s1, ow_s2, ow_s3], extra=late)
                  while late:
                      late.pop(0)()
                  P.barrier()
                  if stop == kind + "A":
                      dumpconv(sA, "d0", kvnT[:, 0:4096 if NKV >= 4096 else NKV], 4096 if NKV >= 4096 else NKV)
                      dumpconv(sA, "d1", KPE[:, 0:NT4 * 128], NT4 * 128)
                      dumpconv(sA, "d2", cqnT[:, 0:4096 if NQ >= 4096 else NQ], 4096 if NQ >= 4096 else NQ)
                      P.barrier()
                      P.flush()
                      raise _Stop()
                  P.flush()

              with contextlib.suppress(_Skip), contextlib.ExitStack() as sB:
                  if 'B' in SKIP:
                      raise _Skip()
                  KT = sb(sB, "KT", [96, NKV], BF16)
                  V = sb(sB, "V", [128, NKT * 65], BF16)
                  QT = sb(sB, "QT", [96, NQ], BF16)
                  wq = sb(sB, "wq", [128, 2 * 768], BF16)
                  wqr = sb(sB, "wqr", [128, 2 * 768], BF16)
                  wkv = sb(sB, "wkv", [128, 1024], BF16)
                  wk_ext = sb(sB, "wk_ext", [128, 8 * 96], BF16)
                  selb = sb(sB, "selb", [128, 4 * 96], BF16)
                  qc = [sb(sB, f"qc{i}", [96, 512], F32) for i in range(2)]
                  qs = [sb(sB, f"qs{i}", [96, 512], F32) for i in range(2)]
                  t1 = sb(sB, "t1", [96, 512], F32)
                  t2 = sb(sB, "t2", [96, 512], F32)
                  NPB = 3
                  pT = [sb(sB, f"pT{i}", [128, 512], BF16) for i in range(NPB)]
                  osb = [sb(sB, f"osb{i}", [65, 512], F32) for i in range(2)]
                  rrow = [sb(sB, f"rrow{i}", [65, 512], F32) for i in range(2)]
                  onb = [sb(sB, f"onb{i}", [64, 512], BF16) for i in range(2)]
                  Sps = [ps(sB, f"S{i}", [128, 512], F32) for i in range(NPB)]
                  Ops = [ps(sB, f"O{i}", [128, 512], F32) for i in range(2)]
                  Xps = [ps(sB, f"X{i}", [128, 512], F32) for i in range(3)]
                  r_w = Res("wB")
                  r_KT, r_V, r_QT = Res("KT"), Res("V"), Res("QT")
                  r_qcs = [Res("qcs") for _ in range(2)]
                  r_t = Res("t12")
                  r_pT = [Res("pT") for _ in range(NPB)]
                  r_S = [Res("S") for _ in range(NPB)]
                  r_O = [Res("O") for _ in range(2)]
                  r_X = [Res("X") for _ in range(3)]
                  r_osb = [Res("osb") for _ in range(2)]
                  r_rrow = [Res("rrow") for _ in range(2)]
                  r_onb = [Res("onb") for _ in range(2)]
                  wst = P.dsem()
                  tst = [P.dsem() for _ in range(2)]
                  ost = [P.dsem() for _ in range(2)]
                  for kc in range(2):
                      P.dma("sp", lambda q, kc=kc: q.dma_start(out=wq[:, kc * 768:(kc + 1) * 768], in_=S["w_q_up"][kc * 128:(kc + 1) * 128, :]), wst, writes=[r_w])
                      P.dma("sp", lambda q, kc=kc: q.dma_start(out=wqr[:, kc * 768:(kc + 1) * 768], in_=S["w_q_up"][kc * 128:(kc + 1) * 128, :]), wst, writes=[r_w])
                  P.dma("sp", lambda q: q.dma_start(out=wkv[:], in_=S["w_kv_up"]), wst, writes=[r_w])
                  for c in range(4):
                      P.dma("sp", lambda q, c=c: q.dma_start(out=selb[:, c * 96:(c + 1) * 96], in_=S["sel"][c * 128:(c + 1) * 128, :]), wst, writes=[r_w])
                  for kc in range(2):
                      for h in range(8):
                          b = kc * 768 + h * 96
                          P.op("pool", lambda q, b=b: q.tensor_scalar(out=wqr[:, b + 64:b + 80], in0=wq[:, b + 80:b + 96], scalar1=-1.0, scalar2=None, op0=ALU.mult),
                               reads=[r_w], writes=[r_w])
                          P.op("pool", lambda q, b=b: q.tensor_copy(out=wqr[:, b + 80:b + 96], in_=wq[:, b + 64:b + 80]), reads=[r_w], writes=[r_w])
                  P.op("pool", lambda q: q.memset(wk_ext[:], 0.0), reads=[r_w], writes=[r_w])
                  for h in range(8):
                      P.op("pool", lambda q, h=h: q.tensor_copy(out=wk_ext[:, h * 96:h * 96 + 64], in_=wkv[:, h * 128:h * 128 + 64]), reads=[r_w], writes=[r_w])
                  P.op("pool", lambda q: q.memset(V[:], 1.0), writes=[r_V])

                  nX = 0
                  npair = 0
                  nO = 0
                  nqc = 0
                  pendB = []
                  lateB = []
                  if kind == "p":
                      lateB = late_conv_tasks(Conv(sB, 1408, engs=("dve", "pool"), store_q="sp"))
                  BB = [Xps[0], Xps[1], Sps[0], Sps[1], Sps[2], Ops[0], Ops[1]]
                  r_BB = [r_X[0], r_X[1], r_S[0], r_S[1], r_S[2], r_O[0], r_O[1]]
                  for h in range(8):
                      while pendB:
                          pendB.pop()[1]()
                      for cn in range(NKV // 512):
                          x = nX % 7
                          nX += 1
                          c = (cn * 512) // (NT4 * 128)
                          loc = cn * 512 - c * NT4 * 128
                          P.op("pe", lambda q, x=x, h=h, cn=cn: q.matmul(BB[x][0:96, :], wk_ext[:, h * 96:(h + 1) * 96], kvnT[:, cn * 512:(cn + 1) * 512], start=True, stop=False),
                               reads=[r_w, r_kvn], writes=[r_BB[x]], signal=False)
                          P.op("pe", lambda q, x=x, c=c, loc=loc: q.matmul(BB[x][0:96, :], selb[:, c * 96:(c + 1) * 96], KPE[:, loc:loc + 512], start=False, stop=True),
                               reads=[r_w, r_kpe], writes=[r_BB[x]])
                          eng = "dve" if cn % 2 == 0 else "act"
                          if eng == "dve":
                              P.op("dve", lambda q, x=x, cn=cn: q.tensor_copy(out=KT[:, cn * 512:(cn + 1) * 512], in_=BB[x][0:96, :]), reads=[r_BB[x]], writes=[r_KT])
                          else:
                              P.op("act", lambda q, x=x, cn=cn: q.copy(out=KT[:, cn * 512:(cn + 1) * 512], in_=BB[x][0:96, :]), reads=[r_BB[x]], writes=[r_KT])
                      for g8 in range(NKT // 8):
                          x = nX % 7
                          nX += 1
                          for j in range(8):
                              kt = g8 * 8 + j
                              P.op("pe", lambda q, x=x, j=j, kt=kt, h=h: q.matmul(BB[x][:, j * 64:(j + 1) * 64], kvnT[:, kt * 128:(kt + 1) * 128], wkv[:, h * 128 + 64:h * 128 + 128], start=True, stop=True),
                                   reads=[r_w, r_kvn], writes=[r_BB[x]], signal=(j == 7))
                          vdst = V[:, g8 * 8 * 65:(g8 + 1) * 8 * 65].rearrange("p (j d) -> p j d", d=65)[:, :, 0:64]
                          vsrc = BB[x][:, :].rearrange("p (j d) -> p j d", d=64)
                          if g8 % 2 == 0:
                              P.op("dve", lambda q, vdst=vdst, vsrc=vsrc: q.tensor_copy(out=vdst, in_=vsrc), reads=[r_BB[x]], writes=[r_V])
                          else:
                              P.op("act", lambda q, vdst=vdst, vsrc=vsrc: q.copy(out=vdst, in_=vsrc), reads=[r_BB[x]], writes=[r_V])
                      for (q0, N) in G["qtiles"]:
                          b = nqc % 2
                          nqc += 1
                          P.dma("sp", lambda q, b=b, q0=q0, N=N: q.dma_start(out=qc[b][:, 0:N], in_=I["qc_" + kind][:, q0:q0 + N]), tst[b], writes=[r_qcs[b]])
                          P.dma("sp", lambda q, b=b, q0=q0, N=N: q.dma_start(out=qs[b][:, 0:N], in_=I["qs_" + kind][:, q0:q0 + N]), tst[b], writes=[r_qcs[b]])
                          xa = nX % 7
                          xb_ = (nX + 1) % 7
                          nX += 2
                          for (xx, wsrc) in ((xa, wq), (xb_, wqr)):
                              for kc in range(2):
                                  P.op("pe", lambda q, xx=xx, wsrc=wsrc, kc=kc, h=h, q0=q0, N=N: q.matmul(BB[xx][0:96, 0:N], wsrc[:, kc * 768 + h * 96:kc * 768 + (h + 1) * 96],
                                                                                                    cqnT[:, kc * NQ + q0:kc * NQ + q0 + N], start=(kc == 0), stop=(kc == 1)),
                                       reads=[r_w, r_cqn], writes=[r_BB[xx]], signal=(kc == 1))
                          P.op("dve", lambda q, xa=xa, b=b, N=N: q.tensor_tensor(out=t1[:, 0:N], in0=BB[xa][0:96, 0:N], in1=qc[b][:, 0:N], op=ALU.mult),
                               reads=[r_BB[xa], r_qcs[b]], writes=[r_t])
                          P.op("dve", lambda q, xb_=xb_, b=b, N=N: q.tensor_tensor(out=t2[:, 0:N], in0=BB[xb_][0:96, 0:N], in1=qs[b][:, 0:N], op=ALU.mult),
                               reads=[r_BB[xb_], r_qcs[b]], writes=[r_t])
                          P.op("dve", lambda q, q0=q0, N=N: q.tensor_tensor(out=QT[:, q0:q0 + N], in0=t1[:, 0:N], in1=t2[:, 0:N], op=ALU.add),
                               reads=[r_t], writes=[r_QT])
                      units = []
                      for (q0, N) in G["qtiles"]:
                          units.append((q0, N, nO % 2))
                          nO += 1
                      pairs = [(u, kt) for u in range(len(units)) for kt in range(NKT)]
                      DL = NPB - 1

                      def qk(n):
                          u, kt = pairs[n]
                          q0, N, o = units[u]
                          i = (npair + n) % NPB
                          P.op("pe", lambda q: q.matmul(Sps[i][:, 0:N], KT[:, kt * 128:(kt + 1) * 128], QT[:, q0:q0 + N], start=True, stop=True),
                               reads=[r_KT, r_QT], writes=[r_S[i]])
                          P.op("act", lambda q: q.activation(out=pT[i][:, 0:N], in_=Sps[i][:, 0:N], func=AF.Exp, scale=96.0 ** -0.5),
                               reads=[r_S[i]], writes=[r_pT[i]])

                      def pv(n):
                          u, kt = pairs[n]
                          q0, N, o = units[u]
                          i = (npair + n) % NPB
                          P.op("pe", lambda q: q.matmul(Ops[o][0:65, 0:N], V[:, kt * 65:(kt + 1) * 65], pT[i][:, 0:N], start=(kt == 0), stop=(kt == NKT - 1)),
                               reads=[r_V, r_pT[i]], writes=[r_O[o]], signal=(kt == NKT - 1))
                          if kt == NKT - 1:
                              P.op("dve", lambda q: q.tensor_copy(out=osb[o][:, 0:N], in_=Ops[o][0:65, 0:N]), reads=[r_O[o]], writes=[r_osb[o]])

                              def fin2(o=o, N=N, h=h, q0=q0):
                                  x = 2
                                  P.op("pe", lambda q: q.matmul(Xps[x][0:64, 0:N], ones_f[64:65, 0:64], osb[o][64:65, 0:N], start=True, stop=True),
                                       reads=[r_osb[o], r_const], writes=[r_X[x]])
                                  P.op("dve", lambda q: q.reciprocal(out=rrow[o][0:64, 0:N], in_=Xps[x][0:64, 0:N]), reads=[r_X[x]], writes=[r_rrow[o]])
                                  P.op("dve", lambda q: q.tensor_tensor(out=onb[o][:, 0:N], in0=osb[o][0:64, 0:N], in1=rrow[o][0:64, 0:N], op=ALU.mult),
                                       reads=[r_osb[o], r_rrow[o]], writes=[r_onb[o]])
                                  P.dma("sp", lambda q: at_store(q, onb[o], h * 64, q0, N), ost[o], reads=[r_onb[o]])
                              while pendB:
                                  pendB.pop()[1]()
                              pendB.append([6, fin2])
                              if lateB:
                                  lateB.pop(0)()

                      tot = len(pairs)
                      for n in range(tot + DL):
                          if n < tot:
                              qk(n)
                          if n - DL >= 0:
                              pv(n - DL)
                          if pendB:
                              pendB[0][0] -= 1
                              if pendB[0][0] <= 0:
                                  pendB.pop()[1]()
                      npair += tot
                  while pendB:
                      pendB.pop()[1]()
                  while lateB:
                      lateB.pop(0)()
                  P.barrier()
                  if stop == kind + "B":
                      tt = sb(sB, "dbgl", [128, 4096], BF16)
                      rr = Res("dbgl")
                      nn = 4096 if NQ >= 4096 else NQ
                      P.dma("sp", lambda q: q.dma_start(out=tt[:, 0:nn], in_=at_d[0:0 + 128, 0:nn]), dst_, writes=[rr])
                      t32 = sb(sB, "dbgt2", [128, 4096], F32)
                      r32 = Res("dbgt2")
                      P.op("dve", lambda q: q.tensor_copy(out=t32[:, 0:nn], in_=tt[:, 0:nn]), reads=[rr], writes=[r32])
                      P.dma("sp", lambda q: q.dma_start(out=dbg["d0"][:, 0:nn], in_=t32[:, 0:nn]), dst_, reads=[r32])
                      P.barrier()
                      P.flush()
                      raise _Stop()
                  P.flush()

          for hh in range(2):
              with contextlib.suppress(_Skip), contextlib.ExitStack() as sC:
                  if 'C' in SKIP:
                      raise _Skip()
                  NKTN = NNA // 128
                  w_n = sb(sC, "w_n", [128, 8 * 768], BF16)
                  nqT = sb(sC, "nqT", [128, 2 * NQ], BF16)
                  nkT = sb(sC, "nkT", [128, 2 * NNA], BF16)
                  nv = sb(sC, "nv", [128, NKTN * 4 * 65], BF16)
                  tr2 = sb(sC, "tr2", [128, 4 * 1536], BF16)
                  tr2i = sb(sC, "tr2i", [128, 4 * 1536], BF16)
                  rmt = sb(sC, "rmt", [8, G["nslots"] * 128], BF16)
                  qrs = sb(sC, "qrs", [8, 512], BF16)
                  r_w = Res("wC")
                  r_nq, r_nk, r_nv = Res("nq"), Res("nk"), Res("nv")
                  wst = P.dsem()
                  for k in range(8):
                      for j in range(3):
                          c0 = 416 + j * 512 + hh * 256
                          P.dma("sp", lambda q, k=k, j=j, c0=c0: q.dma_start(out=w_n[:, k * 768 + j * 256:k * 768 + (j + 1) * 256], in_=S["w_in"][k * 128:(k + 1) * 128, c0:c0 + 256]),
                                wst, writes=[r_w])
                  for j in range(4 if '1' not in SKIP else 0):
                      for c3 in range(3):
                          P.dma("sp", lambda q, j=j, c3=c3: q.dma_start(out=tr2[:, j * 1536 + c3 * 512:j * 1536 + (c3 + 1) * 512], in_=S["tr2"][(4 * hh + j) * 128:(4 * hh + j + 1) * 128, c3 * 512:(c3 + 1) * 512]), wst, writes=[r_w])
                          P.dma("sp", lambda q, j=j, c3=c3: q.dma_start(out=tr2i[:, j * 1536 + c3 * 512:j * 1536 + (c3 + 1) * 512], in_=S["tr2i"][(4 * hh + j) * 128:(4 * hh + j + 1) * 128, c3 * 512:(c3 + 1) * 512]), wst, writes=[r_w])
                  for c0 in range(0, G["nslots"] if '1' not in SKIP else 0, 8):
                      c1 = min(c0 + 8, G["nslots"])
                      P.dma("sp", lambda q, c0=c0, c1=c1: q.dma_start(out=rmt[:, c0 * 128:c1 * 128], in_=S["rm_" + kind][:, c0 * 128:c1 * 128]), wst, writes=[r_w])
                  P.dma("sp", lambda q: q.dma_start(out=qrs[:], in_=S["qrsel"]), wst, writes=[r_w])
                  if '2' not in SKIP:
                      P.op("pool", lambda q: q.memset(nv[:], 1.0), writes=[r_nv])
                  fr = Front(sC, gT_pre, nb=6, ntp=2)
                  NZ = 3
                  zk = [ps(sC, f"c_zk{i}", [128, 512], F32) for i in range(NZ)]
                  zv = [ps(sC, f"c_zv{i}", [128, 512], F32) for i in range(NZ)]
                  r_zk = [Res("zk") for _ in range(NZ)]
                  r_zv = [Res("zv") for _ in range(NZ)]
                  q_of_na = {}
                  for oi, pieces in enumerate(G["own_tiles"]):
                      if len(pieces) == 1:
                          q_of_na[pieces[0][2] // 128] = oi * 128
                  slotc = {}

                  def c_s0(ti):
                      slotc[ti] = fr.run1(xna_d, [(0, 128, ti * 128)])

                  def c_s1(ti):
                      fr.run2(slotc[ti])

                  def c_s2(ti):
                      s_ = slotc[ti]
                      z = ti % NZ
                      for pr in range(2):
                          for k in range(8):
                              P.op("pe", lambda q, pr=pr, k=k: q.matmul(zk[z][:, pr * 128:(pr + 1) * 128], w_n[:, k * 768 + 256 + pr * 128:k * 768 + 256 + (pr + 1) * 128],
                                                                    fr.hT[s_][:, k * 128:(k + 1) * 128], start=(k == 0), stop=(k == 7)),
                                   reads=[fr.r_hT[s_], r_w], writes=[r_zk[z]], signal=False)
                      for pr in range(2):
                          for k in range(8):
                              P.op("pe", lambda q, pr=pr, k=k: q.matmul(zk[z][:, 256 + pr * 128:256 + (pr + 1) * 128], w_n[:, k * 768 + pr * 128:k * 768 + (pr + 1) * 128],
                                                                    fr.hT[s_][:, k * 128:(k + 1) * 128], start=(k == 0), stop=(k == 7)),
                                   reads=[fr.r_hT[s_], r_w], writes=[r_zk[z]], signal=(pr == 1 and k == 7))
                      for k in range(8):
                          P.op("pe", lambda q, k=k: q.matmul(zv[z][:, 0:256], fr.hT[s_][:, k * 128:(k + 1) * 128], w_n[:, k * 768 + 512:k * 768 + 768], start=(k == 0), stop=(k == 7)),
                               reads=[fr.r_hT[s_], r_w], writes=[r_zv[z]], signal=(k == 7))

                  def c_s3(ti):
                      z = ti % NZ
                      qcols = []
                      if kind == "s":
                          qcols = [(0, 128, ti * 128)]
                      else:
                          if ti in q_of_na:
                              qcols = [(0, 128, q_of_na[ti])]
                          elif ti == 2:
                              qcols = [(64, 64, 4096)]
                          elif ti == 35:
                              qcols = [(0, 64, 4160)]
                      for pr in range(2):
                          P.op("dve", lambda q, pr=pr: q.tensor_copy(out=nkT[:, pr * NNA + ti * 128:pr * NNA + (ti + 1) * 128], in_=zk[z][:, pr * 128:(pr + 1) * 128]),
                               reads=[r_zk[z]], writes=[r_nk])
                          for (c0, cn_, qd) in qcols:
                              P.op("dve", lambda q, pr=pr, c0=c0, cn_=cn_, qd=qd: q.tensor_scalar(out=nqT[:, pr * NQ + qd:pr * NQ + qd + cn_], in0=zk[z][:, 256 + pr * 128 + c0:256 + pr * 128 + c0 + cn_],
                                                                                            scalar1=0.125, scalar2=None, op0=ALU.mult),
                                   reads=[r_zk[z]], writes=[r_nq])
                      vdst = nv[:, ti * 260:(ti + 1) * 260].rearrange("p (j d) -> p j d", d=65)[:, :, 0:64]
                      vsrc = zv[z][:, 0:256].rearrange("p (j d) -> p j d", d=64)
                      P.op("act", lambda q: q.copy(out=vdst, in_=vsrc), reads=[r_zv[z]], writes=[r_nv])

                  pipeline(NKTN, [c_s0, c_s1, c_s2, c_s3])
                  P.barrier()
                  if stop == kind + "Cp":
                      dumpconv(sC, "d0", nkT[:, 0:4096], 4096)
                      dumpconv(sC, "d1", nqT[:, 0:4096], 4096)
                      dumpconv(sC, "d2", nv[:, 0:4096], 4096)
                      P.barrier()
                      P.flush()
                      raise _Stop()
                  P.flush()

                  NPB = 3
                  pT = [sb(sC, f"c_pT{i}", [128, 512], BF16) for i in range(NPB)]
                  osb = [sb(sC, f"c_osb{i}", [65, 512], F32) for i in range(2)]
                  rrow = [sb(sC, f"c_rrow{i}", [65, 512], F32) for i in range(2)]
                  onb = [sb(sC, f"c_onb{i}", [64, 512], BF16) for i in range(2)]
                  Sps = [zk[1], zk[2], zv[0]]
                  OpsL = [zv[1], zv[2]]
                  Bps = zk[0]
                  r_Bps = r_zk[0]
                  r_pT = [Res("pT") for _ in range(NPB)]
                  r_S = [r_zk[1], r_zk[2], r_zv[0]]
                  r_OL = [r_zv[1], r_zv[2]]
                  r_osb = [Res("osb") for _ in range(2)]
                  r_rrow = [Res("rrow") for _ in range(2)]
                  r_onb = [Res("onb") for _ in range(2)]
                  ost = [P.dsem() for _ in range(2)]
                  npair = 0
                  nO = 0
                  pendC = []
                  for gr in G["groups"]:
                      q0, N = gr["q0"], gr["N"]
                      for hl in range(4):
                          pr, hp = hl // 2, (hl % 2) * 64
                          h = 4 * hh + hl
                          tiles = gr["tiles"]
                          nt = len(tiles)
                          DL = NPB - 1

                          def crange(off, N=N, gr=gr):
                              if not gr["interior"]:
                                  return 0, N
                              lo, hi = max(0, off - 3), min(7, off + 5)
                              return 64 * lo, 64 * (hi + 1)

                          def qk(m, i, q0=q0, N=N, pr=pr, hp=hp, hl=hl, gr=gr):
                              ti, off = gr["tiles"][m]
                              slot = gr["slots"][m]
                              w0 = (10 - off) * 64
                              c0, c1 = crange(off)
                              n_ = c1 - c0
                              inter = gr["interior"]
                              tab = tr2i if inter else tr2
                              P.op("pe", lambda q: q.matmul(Sps[i][:, 0:n_], nkT[hp:hp + 64, pr * NNA + ti * 128:pr * NNA + (ti + 1) * 128], nqT[hp:hp + 64, pr * NQ + q0 + c0:pr * NQ + q0 + c1], start=True, stop=False),
                                   reads=[r_nk, r_nq], writes=[r_S[i]], signal=False)
                              P.op("pe", lambda q: q.matmul(Sps[i][:, 0:n_], ident[:], tab[:, hl * 1536 + w0 + c0:hl * 1536 + w0 + c1], start=False, stop=inter),
                                   reads=[r_w, r_const], writes=[r_S[i]], signal=inter)
                              if not inter:
                                  P.op("pe", lambda q: q.matmul(Sps[i][:, 0:n_], rmt[:, slot * 128:(slot + 1) * 128], qrs[:, 0:n_], start=False, stop=True),
                                       reads=[r_w], writes=[r_S[i]])
                              P.op("act", lambda q: q.activation(out=pT[i][:, 0:n_], in_=Sps[i][:, 0:n_], func=AF.Exp, scale=1.0),
                                   reads=[r_S[i]], writes=[r_pT[i]])

                          o = nO % 2
                          nO += 1
                          Ops = OpsL[o]
                          r_O = r_OL[o]

                          def pv(m, i, N=N, hl=hl, gr=gr, nt=nt, Ops=Ops, r_O=r_O):
                              ti, off = gr["tiles"][m]
                              c0, c1 = crange(off)
                              n_ = c1 - c0
                              P.op("pe", lambda q: q.matmul(Ops[0:65, c0:c1], nv[:, ti * 260 + hl * 65:ti * 260 + (hl + 1) * 65], pT[i][:, 0:n_], start=(m == 0), stop=(m == nt - 1)),
                                   reads=[r_nv, r_pT[i]], writes=[r_O], signal=(m == nt - 1))

                          base = npair
                          for m in range(nt + DL):
                              if m < nt:
                                  qk(m, (base + m) % NPB)
                              if m - DL >= 0:
                                  pv(m - DL, (base + m - DL) % NPB)
                              if m == 2 and pendC:
                                  pendC.pop()()
                          npair += nt
                          P.op("dve", lambda q, o=o, N=N, Ops=Ops: q.tensor_copy(out=osb[o][:, 0:N], in_=Ops[0:65, 0:N]), reads=[r_O], writes=[r_osb[o]])

                          def fin2(o=o, N=N, h=h, q0=q0):
                              P.op("pe", lambda q: q.matmul(Bps[0:64, 0:N], ones_f[64:65, 0:64], osb[o][64:65, 0:N], start=True, stop=True),
                                   reads=[r_osb[o], r_const], writes=[r_Bps])
                              P.op("dve", lambda q: q.reciprocal(out=rrow[o][0:64, 0:N], in_=Bps[0:64, 0:N]), reads=[r_Bps], writes=[r_rrow[o]])
                              P.op("dve", lambda q: q.tensor_tensor(out=onb[o][:, 0:N], in0=osb[o][0:64, 0:N], in1=rrow[o][0:64, 0:N], op=ALU.mult),
                                   reads=[r_osb[o], r_rrow[o]], writes=[r_onb[o]])
                              P.dma("sp", lambda q: at_store(q, onb[o], 512 + h * 64, q0, N), ost[o], reads=[r_onb[o]])
                          if pendC:
                              pendC.pop()()
                          pendC.append(fin2)
                  while pendC:
                      pendC.pop()()
                  P.barrier()
                  if hh == 1:
                      if stop == kind + "C":
                          tt = sb(sC, "dbgl", [128, 1024], BF16)
                          rr = Res("dbgl")
                          nn = 1024
                          P.dma("sp", lambda q: q.dma_start(out=tt[:, 0:nn], in_=at_d[512:512 + 128, 0:nn]), dst_, writes=[rr])
                          t32 = sb(sC, "dbgt2", [128, 1024], F32)
                          r32 = Res("dbgt2")
                          P.op("dve", lambda q: q.tensor_copy(out=t32[:, 0:nn], in_=tt[:, 0:nn]), reads=[rr], writes=[r32])
                          P.dma("sp", lambda q: q.dma_start(out=dbg["d0"][:, 0:nn], in_=t32[:, 0:nn]), dst_, reads=[r32])
                          P.barrier()
                          P.flush()
                          raise _Stop()
                  P.flush()

          with contextlib.ExitStack() as sD:
              NSUB = NQ // 128
              NOT_ = NOWN // 512
              w_o = sb(sD, "w_o", [128, 8 * D], BF16)
              w_dn = sb(sD, "w_dn", [128, NJ * D], BF16)
              gpost = sb(sD, "gpost", [128, D], F32)
              gfpost = sb(sD, "gfpost", [128, D], F32)
              cw = sb(sD, "cw", [128, 4 * 44], F32)
              HT = [sb(sD, f"HT{i}", [128, 8 * 514], BF16) for i in range(2)]
              NX1 = 9
              x1 = [sb(sD, f"x1_{i}", [128, D], F32) for i in range(NX1)]
              aT = [sb(sD, f"aT{i}", [128, 8 * 128], BF16) for i in range(2)]
              junk = sb(sD, "d_junk", [128, D], F32)
              tmpm = sb(sD, "d_tmpm", [128, D], F32)
              h2 = [sb(sD, f"h2_{i}", [128, D], BF16) for i in range(2)]
              stt = [sb(sD, f"d_st{i}", [128, 8], F32) for i in range(2)]
              actT = sb(sD, "actT", [128, NJ * 512], BF16)
              wu = [sb(sD, f"wu{i}", [128, 8 * 256], BF16) for i in range(2)]
              hs = [sb(sD, f"hs{i}", [128, 2 * 514], F32) for i in range(2)]
              hc = [sb(sD, f"hc{i}", [128, 2 * 512], F32) for i in range(2)]
              gl = [sb(sD, "gl0", [128, 512], F32)] * 2
              mix_ps = [ps(sD, f"mix{i}", [128, 512], F32) for i in range(2)]
              tp_ps = ps(sD, "d_tp", [128, D], BF16)
              hg_ps = [[ps(sD, f"hg{p_}{i}", [128, 512], F32) for i in range(2)] for p_ in range(2)]
              he_ps = ps(sD, "he", [128, 512], F32)
              dn_ps = mix_ps
              r_w = Res("wD")
              r_HT = [Res("HT") for _ in range(2)]
              r_x1 = [Res("x1") for _ in range(NX1)]
              r_aT = [Res("aT") for _ in range(2)]
              r_junk, r_tmpm = Res("junk"), Res("tmpm")
              r_h2 = [Res("h2") for _ in range(2)]
              r_stt = [Res("stt") for _ in range(2)]
              r_act = [Res("actT") for _ in range(NJ)]
              r_wu = [Res("wu") for _ in range(2)]
              r_hs = [Res("hs") for _ in range(2)]
              r_hc = [Res("hc") for _ in range(2)]
              r_gl = [Res("gl")] * 2
              r_mix = [Res("mix") for _ in range(2)]
              r_tp = Res("tp")
              r_hg = [[Res("hg") for _ in range(2)] for _ in range(2)]
              r_he = Res("he")
              r_dn = r_mix
              dnb = [mix_ps, hg_ps[0]]
              r_dnb = [r_mix, r_hg[0]]
              wst = P.dsem()
              xld = [P.dsem() for _ in range(2)]
              ald = [P.dsem() for _ in range(2)]
              wuld = [P.dsem() for _ in range(2)]
              yst = [P.dsem() for _ in range(2)]
              for k in range(8):
                  P.dma("sp", lambda q, k=k: q.dma_start(out=w_o[:, k * D:(k + 1) * D], in_=S["w_o"][k * 128:(k + 1) * 128, :]), wst, writes=[r_w])
              r_wdn = Res("wdn")
              wdst = P.dsem()
              for j in range(NJ):
                  P.dma("act", lambda q, j=j: q.dma_start(out=w_dn[:, j * D:(j + 1) * D], in_=S["wd"][j * 128:(j + 1) * 128, :]), wdst, writes=[r_wdn])
              P.dma("sp", lambda q: q.dma_start(out=gpost[:], in_=I["g_mix_post"].partition_broadcast(128)), wst, writes=[r_w])
              P.dma("sp", lambda q: q.dma_start(out=gfpost[:], in_=I["g_ffn_post"].partition_broadcast(128)), wst, writes=[r_w])
              P.dma("sp", lambda q: q.dma_start(out=cw[:], in_=I["cwl"]), wst, writes=[r_w])
              for i in range(2):
                  P.op("pool", lambda q, i=i: q.memset(HT[i][:], 0.0), writes=[r_HT[i]])

              cnt = {"d1": 0}
              x1slot = {}

              def D1_stages(sub, post):
                  n = cnt["d1"]
                  cnt["d1"] += 1
                  b = n % 2
                  xs = n % NX1
                  x1slot[sub] = xs
                  q0 = sub * 128
                  st = stt[b]

                  def dl():
                      P.dma("sp", lambda q: q.dma_start(out=aT[b][:], in_=at_d[sub].rearrange("p k t -> p (k t)")),
                            ald[b], writes=[r_aT[b]])
                      for (p0, nr, r0) in G["own_tiles"][sub]:
                          P.dma("sp", lambda q, p0=p0, nr=nr, r0=r0: q.dma_start(out=x1[xs][p0:p0 + nr, :], in_=xna_d[r0:r0 + nr, :]), xld[b], writes=[r_x1[xs]])

                  def d0():
                      for half in range(2):
                          for k in range(8):
                              P.op("pe", lambda q, half=half, k=k: q.matmul(mix_ps[half][:, :], aT[b][:, k * 128:(k + 1) * 128], w_o[:, k * D + half * 512:k * D + (half + 1) * 512], start=(k == 0), stop=(k == 7)),
                                   reads=[r_aT[b], r_w], writes=[r_mix[half]], signal=(k == 7))

                  def d1():
                      for half in range(2):
                          P.op("act", lambda q, half=half: q.activation(out=junk[:, half * 512:(half + 1) * 512], in_=mix_ps[half][:, :], func=AF.Square, accum_out=st[:, half:half + 1]),
                               reads=[r_mix[half]], writes=[r_junk, r_stt[b]])
                      P.op("dve", lambda q: q.tensor_tensor(out=st[:, 2:3], in0=st[:, 0:1], in1=st[:, 1:2], op=ALU.add), reads=[r_stt[b]], writes=[r_stt[b]])
                      P.op("act", lambda q: q.activation(out=st[:, 3:4], in_=st[:, 2:3], func=AF.Sqrt, bias=epsb[:, 0:1], scale=1.0 / D), reads=[r_stt[b], r_const], writes=[r_stt[b]])
                      P.op("dve", lambda q: q.reciprocal(out=st[:, 3:4], in_=st[:, 3:4]), reads=[r_stt[b]], writes=[r_stt[b]])
                      for half in range(2):
                          P.op("dve", lambda q, half=half: q.scalar_tensor_tensor(out=tmpm[:, half * 512:(half + 1) * 512], in0=mix_ps[half][:, :], scalar=st[:, 3:4], in1=gpost[:, half * 512:(half + 1) * 512],
                                                                                  op0=ALU.mult, op1=ALU.mult),
                               reads=[r_mix[half], r_stt[b], r_w], writes=[r_tmpm])
                      P.op("pool", lambda q: q.tensor_tensor(out=x1[xs][:], in0=tmpm[:], in1=x1[xs][:], op=ALU.add), reads=[r_tmpm, r_x1[xs]], writes=[r_x1[xs]])

                  def d2():
                      P.op("act", lambda q: q.activation(out=junk[:], in_=x1[xs][:], func=AF.Square, accum_out=st[:, 4:5]), reads=[r_x1[xs]], writes=[r_junk, r_stt[b]])
                      P.op("act", lambda q: q.activation(out=st[:, 5:6], in_=st[:, 4:5], func=AF.Sqrt, bias=epsb[:, 0:1], scale=1.0 / D), reads=[r_stt[b], r_const], writes=[r_stt[b]])
                      P.op("dve", lambda q: q.reciprocal(out=st[:, 5:6], in_=st[:, 5:6]), reads=[r_stt[b]], writes=[r_stt[b]])
                      P.op("dve", lambda q: q.tensor_scalar(out=h2[b][:], in0=x1[xs][:], scalar1=st[:, 5:6], scalar2=None, op0=ALU.mult), reads=[r_x1[xs], r_stt[b]], writes=[r_h2[b]])

                  def d3():
                      for k in range(8):
                          P.op("pe", lambda q, k=k: q.transpose(out=tp_ps[:, k * 128:(k + 1) * 128], in_=h2[b][:, k * 128:(k + 1) * 128], identity=ident[:]),
                               reads=[r_h2[b], r_const], writes=[r_tp], signal=(k == 7))
                      post()
                  return [dl, d0, d1, d2, d3]

              def D1(sub, post):
                  for f in D1_stages(sub, post):
                      f()

              def put_ht(htb, col0, src0, ncol, flagcol=None):
                  dst = HT[htb][:].rearrange("p (k t) -> p k t", k=8)[:, :, col0:col0 + ncol]
                  src = tp_ps[:].rearrange("p (k t) -> p k t", k=8)[:, :, src0:src0 + ncol]
                  gsrc = gT_ffn[:].rearrange("p (k t) -> p k t", k=8)[:, :, 0:ncol]
                  P.op("dve", lambda q: q.tensor_tensor(out=dst, in0=src, in1=gsrc, op=ALU.mult), reads=[r_tp, r_const], writes=[r_HT[htb]])
                  if flagcol is not None:
                      P.op("dve", lambda q: q.tensor_scalar(out=dst, in0=dst, scalar1=flags[:, flagcol:flagcol + 1], scalar2=None, op0=ALU.mult),
                           reads=[r_HT[htb], r_const], writes=[r_HT[htb]])

              edge = sb(sD, "edge", [128, 8], BF16)
              r_edge = Res("edge")

              nwu = {"n": 0}

              def FFN(t, sched=None):
                  hb = t % 2
                  HTv = HT[hb]
                  pendG = []
                  sched = sched or {}
                  for j in range(NJ):
                      for f in sched.pop(j, []):
                          f()
                      wb = nwu["n"] % 2
                      nwu["n"] += 1
                      P.dma("sp", lambda q, j=j, wb=wb: q.dma_start(out=wu[wb][:], in_=S["wu"][j].rearrange("p k c -> p (k c)")), wuld[wb], writes=[r_wu[wb]])
                      pb = j % 2
                      for gu in range(2):
                          for k in range(8):
                              P.op("pe", lambda q, gu=gu, k=k, wb=wb, pb=pb: q.matmul(hg_ps[pb][gu][:, :], wu[wb][:, k * 256 + gu * 128:k * 256 + (gu + 1) * 128], HTv[:, k * 514:k * 514 + 512], start=(k == 0), stop=(k == 7)),
                                   reads=[r_wu[wb], r_HT[hb]], writes=[r_hg[pb][gu]], signal=(k == 7))
                      for gu in range(2):
                          for k in range(8):
                              P.op("pe", lambda q, gu=gu, k=k, wb=wb, pb=pb: q.matmul(he_ps[:, pb * 4 + gu * 2:pb * 4 + gu * 2 + 2], wu[wb][:, k * 256 + gu * 128:k * 256 + (gu + 1) * 128], HTv[:, k * 514 + 512:k * 514 + 514],
                                                                                    start=(k == 0), stop=(k == 7)),
                                   reads=[r_wu[wb], r_HT[hb]], writes=[r_he], signal=(gu == 1 and k == 7))
                      for gu in range(2):
                          P.op("act", lambda q, gu=gu, pb=pb: q.copy(out=hs[pb][:, gu * 514:gu * 514 + 512], in_=hg_ps[pb][gu][:, :]), reads=[r_hg[pb][gu]], writes=[r_hs[pb]])
                          P.op("act", lambda q, gu=gu, pb=pb: q.copy(out=hs[pb][:, gu * 514 + 512:gu * 514 + 514], in_=he_ps[:, pb * 4 + gu * 2:pb * 4 + gu * 2 + 2]), reads=[r_he], writes=[r_hs[pb]])
                      for gu in range(2 if 'G' not in SKIP else 0):
                          ch = gu * NJ + j
                          eng = "dve" if (gu == 0 or 'P' in SKIP) else "pool"
                          hv = hs[pb]
                          o0 = gu * 514
                          hcv = hc[pb][:, gu * 512:(gu + 1) * 512]
                          P.op(eng, lambda q, hv=hv, o0=o0, hcv=hcv, ch=ch: q.tensor_scalar(out=hcv, in0=hv[:, o0:o0 + 512], scalar1=cw[:, ch:ch + 1], scalar2=cw[:, 132 + ch:133 + ch], op0=ALU.mult, op1=ALU.add),
                               reads=[r_hs[pb], r_w], writes=[r_hc[pb]])
                          P.op("dve", lambda q, hv=hv, o0=o0, hcv=hcv, ch=ch: q.scalar_tensor_tensor(out=hcv, in0=hv[:, o0 + 1:o0 + 513], scalar=cw[:, 44 + ch:45 + ch], in1=hcv, op0=ALU.mult, op1=ALU.add),
                               reads=[r_hs[pb], r_hc[pb], r_w], writes=[r_hc[pb]])
                          P.op("dve", lambda q, hv=hv, o0=o0, hcv=hcv, ch=ch: q.scalar_tensor_tensor(out=hcv, in0=hv[:, o0 + 2:o0 + 514], scalar=cw[:, 88 + ch:89 + ch], in1=hcv, op0=ALU.mult, op1=ALU.add),
                               reads=[r_hs[pb], r_hc[pb], r_w], writes=[r_hc[pb]])
                      def gelu_part(pb=pb, j=j):
                          P.op("act", lambda q: q.activation(out=gl[pb][:], in_=hc[pb][:, 0:512], func=AF.Gelu_apprx_tanh), reads=[r_hc[pb]], writes=[r_gl[pb]])
                          P.op("dve", lambda q: q.tensor_tensor(out=actT[:, j * 512:(j + 1) * 512], in0=gl[pb][:], in1=hc[pb][:, 512:1024], op=ALU.mult),
                               reads=[r_gl[pb], r_hc[pb]], writes=[r_act[j]])
                      if pendG:
                          pendG.pop()()
                      pendG.append(gelu_part)
                  while pendG:
                      pendG.pop()()
                  for j in sorted(sched):
                      for f in sched[j]:
                          f()
                  for s4 in range(4 if 'H' not in SKIP else 0):
                      sub = t * 4 + s4
                      xs = x1slot[sub]
                      yb = sub % 2
                      for half in range(2):
                          for j in range(NJ):
                              P.op("pe", lambda q, half=half, j=j, s4=s4: q.matmul(dnb[s4 % 2][half][:, :], actT[:, j * 512 + s4 * 128:j * 512 + (s4 + 1) * 128], w_dn[:, j * D + half * 512:j * D + (half + 1) * 512],
                                                                                   start=(j == 0), stop=(j == NJ - 1)),
                                   reads=[r_act[j], r_wdn], writes=[r_dnb[s4 % 2][half]], signal=(j == NJ - 1))
                      if 'J' in SKIP:
                          continue
                      st = stt[yb]
                      for half in range(2):
                          P.op("act", lambda q, half=half, st=st, s4=s4: q.activation(out=junk[:, half * 512:(half + 1) * 512], in_=dnb[s4 % 2][half][:, :], func=AF.Square, accum_out=st[:, 6 + half:7 + half]),
                               reads=[r_dnb[s4 % 2][half]], writes=[r_junk, r_stt[yb]])
                      P.op("dve", lambda q, st=st: q.tensor_tensor(out=st[:, 6:7], in0=st[:, 6:7], in1=st[:, 7:8], op=ALU.add), reads=[r_stt[yb]], writes=[r_stt[yb]])
                      P.op("act", lambda q, st=st: q.activation(out=st[:, 7:8], in_=st[:, 6:7], func=AF.Sqrt, bias=epsb[:, 0:1], scale=1.0 / D), reads=[r_stt[yb], r_const], writes=[r_stt[yb]])
                      P.op("dve", lambda q, st=st: q.reciprocal(out=st[:, 7:8], in_=st[:, 7:8]), reads=[r_stt[yb]], writes=[r_stt[yb]])
                      for half in range(2):
                          P.op("dve", lambda q, half=half, st=st, s4=s4: q.scalar_tensor_tensor(out=tmpm[:, half * 512:(half + 1) * 512], in0=dnb[s4 % 2][half][:, :], scalar=st[:, 7:8], in1=gfpost[:, half * 512:(half + 1) * 512],
                                                                                  op0=ALU.mult, op1=ALU.mult),
                               reads=[r_dnb[s4 % 2][half], r_stt[yb], r_w], writes=[r_tmpm])
                      P.op("pool", lambda q, xs=xs: q.tensor_tensor(out=x1[xs][:], in0=tmpm[:], in1=x1[xs][:], op=ALU.add), reads=[r_tmpm, r_x1[xs]], writes=[r_x1[xs]])
                      if 'I' not in SKIP:
                          P.dma("act", lambda q, xs=xs, sub=sub: q.dma_start(out=y_d[sub * 128:(sub + 1) * 128, :], in_=x1[xs][:]), yst[yb], reads=[r_x1[xs]])

              def ht3(i):
                  return HT[i][:].rearrange("p (k t) -> p k t", k=8)

              edv = edge[:].rearrange("p (k t) -> p k t", k=8)

              def post_halo():
                  put_ht(0, 0, 63, 1, flagcol=0)
                  srcv = tp_ps[:].rearrange("p (k t) -> p k t", k=8)[:, :, 64:65]
                  gsrc = gT_ffn[:].rearrange("p (k t) -> p k t", k=8)[:, :, 0:1]
                  P.op("dve", lambda q: q.tensor_tensor(out=edv, in0=srcv, in1=gsrc, op=ALU.mult), reads=[r_tp, r_const], writes=[r_edge])
                  P.op("dve", lambda q: q.tensor_scalar(out=edv, in0=edv, scalar1=flags[:, 1:2], scalar2=None, op0=ALU.mult), reads=[r_edge, r_const], writes=[r_edge])

              if G["halo"]:
                  D1(32, post_halo)
              else:
                  P.op("pool", lambda q: q.memset(edge[:], 0.0), writes=[r_edge])

              def post_main(T, s4):
                  return lambda: put_ht(T % 2, 1 + s4 * 128, 0, 128)

              def lookahead(T):
                  def post():
                      put_ht((T - 1) % 2, 513, 0, 1)
                      put_ht(T % 2, 1, 0, 128)
                      P.op("pool", lambda q: q.tensor_copy(out=ht3(T % 2)[:, :, 0:1], in_=ht3((T - 1) % 2)[:, :, 512:513]),
                           reads=[r_HT[(T - 1) % 2]], writes=[r_HT[T % 2]])
                  D1(4 * T, post)

              def right_edge(T):
                  P.op("pool", lambda q: q.tensor_copy(out=ht3(T % 2)[:, :, 513:514], in_=edv), reads=[r_edge], writes=[r_HT[T % 2]])

              for s4 in range(4):
                  D1(s4, post_main(0, s4))
              if NOT_ > 1:
                  lookahead(1)
              else:
                  right_edge(0)
              def lookahead_stages(T):
                  def post():
                      put_ht((T - 1) % 2, 513, 0, 1)
                      put_ht(T % 2, 1, 0, 128)
                      P.op("pool", lambda q: q.tensor_copy(out=ht3(T % 2)[:, :, 0:1], in_=ht3((T - 1) % 2)[:, :, 512:513]),
                           reads=[r_HT[(T - 1) % 2]], writes=[r_HT[T % 2]])
                  return D1_stages(4 * T, post)

              slots_j = [(0, 1, 3, 5, 7), (4, 7, 9, 11, 13), (8, 12, 14, 16, 18)]
              for t in range(NOT_):
                  sched = {}
                  if t + 1 < NOT_:
                      for s4 in range(1, 4):
                          st5 = D1_stages(4 * (t + 1) + s4, post_main(t + 1, s4))
                          for jj, f in zip(slots_j[s4 - 1], st5):
                              sched.setdefault(jj, []).append(f)
                  last_stage = None
                  if t + 2 < NOT_:
                      st5 = lookahead_stages(t + 2)
                      for jj, f in zip((10, 17, 19, 20), st5[:4]):
                          sched.setdefault(jj, []).append(f)
                      last_stage = st5[4]
                  FFN(t, sched)
                  if last_stage is not None:
                      last_stage()
                  elif t + 1 < NOT_:
                      right_edge(t + 1)
              P.barrier()
              P.flush()
              if stop == kind + "D":
                  raise _Stop()


    except _Stop:
        pass

    P.barrier()
    P.flush()
    print("instr counts", P.ninstr, "sems", len(P.allsems) + len(P.dmastates))
    return nc, P, gs


def _rope_tabs(pos):
    inv = (1.0 / (np.float32(10000.0) ** (np.arange(0, 32, 2, dtype=np.float32) / np.float32(32)))).astype(np.float32)
    ang = pos.astype(np.float32)[:, None] * inv[None, :]
    return np.cos(ang).astype(np.float32), np.sin(ang).astype(np.float32)


def _rm_entry(abs_q_rows, abs_key_row0, rows_total):
    m = np.full((8, 128), NEG, np.float32)
    for gi, r in enumerate(abs_q_rows):
        if r is None or r < 0 or r >= rows_total:
            m[gi, :] = 0.0
            continue
        rs = min(max(r - 4, 0), rows_total - 8)
        for krl in range(2):
            kr = abs_key_row0 + krl
            if 0 <= kr < rows_total and rs <= kr < rs + 8:
                m[gi, krl * 64:(krl + 1) * 64] = 0.0
    return m


def _tr2_table(rpb, interior=False):
    H = rpb.shape[0]
    T = np.full((H, 128, 24, 64), NEG, np.float32)
    kc = np.arange(64)[:, None]
    qc = np.arange(64)[None, :]
    qs = np.clip(qc - 8, 0, 48)
    colv = (kc >= qs) & (kc < qs + 16)
    dc = np.clip(kc - qc + 15, 0, 30)
    for krl in range(2):
        for ei in range(24):
            dr = 7 + krl - (ei - 10)
            if (3 <= dr <= 10) if interior else (0 <= dr <= 14):
                blk = rpb[:, dr, :][:, dc]
                blk = np.where(colv[None], blk, np.float32(NEG))
                T[:, krl * 64:(krl + 1) * 64, ei, :] = blk
    return T.reshape(H, 128, 24 * 64)


_CACHE = {}


def make_in_maps(x_prompt, x_sample, g_mix_pre, w_in, g_q_lat, w_q_up, g_kv_lat, w_kv_up, na_rpb, w_o,
                 g_mix_post, g_ffn_pre, w_ffn_up, ffn_conv_w, ffn_conv_b, w_ffn_down, g_ffn_post):
    f32 = np.float32
    x_prompt = np.asarray(x_prompt, f32)
    x_sample = np.asarray(x_sample, f32)

    shared = {
        "g_mix_pre": np.asarray(g_mix_pre[0], f32), "w_in": np.asarray(w_in[0], f32), "g_q_lat": np.asarray(g_q_lat[0], f32),
        "w_q_up": np.asarray(w_q_up[0], f32), "g_kv_lat": np.asarray(g_kv_lat[0], f32), "w_kv_up": np.asarray(w_kv_up[0], f32),
        "w_o": np.asarray(w_o[0], f32), "g_mix_post": np.asarray(g_mix_post[0], f32), "g_ffn_pre": np.asarray(g_ffn_pre[0], f32),
        "w_ffn_up": np.asarray(w_ffn_up[0], f32), "ffn_conv_w": np.asarray(ffn_conv_w[0], f32), "ffn_conv_b": np.asarray(ffn_conv_b[0], f32),
        "w_ffn_down": np.asarray(w_ffn_down[0], f32), "g_ffn_post": np.asarray(g_ffn_post[0], f32),
        "tr2": _tr2_table(np.asarray(na_rpb[0], f32)),
        "tr2i": _tr2_table(np.asarray(na_rpb[0], f32), interior=True),
        "ident": np.eye(128, dtype=f32),
    }
    gc = np.zeros((128, 32), f32)
    gc[:, 0:8] = shared["g_mix_pre"].reshape(8, 128).T
    gc[:, 8:16] = shared["g_ffn_pre"].reshape(8, 128).T
    gc[:, 16:18] = shared["g_q_lat"].reshape(2, 128).T
    gc[:, 18:19] = shared["g_kv_lat"].reshape(1, 128).T
    shared["gcols"] = gc
    cwl = np.zeros((128, 176), f32)
    for t3 in range(3):
        cwl[:, t3 * 44:(t3 + 1) * 44] = shared["ffn_conv_w"][t3].reshape(44, 128).T
    cwl[:, 132:176] = shared["ffn_conv_b"].reshape(44, 128).T
    shared["cwl"] = cwl
    sel = np.zeros((4, 128, 96), f32)
    for c in range(4):
        for r in range(32):
            sel[c, 32 * c + r, 64 + r] = 1.0
    shared["sel"] = sel
    qrsel = np.zeros((8, 512), f32)
    for gi in range(8):
        qrsel[gi, gi * 64:(gi + 1) * 64] = 1.0
    shared["qrsel"] = qrsel

    in_maps = []
    for core in range(8):
        pb, pq = core // 4, core % 4
        m = dict(shared)
        G = GEO["p"]
        xb = x_prompt[pb]
        order = [pq] + [i for i in range(4) if i != pq]
        pos_kv = np.concatenate([np.arange(o * 4096, (o + 1) * 4096) for o in order])
        m["xkv_p"] = np.ascontiguousarray(xb[pos_kv])
        r0 = pq * 64
        xg = xb.reshape(256, 64, D)
        xna = np.zeros((74, 64, D), f32)
        lo, hi = r0 - 6, r0 + 68
        a, b = max(lo, 0), min(hi, 256)
        xna[a - lo:b - lo] = xg[a:b]
        m["xna_p"] = xna.reshape(74 * 64, D)
        ck, sk = _rope_tabs(pos_kv)
        m["cosk_p"] = np.ascontiguousarray(ck.reshape(128, 128, 16).transpose(1, 0, 2).reshape(128, 128 * 16))
        m["sink_p"] = np.ascontiguousarray(sk.reshape(128, 128, 16).transpose(1, 0, 2).reshape(128, 128 * 16))
        pos_q = np.concatenate([np.arange(pq * 4096, (pq + 1) * 4096), np.arange(pq * 4096 - 64, pq * 4096), np.arange((pq + 1) * 4096, (pq + 1) * 4096 + 64)])
        cq, sq = _rope_tabs(np.clip(pos_q, 0, 16383))
        qc_t = np.ones((96, G["NQ"]), f32)
        qs_t = np.zeros((96, G["NQ"]), f32)
        qc_t[64:80] = cq.T
        qc_t[80:96] = cq.T
        qs_t[64:80] = sq.T
        qs_t[80:96] = sq.T
        m["qc_p"], m["qs_p"] = qc_t, qs_t
        rm = np.zeros((8, G["nslots"] * 128), f32)
        for gi, gr in enumerate(G["groups"]):
            if gi < 8:
                qrows = [r0 + 8 * gi + i for i in range(8)]
            elif gi == 8:
                qrows = [r0 - 1] + [None] * 7
            else:
                qrows = [r0 + 64] + [None] * 7
            for (ti, off), slot in zip(gr["tiles"], gr["slots"]):
                rm[:, slot * 128:(slot + 1) * 128] = _rm_entry(qrows, (r0 - 6) + 2 * ti, 256)
        m["rm_p"] = rm
        m["flags"] = np.tile(np.array([[1.0 if pq > 0 else 0.0, 1.0 if pq < 3 else 0.0]], f32), (128, 1))
        G = GEO["s"]
        xs = x_sample[core]
        m["xkv_s"] = xs
        m["xna_s"] = xs
        pos = np.arange(2048)
        ck, sk = _rope_tabs(pos)
        m["cosk_s"] = np.ascontiguousarray(ck.reshape(16, 128, 16).transpose(1, 0, 2).reshape(128, 16 * 16))
        m["sink_s"] = np.ascontiguousarray(sk.reshape(16, 128, 16).transpose(1, 0, 2).reshape(128, 16 * 16))
        qc_t = np.ones((96, 2048), f32)
        qs_t = np.zeros((96, 2048), f32)
        qc_t[64:80] = ck.T
        qc_t[80:96] = ck.T
        qs_t[64:80] = sk.T
        qs_t[80:96] = sk.T
        m["qc_s"], m["qs_s"] = qc_t, qs_t
        rm = np.zeros((8, G["nslots"] * 128), f32)
        for gi, gr in enumerate(G["groups"]):
            qrows = [8 * gi + i for i in range(8)]
            for (ti, off), slot in zip(gr["tiles"], gr["slots"]):
                rm[:, slot * 128:(slot + 1) * 128] = _rm_entry(qrows, 2 * ti, 32)
        m["rm_s"] = rm
        in_maps.append(m)
    return in_maps


def kernel(**inputs):
    f32 = np.float32
    in_maps = make_in_maps(**inputs)
    if "nc" not in _CACHE:
        _CACHE["nc"] = build_program()
    nc, P, gs = _CACHE["nc"]
    res = run_bass_kernel_spmd(nc, in_maps, core_ids=list(range(8)))
    y_p = np.zeros((2, 16384, D), f32)
    y_s = np.zeros((8, 2048, D), f32)
    for core in range(8):
        pb, pq = core // 4, core % 4
        y_p[pb, pq * 4096:(pq + 1) * 4096] = res.results[core]["y_p"]
        y_s[core] = res.results[core]["y_s"]
    return (y_p, y_s)
```
